# Optimizing a Trainium2 kernel written in Bass

```python
import math
import jax, jax.numpy as jnp
from jax import lax
import numpy as np

D_MODEL = 1024
BATCH = 4
SEQ = 4096
DEPTH = 2

D_A = 512
D_B = 512
N_HEADS = 8
HEAD_DIM = 64
D_C = N_HEADS * HEAD_DIM
CONV_WIDTH = 31
CHUNK = 128
N_GROUPS_B = 4
GROUP_DIM_B = D_B // N_GROUPS_B
Q_BLOCK = 128
N_BRANCHES = 3
EPS = 1e-6
SPLIT_SIZES = (N_BRANCHES * D_MODEL, 2 * D_A, D_A, 2 * D_B, D_B, D_C, D_C, D_C, D_C, N_HEADS)
N_IN = N_BRANCHES * D_MODEL + 3 * D_A + 3 * D_B + 4 * D_C + N_HEADS

kernel_name = "hybrid_gated_conformer_gmlp_fox"


def rmsnorm(x, g):
    x32 = x.astype(jnp.float32)
    y = x32 * lax.rsqrt(jnp.mean(x32 * x32, axis=-1, keepdims=True) + EPS)
    return (y * g.astype(jnp.float32)).astype(x.dtype)


def layernorm(x, g, b):
    x32 = x.astype(jnp.float32)
    mu = jnp.mean(x32, axis=-1, keepdims=True)
    xc = x32 - mu
    y = xc * lax.rsqrt(jnp.mean(xc * xc, axis=-1, keepdims=True) + EPS)
    return (y * g.astype(jnp.float32) + b.astype(jnp.float32)).astype(x.dtype)


def split_columns(p):
    idx = []
    acc = 0
    for s in SPLIT_SIZES[:-1]:
        acc += s
        idx.append(acc)
    return jnp.split(p, idx, axis=-1)


def causal_depthwise_conv(a, w, b):
    out = lax.conv_general_dilated(
        a, w[:, None, :].astype(a.dtype), window_strides=(1,),
        padding=[(CONV_WIDTH - 1, 0)],
        dimension_numbers=("NWC", "WIO", "NWC"),
        feature_group_count=a.shape[-1])
    return out + b.astype(a.dtype)


def conformer_branch(a_in, a_z, conv_w, conv_b, cn_g, cn_b, w_a):
    a = a_in[..., :D_A] * jax.nn.sigmoid(a_in[..., D_A:])
    a = causal_depthwise_conv(a, conv_w, conv_b)
    a = jax.nn.silu(layernorm(a, cn_g, cn_b))
    a = a * jax.nn.silu(a_z)
    return a @ w_a


def gmlp_branch(b_uv, b_z, gn_g, w_s, b_s, w_b):
    bsz, seq, _ = b_uv.shape
    u, v = b_uv[..., :D_B], b_uv[..., D_B:]
    v = rmsnorm(v, gn_g)
    v = v.reshape(bsz, seq // CHUNK, CHUNK, N_GROUPS_B, GROUP_DIM_B)
    tril = jnp.tril(jnp.ones((CHUNK, CHUNK), dtype=bool))
    ws = jnp.where(tril[None], w_s, jnp.zeros_like(w_s))
    mixed = jnp.einsum("gts,bnsgc->bntgc", ws, v) + b_s.T[None, None, :, :, None]
    y = u * mixed.reshape(bsz, seq, D_B)
    y = y * jax.nn.silu(b_z)
    return y @ w_b


def fox_attention(q, k, v, logf):
    bsz, seq = q.shape[0], q.shape[1]
    n_blocks = seq // Q_BLOCK
    scale = 1.0 / math.sqrt(HEAD_DIM)
    c = jnp.cumsum(logf, axis=1).transpose(0, 2, 1)
    qb = q.reshape(bsz, n_blocks, Q_BLOCK, N_HEADS, HEAD_DIM).transpose(1, 0, 2, 3, 4)
    cb = c.reshape(bsz, N_HEADS, n_blocks, Q_BLOCK).transpose(2, 0, 1, 3)
    kpos = jnp.arange(seq)

    def one_block(args):
        i, qi, ci = args
        s = jnp.einsum("bqhd,bkhd->bhqk", qi, k).astype(jnp.float32) * scale
        s = s + ci[..., :, None] - c[:, :, None, :]
        qpos = i * Q_BLOCK + jnp.arange(Q_BLOCK)
        mask = kpos[None, :] <= qpos[:, None]
        s = jnp.where(mask[None, None], s, -jnp.inf)
        p = jax.nn.softmax(s, axis=-1)
        return jnp.einsum("bhqk,bkhd->bqhd", p.astype(v.dtype), v)

    out = lax.map(one_block, (jnp.arange(n_blocks), qb, cb))
    return out.transpose(1, 0, 2, 3, 4).reshape(bsz, seq, N_HEADS, HEAD_DIM)


def fox_branch(q, k, v, c_z, f_pre, qn_g, kn_g, b_f, w_c):
    bsz, seq, _ = q.shape
    q = rmsnorm(q.reshape(bsz, seq, N_HEADS, HEAD_DIM), qn_g)
    k = rmsnorm(k.reshape(bsz, seq, N_HEADS, HEAD_DIM), kn_g)
    v = v.reshape(bsz, seq, N_HEADS, HEAD_DIM)
    logf = jax.nn.log_sigmoid(f_pre.astype(jnp.float32) + b_f.astype(jnp.float32))
    o = fox_attention(q, k, v, logf).reshape(bsz, seq, D_C)
    o = o * jax.nn.silu(c_z)
    return o @ w_c


def setup_inputs(seed: int = 0) -> dict:
    key = jax.random.key(seed)
    ks = jax.random.split(key, 20)
    n = jax.random.normal
    L = DEPTH
    return {
        "x": n(ks[0], (BATCH, SEQ, D_MODEL), jnp.float32),
        "norm_g": 1.0 + 0.02 * n(ks[1], (L, D_MODEL), jnp.float32),
        "w_in": n(ks[2], (L, D_MODEL, N_IN), jnp.float32) * D_MODEL ** -0.5,
        "b_gate": 0.02 * n(ks[3], (L, N_BRANCHES * D_MODEL), jnp.float32),
        "conv_w": n(ks[4], (L, CONV_WIDTH, D_A), jnp.float32) * CONV_WIDTH ** -0.5,
        "conv_b": 0.02 * n(ks[5], (L, D_A), jnp.float32),
        "conv_norm_g": 1.0 + 0.02 * n(ks[6], (L, D_A), jnp.float32),
        "conv_norm_b": 0.02 * n(ks[7], (L, D_A), jnp.float32),
        "w_a": n(ks[8], (L, D_A, D_MODEL), jnp.float32) * D_A ** -0.5,
        "gmlp_norm_g": 1.0 + 0.02 * n(ks[9], (L, D_B), jnp.float32),
        "w_s": n(ks[10], (L, N_GROUPS_B, CHUNK, CHUNK), jnp.float32) * CHUNK ** -0.5,
        "b_s": 1.0 + 0.1 * n(ks[11], (L, N_GROUPS_B, CHUNK), jnp.float32),
        "w_b": n(ks[12], (L, D_B, D_MODEL), jnp.float32) * D_B ** -0.5,
        "q_norm_g": 1.0 + 0.02 * n(ks[13], (L, HEAD_DIM), jnp.float32),
        "k_norm_g": 1.0 + 0.02 * n(ks[14], (L, HEAD_DIM), jnp.float32),
        "b_f": 2.0 + 0.1 * n(ks[15], (L, N_HEADS), jnp.float32),
        "w_c": n(ks[16], (L, D_C, D_MODEL), jnp.float32) * D_C ** -0.5,
        "w_out": n(ks[17], (L, D_MODEL, D_MODEL), jnp.float32) * D_MODEL ** -0.5,
    }


def reference(x, norm_g, w_in, b_gate, conv_w, conv_b, conv_norm_g, conv_norm_b, w_a,
              gmlp_norm_g, w_s, b_s, w_b, q_norm_g, k_norm_g, b_f, w_c, w_out):
    for l in range(DEPTH):
        h = rmsnorm(x, norm_g[l])
        p = h @ w_in[l]
        gate_pre, a_in, a_z, b_uv, b_z, q, k, v, c_z, f_pre = split_columns(p)
        gates = jax.nn.sigmoid(gate_pre + b_gate[l])
        g_a = gates[..., :D_MODEL]
        g_b = gates[..., D_MODEL:2 * D_MODEL]
        g_c = gates[..., 2 * D_MODEL:]
        y_a = conformer_branch(a_in, a_z, conv_w[l], conv_b[l], conv_norm_g[l],
                               conv_norm_b[l], w_a[l])
        y_b = gmlp_branch(b_uv, b_z, gmlp_norm_g[l], w_s[l], b_s[l], w_b[l])
        y_c = fox_branch(q, k, v, c_z, f_pre, q_norm_g[l], k_norm_g[l], b_f[l], w_c[l])
        merged = g_a * y_a + g_b * y_b + g_c * y_c
        x = x + merged @ w_out[l]
    return x
```

```python
import os
import numpy as np
import concourse.bass as bass
import concourse.mybir as mybir
from concourse.bass_utils import run_bass_kernel_spmd

F32 = mybir.dt.float32
BF16 = mybir.dt.bfloat16
ALU = mybir.AluOpType
AF = mybir.ActivationFunctionType

D = 1024
SEQ = 4096
NIN = 8200
EPS = 1e-6
PASS = 1024
NSP = 170
C_GATE, C_AV, C_AG, C_AZ, C_U, C_V, C_BZ, C_Q, C_K, C_VV, C_CZ, C_F = (
    0, 3072, 3584, 4096, 4608, 5120, 5632, 6144, 6656, 7168, 7680, 8192)
SP_G, SP_BG, SP_CW, SP_CB, SP_CNG, SP_CNB, SP_QG, SP_KG = 0, 8, 32, 156, 160, 164, 168, 169

ENGS = ("pe", "act", "dve", "pool", "sp")


class Op:
    __slots__ = ("eng", "fn", "deps", "dma", "signal", "count", "idx", "grp")

    def __init__(self, eng, fn, dma):
        self.eng = eng
        self.fn = fn
        self.dma = dma
        self.deps = {}
        self.signal = False
        self.count = 0
        self.grp = ("d:" + dma) if dma is not None else ("e:" + eng)


class Sched:
    def __init__(self, nc):
        self.nc = nc
        self.ops = []
        self.last_w = {}
        self.readers = {}
        self.maxops = int(os.environ.get("MK_MAXOPS", "0")) or None

    def _dep(self, op, d):
        src = self.ops[d]
        if src.fn is None:
            for g, i in src.deps.items():
                if op.deps.get(g, -1) < i:
                    op.deps[g] = i
            return
        if op.deps.get(src.grp, -1) < d:
            op.deps[src.grp] = d

    def add(self, eng, fn, reads=(), writes=(), dma=None):
        if self.maxops is not None and len(self.ops) >= self.maxops:
            return len(self.ops) - 1
        op = Op(eng, fn, dma)
        op.idx = len(self.ops)
        self.ops.append(op)
        excl = [b for b in reads if isinstance(b, tuple) and b[0] in ("pf", "pb")]
        if excl:
            reads = [b for b in reads if b not in excl]
            writes = list(writes) + excl
        for b in reads:
            w = self.last_w.get(b)
            if w is not None:
                self._dep(op, w)
        for b in writes:
            w = self.last_w.get(b)
            if w is not None:
                self._dep(op, w)
            for r in self.readers.get(b, ()):
                self._dep(op, r)
        for b in reads:
            self.readers.setdefault(b, []).append(op.idx)
        for b in writes:
            self.last_w[b] = op.idx
            self.readers[b] = []
        return op.idx

    def barrier(self, key):
        return self.add("sp", None, writes=[key])

    def emit(self):
        nc = self.nc
        ops = self.ops
        for op in ops:
            if op.fn is None:
                continue
            for g, d in op.deps.items():
                src = ops[d]
                if src.dma is None and src.eng == "pe" and op.eng == "pe" and op.dma is None:
                    continue
                src.signal = True
        counters = {}
        for op in ops:
            if op.fn is None:
                continue
            if op.dma is not None:
                counters[op.grp] = counters.get(op.grp, 0) + 16
                op.count = counters[op.grp]
            elif op.signal:
                counters[op.grp] = counters.get(op.grp, 0) + 1
                op.count = counters[op.grp]
        sems = {k: nc.alloc_semaphore(name="s_" + k.replace(":", "_")) for k in counters}
        per_eng = {e: [] for e in ENGS}
        for op in ops:
            per_eng[op.eng].append(op)

        def run(eng_name, eng):
            waited = {}
            for op in per_eng[eng_name]:
                if op.fn is None:
                    if op.deps and op is ops[-1]:
                        pass
                    else:
                        continue
                for g, d in op.deps.items():
                    src = ops[d]
                    if src.dma is None and src.eng == "pe" and eng_name == "pe" and op.dma is None:
                        continue
                    if waited.get(g, 0) >= src.count:
                        continue
                    eng.wait_ge(sems[g], src.count)
                    waited[g] = src.count
                if op.fn is None:
                    continue
                ins = op.fn(eng)
                if op.dma is not None:
                    ins.then_inc(sems[op.grp], 16)
                elif op.signal:
                    ins.then_inc(sems[op.grp], 1)

        with nc.allow_low_precision(reason="bf16 activations feeding bf16 matmuls"), nc.Block() as block:
            @block.tensor
            def _(e):
                run("pe", e)

            @block.scalar
            def _(e):
                run("act", e)

            @block.vector
            def _(e):
                run("dve", e)

            @block.gpsimd
            def _(e):
                run("pool", e)

            @block.sync
            def _(e):
                run("sp", e)


class Rot:
    def __init__(self, items):
        self.items = list(items)
        self.i = 0

    def next(self):
        it = self.items[self.i % len(self.items)]
        self.i += 1
        return it


def build(NL=2, NP=4, dbg=False):
    nc = bass.Bass("TRN2", target_bir_lowering=False)
    STOP = int(os.environ.get("MK_STOP", "9"))
    SUB = int(os.environ.get("MK_SUB", "9"))

    def din(name, shape, dt=F32):
        return nc.dram_tensor(name, list(shape), dt, kind="ExternalInput").ap()

    x_d = din("x", [SEQ, D])
    w_in_l = [din("w_in0", [D, NIN]), din("w_in1", [D, NIN])]
    w_a_d = din("w_a", [2, 512, D])
    w_b_d = din("w_b", [2, 512, D])
    w_c_d = din("w_c", [2, 512, D])
    w_out_d = din("w_out", [2, D, D])
    w_s_d = din("w_s", [2, 4, 128, 128])
    b_s_d = din("b_s", [2, 4, 128])
    gg_d = din("gmlp_norm_g", [2, 512])
    bf_d = din("b_f", [2, 8])
    sp_d = din("spT", [2, 128, NSP])
    ident_d = din("ident", [128, 128])
    triu_d = din("triu", [128, 128])
    tril_d = din("tril", [128, 128])
    maskU_d = din("maskU", [128, 4, 512])
    out_d = nc.dram_tensor("out", [SEQ, D], F32, kind="ExternalOutput").ap()
    xmid_d = nc.dram_tensor("xmid", [SEQ, D], F32, kind="ExternalOutput").ap()
    KT_d = nc.dram_tensor("KTd", [512, SEQ], BF16, kind="ExternalOutput").ap()
    V_d = nc.dram_tensor("Vd", [SEQ, 512], BF16, kind="ExternalOutput").ap()

    def sb(name, shape, dt):
        return nc.alloc_sbuf_tensor(name, list(shape), dt).ap()

    identb = sb("identb", [128, 128], BF16)
    onesf = sb("onesf", [128, 128], F32)
    blockones = sb("blockones", [128, 128], BF16)
    triuf = sb("triuf", [128, 128], F32)
    trilf = sb("trilf", [128, 128], F32)
    maskU = sb("maskU_s", [128, 4, 512], BF16)
    spT = sb("spT_s", [128, NSP], F32)
    qgs = sb("qgs", [128, 1], F32)
    nbg = sb("nbg", [128, 24], F32)
    cwh = sb("cwh", [128, 124], F32)
    ggbc = sb("ggbc", [128, 512], F32)
    bfbc = sb("bfbc", [128, 8], F32)
    bsf = sb("bsf", [1, 4, 128], F32)
    wsf = sb("wsf", [128, 4, 128], F32)
    wsm = sb("wsm", [128, 4, 128], BF16)
    WsT = sb("WsT", [128, 4, 128], BF16)
    wout = sb("wout", [128, 8, 1024], BF16)
    LF = sb("LF", [128, 32, 8], F32)
    TOTs = sb("TOTs", [128, 32, 8], F32)
    EX = sb("EX", [128, 32, 8], F32)
    Pt = sb("Pt", [128, 32, 8], F32)
    fpb = sb("fpb", [128, 64], F32)
    hT = sb("hT", [128, 8, PASS], BF16)
    wbuf = [sb(f"wbuf{i}", [128, 8, 256], BF16) for i in range(4)]
    cbr = sb("cbr", [128, 4, PASS], BF16)
    abr = sb("abr", [128, 4, PASS], BF16)
    bbr = sb("bbr", [128, 4, PASS], BF16)
    vn = sb("vn", [128, 8, 512], BF16)
    ssqv = sb("ssqv", [128, 8, 2], F32)
    ssqv1 = sb("ssqv1", [128, 8], F32)
    rstdv = sb("rstdv", [128, 8], F32)
    ssq = sb("ssq", [128, 8], F32)
    rstd = sb("rstd", [128, 8], F32)
    ksq = sb("ksq", [128, 512], BF16)
    lnt = sb("lnt", [128, 512], F32)
    rstdq = sb("rstdq", [128, 512], F32)
    kst = [sb(f"kst{i}", [128, 512], BF16) for i in range(2)]
    aT = sb("aT", [128, 4, 32 + PASS], BF16)
    acc = sb("acc", [128, 4, 512], F32)
    ysq = [sb(f"ysq{i}", [128, 512], F32) for i in range(2)]
    ew1 = sb("ew1", [128, 2, PASS], BF16)
    ew2 = sb("ew2", [128, 2, PASS], BF16)
    et = sb("et", [128, 512], F32)
    mean = sb("mean", [128, 512], F32)
    var = sb("var", [128, 512], F32)
    rstdc = sb("rstdc", [128, 512], F32)
    zt = sb("zt", [128, 512], F32)
    AR = 60 * 1024 // 2
    arena = sb("arena", [128, AR], BF16)

    def carve(off, nbytes, dt, shape=None):
        v = arena[:, off // 2:(off + nbytes) // 2]
        if dt == F32:
            v = v.bitcast(F32)
        if shape is not None and len(shape) == 3:
            v = v.rearrange("p (a b) -> p a b", a=shape[1])
        return v

    K = 1024
    xin = [carve(0, 4 * K, F32), carve(4 * K, 4 * K, F32)]
    xs = [carve(8 * K, 2 * K, BF16), carve(10 * K, 2 * K, BF16)]
    junk = carve(12 * K, 2 * K, BF16)
    vstage = carve(14 * K, 8 * K, BF16, [128, 8, 512])
    KTb = [carve(0, 8 * K, BF16), carve(8 * K, 8 * K, BF16)]
    Vb = [carve(16 * K, 12 * K, BF16, [128, 32, 192]), carve(28 * K, 12 * K, BF16, [128, 32, 192])]
    PT = [carve(40 * K + i * K, K, BF16) for i in range(3)]
    biasT = carve(43 * K, 2 * K, F32).rearrange("p (s j h) -> p s j h", s=2, j=32)
    rden = carve(45 * K, 2 * K, F32)
    tmpO = carve(47 * K, 2 * K, F32)
    qT = [carve(49 * K, 2 * K, BF16), carve(51 * K, 2 * K, BF16)]
    scz = [carve(53 * K, 2 * K, BF16), carve(55 * K, 2 * K, BF16)]
    mergedT = carve(0, 16 * K, BF16, [128, 8, PASS])
    gden = [carve(16 * K + i * 2 * K, 2 * K, F32) for i in range(3)]
    macc = [carve(22 * K, 2 * K, F32), carve(24 * K, 2 * K, F32)]
    mt = carve(26 * K, 2 * K, F32)
    xo = [carve(28 * K, 4 * K, F32), carve(32 * K, 4 * K, F32)]
    xh = [carve(36 * K, 4 * K, F32), carve(40 * K, 4 * K, F32)]

    pf = [nc.alloc_psum_tensor(f"pf{i}", [128, 512], F32).ap() for i in range(6)]
    pb = [nc.alloc_psum_tensor(f"pb{i}", [128, 1024], BF16).ap() for i in range(2)]

    S = Sched(nc)
    A1 = "ar1"
    wrot = Rot(range(4))
    crot = Rot(range(4))
    gen = Rot(range(6))

    def mm(out, lhsT, rhs, start, stop, reads, writes):
        S.add("pe", lambda e: e.matmul(out, lhsT=lhsT, rhs=rhs, start=start, stop=stop), reads=reads, writes=writes)

    def cast_dma(dst, src, wkeys, grp):
        S.add("pool", lambda e: e.dma_start(out=dst, in_=src), writes=wkeys, dma=grp)

    def load_w(src2d, nkc, ncols):
        s = wrot.next()
        dst = wbuf[s][:, 0:nkc, 0:ncols]
        src = src2d.rearrange("(kc p) c -> p kc c", p=128)
        S.add("pool", lambda e: e.dma_start(out=dst, in_=src), writes=[("w", s)], dma=f"w{s}")
        return s

    def proj_fm(ws, col0, sl, bank, nkc=8, src=None, srckey="hT"):
        src = hT if src is None else src
        for kc in range(nkc):
            mm(pf[bank][:, :], wbuf[ws][:, kc, col0:col0 + 128], src[:, kc, sl * 512:(sl + 1) * 512],
               kc == 0, kc == nkc - 1, [("w", ws), srckey], [("pf", bank)])

    cast_dma(identb, ident_d, ["identb"], "cdid")
    cast_dma(maskU, maskU_d, ["maskU"], "cdmu")
    S.add("sp", lambda e: e.dma_start(out=triuf, in_=triu_d), writes=["triuf"], dma="c2")
    S.add("sp", lambda e: e.dma_start(out=trilf, in_=tril_d), writes=["trilf"], dma="c3")
    S.add("dve", lambda e: e.memset(onesf, 1.0), writes=["onesf"])
    S.add("dve", lambda e: e.memset(blockones, 0.0), writes=["blockones"])
    S.add("dve", lambda e: e.memset(blockones[0:64, 0:64], 1.0), writes=["blockones"])
    S.add("dve", lambda e: e.memset(blockones[64:128, 64:128], 1.0), writes=["blockones"])

    for L in range(NL):
        xsrc = x_d if L == 0 else xmid_d
        xdst = out_d if L == NL - 1 else xmid_d
        srckey = "xin_d" if L == 0 else "xmid"
        dstkey = "out_d" if L == NL - 1 else "xmid"
        S.add("sp", lambda e, L=L: e.dma_start(out=spT, in_=sp_d[L]), writes=["spT"], dma="p0")
        S.add("sp", lambda e, L=L: e.dma_start(out=ggbc, in_=gg_d[L:L + 1, :].broadcast_to([128, 512])),
              writes=["ggbc"], dma="p1")
        S.add("sp", lambda e, L=L: e.dma_start(out=bfbc, in_=bf_d[L:L + 1, :].broadcast_to([128, 8])),
              writes=["bfbc"], dma="p2")
        S.add("sp", lambda e, L=L: e.dma_start(out=bsf, in_=b_s_d[L:L + 1, :, :]), writes=["bsf"], dma="p3")
        S.add("sp", lambda e, L=L: e.dma_start(out=wsf, in_=w_s_d[L].rearrange("g t s -> t g s")),
              writes=["wsf"], dma="p4")
        S.add("dve", lambda e: e.tensor_scalar(out=qgs, in0=spT[:, SP_QG:SP_QG + 1], scalar1=0.125, scalar2=None,
                                               op0=ALU.mult), reads=["spT"], writes=["qgs"])
        S.add("dve", lambda e: e.tensor_scalar(out=nbg, in0=spT[:, SP_BG:SP_BG + 24], scalar1=0.5, scalar2=None,
                                               op0=ALU.mult), reads=["spT"], writes=["nbg"])
        S.add("dve", lambda e: e.tensor_scalar(out=cwh, in0=spT[:, SP_CW:SP_CW + 124], scalar1=0.5, scalar2=None,
                                               op0=ALU.mult), reads=["spT"], writes=["cwh"])
        S.add("dve", lambda e: e.tensor_tensor(out=wsm, in0=wsf, in1=trilf.unsqueeze(1).broadcast_to([128, 4, 128]),
                                               op=ALU.mult), reads=["wsf", "trilf"], writes=["wsm"])
        for g in range(4):
            S.add("pe", lambda e, g=g: e.transpose(pb[0][:, g * 128:(g + 1) * 128], wsm[:, g, :], identb),
                  reads=["wsm", "identb"], writes=[("pb", 0)])
        S.add("dve", lambda e: e.tensor_copy(out=WsT, in_=pb[0][:, 0:512].rearrange("p (g t) -> p g t", g=4)),
              reads=[("pb", 0)], writes=["WsT"])
        for u in range(4):
            cast_dma(wout[:, :, u * 256:(u + 1) * 256],
                     w_out_d[L, :, u * 256:(u + 1) * 256].rearrange("(kc p) c -> p kc c", p=128), [("wout", u)], f"wo{u}")

        for p in range(NP):
            tok0 = p * PASS
            nblk_total = (p + 1) * 8
            S.add("dve", lambda e: e.memset(ssq, 0.0), writes=["ssq"])
            for tb in range(8):
                r0 = tok0 + tb * 128
                xi = xin[tb % 2]
                xb = xs[tb % 2]
                S.add("sp", lambda e, xi=xi, r0=r0, xsrc=xsrc: e.dma_start(out=xi, in_=xsrc[r0:r0 + 128, :]),
                      reads=[(srckey, p, tb), A1], writes=[("xin", tb % 2)], dma=f"xin{tb % 2}")
                S.add("act", lambda e, xi=xi, tb=tb: e.activation(out=junk, in_=xi, func=AF.Square,
                                                                 accum_out=ssq[:, tb:tb + 1]),
                      reads=[("xin", tb % 2), A1], writes=["junk", "ssq"])
                S.add("act", lambda e, tb=tb: e.activation(out=rstd[:, tb:tb + 1], in_=ssq[:, tb:tb + 1], func=AF.Ln,
                                                           bias=EPS, scale=1.0 / D), reads=["ssq"], writes=["rstd"])
                S.add("act", lambda e, tb=tb: e.activation(out=rstd[:, tb:tb + 1], in_=rstd[:, tb:tb + 1], func=AF.Exp,
                                                           scale=-0.5), reads=["rstd"], writes=["rstd"])
                S.add("dve", lambda e, xi=xi, xb=xb, tb=tb: e.tensor_scalar(out=xb, in0=xi, scalar1=rstd[:, tb:tb + 1],
                                                                          scalar2=None, op0=ALU.mult),
                      reads=[("xin", tb % 2), "rstd", A1], writes=[("xs", tb % 2)])
                for kc in range(8):
                    S.add("pe", lambda e, xb=xb, kc=kc, tb=tb: e.transpose(
                        pb[tb % 2][:, kc * 128:(kc + 1) * 128], xb[:, kc * 128:(kc + 1) * 128], identb),
                        reads=[("xs", tb % 2), "identb", A1], writes=[("pb", tb % 2)])
                S.add("dve", lambda e, tb=tb: e.tensor_tensor(
                    out=hT[:, :, tb * 128:(tb + 1) * 128], in0=pb[tb % 2].rearrange("p (k t) -> p k t", k=8),
                    in1=spT[:, SP_G:SP_G + 8].unsqueeze(2).broadcast_to([128, 8, 128]), op=ALU.mult),
                    reads=[("pb", tb % 2), "spT"], writes=["hT"])

            if STOP <= 1:
                continue
            S.add("dve", lambda e: e.memset(ssqv, 0.0), writes=["ssqv"])
            for which, c0 in ((("v", C_V), ("V", C_VV)) if SUB > 1 else (("v", C_V),)):
                for u in range(2):
                    ws = load_w(w_in_l[L][:, c0 + u * 256: c0 + (u + 1) * 256], 8, 256)
                    for tb in range(8):
                        bk = gen.next()
                        for kc in range(8):
                            mm(pf[bk][:, 0:256], hT[:, kc, tb * 128:(tb + 1) * 128], wbuf[ws][:, kc, 0:256],
                               kc == 0, kc == 7, [("w", ws), "hT"], [("pf", bk)])
                        if which == "v":
                            S.add("act", lambda e, bk=bk, tb=tb, u=u: e.activation(
                                out=lnt[:, 0:256], in_=pf[bk][:, 0:256], func=AF.Square, accum_out=ssqv[:, tb, u:u + 1]),
                                reads=[("pf", bk)], writes=["lnt", "ssqv"])
                            S.add("dve", lambda e, bk=bk, tb=tb, u=u: e.tensor_copy(
                                out=vn[:, tb, u * 256:(u + 1) * 256], in_=pf[bk][:, 0:256]),
                                reads=[("pf", bk)], writes=["vn"])
                        else:
                            S.add("act", lambda e, bk=bk, tb=tb, u=u: e.copy(
                                out=vstage[:, tb, u * 256:(u + 1) * 256], in_=pf[bk][:, 0:256]),
                                reads=[("pf", bk), A1], writes=["vstage"])
            if SUB <= 1:
                continue
            for tb in range(8):
                r0 = tok0 + tb * 128
                S.add("sp", lambda e, tb=tb, r0=r0: e.dma_start(out=V_d[r0:r0 + 128, :], in_=vstage[:, tb, :]),
                      reads=["vstage", A1], writes=[("Vd", p, tb)], dma="vst")
            if SUB <= 2:
                continue
            S.add("dve", lambda e: e.tensor_tensor(out=ssqv1, in0=ssqv[:, :, 0], in1=ssqv[:, :, 1], op=ALU.add),
                  reads=["ssqv"], writes=["ssqv1"])
            S.add("act", lambda e: e.activation(out=rstdv, in_=ssqv1, func=AF.Ln, bias=EPS, scale=1.0 / 512),
                  reads=["ssqv1"], writes=["rstdv"])
            S.add("act", lambda e: e.activation(out=rstdv, in_=rstdv, func=AF.Exp, scale=-0.5),
                  reads=["rstdv"], writes=["rstdv"])
            for tb in range(8):
                S.add("dve", lambda e, tb=tb: e.scalar_tensor_tensor(
                    out=vn[:, tb, :], in0=vn[:, tb, :], scalar=rstdv[:, tb:tb + 1], in1=ggbc, op0=ALU.mult, op1=ALU.mult),
                    reads=["vn", "rstdv", "ggbc"], writes=["vn"])
            if SUB <= 3:
                continue
            ws = load_w(w_in_l[L][:, C_F:C_F + 8], 8, 8)
            bk = gen.next()
            for tb in range(8):
                for kc in range(8):
                    mm(pf[bk][:, tb * 8:(tb + 1) * 8], hT[:, kc, tb * 128:(tb + 1) * 128], wbuf[ws][:, kc, 0:8],
                       kc == 0, kc == 7, [("w", ws), "hT"], [("pf", bk)])
            S.add("dve", lambda e, bk=bk: e.tensor_tensor(
                out=fpb.rearrange("p (t h) -> p t h", t=8), in0=pf[bk][:, 0:64].rearrange("p (t h) -> p t h", t=8),
                in1=bfbc.unsqueeze(1).broadcast_to([128, 8, 8]), op=ALU.add),
                reads=[("pf", bk), "bfbc"], writes=["fpb"])
            S.add("act", lambda e: e.activation(out=fpb, in_=fpb, func=AF.Exp, scale=-1.0), reads=["fpb"], writes=["fpb"])
            if SUB <= 4:
                continue
            lfp = LF[:, p * 8:(p + 1) * 8, :].rearrange("p j h -> p (j h)")
            S.add("act", lambda e, lfp=lfp: e.activation(out=lfp, in_=fpb, func=AF.Ln, bias=1.0, scale=1.0),
                  reads=["fpb"], writes=["LF"])
            bk = gen.next()
            mm(pf[bk][:, 0:64], triuf, lfp, True, True, ["triuf", "LF"], [("pf", bk)])
            mm(pf[bk][:, 64:128], onesf, lfp, True, True, ["onesf", "LF"], [("pf", bk)])
            S.add("dve", lambda e, bk=bk, p=p: e.tensor_copy(
                out=TOTs[:, p * 8:(p + 1) * 8, :].rearrange("p j h -> p (j h)"), in_=pf[bk][:, 64:128]),
                reads=[("pf", bk)], writes=["TOTs"])
            for j in range(p * 8, (p + 1) * 8):
                if j == 0:
                    S.add("dve", lambda e: e.memset(EX[:, 0, :], 0.0), writes=["EX"])
                else:
                    S.add("dve", lambda e, j=j: e.tensor_tensor(out=EX[:, j, :], in0=EX[:, j - 1, :],
                                                               in1=TOTs[:, j - 1, :], op=ALU.add),
                          reads=["EX", "TOTs"], writes=["EX"])
            S.add("dve", lambda e, bk=bk, p=p: e.tensor_tensor(
                out=Pt[:, p * 8:(p + 1) * 8, :].rearrange("p j h -> p (j h)"), in0=pf[bk][:, 0:64],
                in1=EX[:, p * 8:(p + 1) * 8, :].rearrange("p j h -> p (j h)"), op=ALU.add),
                reads=[("pf", bk), "EX"], writes=["Pt"])

            if STOP <= 2:
                continue
            def qk_norm(bk, gcol, dst, dstkeys, extra_reads=(), sbank=None):
                S.add("act", lambda e: e.activation(out=ksq, in_=pf[bk], func=AF.Square),
                      reads=[("pf", bk)], writes=["ksq"])
                sb_ = gen.next() if sbank is None else sbank
                mm(pf[sb_], blockones, ksq, True, True, ["blockones", "ksq"], [("pf", sb_)])
                S.add("act", lambda e: e.activation(out=lnt, in_=pf[sb_], func=AF.Ln, bias=EPS, scale=1.0 / 64),
                      reads=[("pf", sb_)], writes=["lnt"])
                S.add("act", lambda e: e.activation(out=rstdq, in_=lnt, func=AF.Exp, scale=-0.5),
                      reads=["lnt"], writes=["rstdq"])
                S.add("dve", lambda e: e.scalar_tensor_tensor(out=dst, in0=pf[bk], scalar=gcol, in1=rstdq,
                                                              op0=ALU.mult, op1=ALU.mult),
                      reads=[("pf", bk), "rstdq", "spT", "qgs"] + list(extra_reads), writes=dstkeys)

            kr = Rot(range(2))
            for u in range(2):
                ws = load_w(w_in_l[L][:, C_K + u * 256:C_K + (u + 1) * 256], 8, 256)
                for ci in range(2):
                    hc = u * 2 + ci
                    for sl in range(2):
                        bk = gen.next()
                        proj_fm(ws, ci * 128, sl, bk)
                        ks = kr.next()
                        qk_norm(bk, spT[:, SP_KG:SP_KG + 1], kst[ks], [("kst", ks)])
                        c0 = tok0 + sl * 512
                        S.add("sp", lambda e, ks=ks, hc=hc, c0=c0: e.dma_start(
                            out=KT_d[hc * 128:(hc + 1) * 128, c0:c0 + 512], in_=kst[ks]),
                            reads=[("kst", ks)], writes=[("KTd", hc, p, sl)], dma=f"kst{ks}")

            def silu_from_psum(bk, dst, dstkey):
                S.add("act", lambda e: e.activation(out=et, in_=pf[bk], func=AF.Tanh, scale=0.5),
                      reads=[("pf", bk)], writes=["et"])
                S.add("dve", lambda e: e.scalar_tensor_tensor(out=dst, in0=et, scalar=1.0, in1=pf[bk],
                                                              op0=ALU.add, op1=ALU.mult),
                      reads=[("pf", bk), "et"], writes=[dstkey])

            if p == 0:
                S.add("dve", lambda e: e.memset(aT[:, :, 0:32], 0.0), writes=["aT"])
            else:
                S.add("dve", lambda e: e.tensor_copy(out=aT[:, :, 0:32], in_=aT[:, :, PASS:PASS + 32]),
                      reads=["aT"], writes=["aT"])
            for u in range(2):
                ws = load_w(w_in_l[L][:, C_AG + u * 256:C_AG + (u + 1) * 256], 8, 256)
                for ci in range(2):
                    for sl in range(2):
                        bk = gen.next()
                        proj_fm(ws, ci * 128, sl, bk)
                        S.add("act", lambda e, bk=bk, ci=ci, sl=sl: e.activation(
                            out=ew1[:, ci, sl * 512:(sl + 1) * 512], in_=pf[bk], func=AF.Tanh, scale=0.5),
                            reads=[("pf", bk)], writes=["ew1"])
                ws = load_w(w_in_l[L][:, C_AV + u * 256:C_AV + (u + 1) * 256], 8, 256)
                for ci in range(2):
                    cc = u * 2 + ci
                    for sl in range(2):
                        bk = gen.next()
                        proj_fm(ws, ci * 128, sl, bk)
                        S.add("dve", lambda e, bk=bk, cc=cc, ci=ci, sl=sl: e.scalar_tensor_tensor(
                            out=aT[:, cc, 32 + sl * 512:32 + (sl + 1) * 512], in0=ew1[:, ci, sl * 512:(sl + 1) * 512],
                            scalar=1.0, in1=pf[bk], op0=ALU.add, op1=ALU.mult),
                            reads=[("pf", bk), "ew1"], writes=["aT"])

            def conv_tap(sl, cc, k):
                off = 32 + sl * 512 - (30 - k)
                src = aT[:, cc, off:off + 512]
                wcol = cwh[:, cc * 31 + k:cc * 31 + k + 1]
                if k == 30:
                    S.add("dve", lambda e: e.tensor_scalar(
                        out=acc[:, cc, :], in0=src, scalar1=wcol, scalar2=spT[:, SP_CB + cc:SP_CB + cc + 1],
                        op0=ALU.mult, op1=ALU.add), reads=["aT", "spT", "cwh"], writes=[("acc", cc)])
                else:
                    S.add("dve", lambda e: e.scalar_tensor_tensor(
                        out=acc[:, cc, :], in0=src, scalar=wcol, in1=acc[:, cc, :], op0=ALU.mult, op1=ALU.add),
                        reads=["aT", "cwh", ("acc", cc)], writes=[("acc", cc)])

            def ln_slot(sl):
                b1, b2 = 4, 5
                for cc in range(4):
                    mm(pf[b1], onesf, acc[:, cc, :], cc == 0, cc == 3, ["onesf", ("acc", cc)], [("pf", b1)])
                for cc in range(4):
                    yq = cc % 2
                    S.add("act", lambda e, cc=cc, yq=yq: e.activation(out=ysq[yq], in_=acc[:, cc, :], func=AF.Square),
                          reads=[("acc", cc)], writes=[("ysq", yq)])
                    mm(pf[b2], onesf, ysq[yq], cc == 0, cc == 3, ["onesf", ("ysq", yq)], [("pf", b2)])
                S.add("dve", lambda e: e.tensor_scalar(out=mean, in0=pf[b1], scalar1=1.0 / 512, scalar2=None,
                                                       op0=ALU.mult), reads=[("pf", b1)], writes=["mean"])
                S.add("dve", lambda e: e.tensor_tensor(out=var, in0=mean, in1=mean, op=ALU.mult),
                      reads=["mean"], writes=["var"])
                S.add("dve", lambda e: e.scalar_tensor_tensor(out=var, in0=pf[b2], scalar=1.0 / 512, in1=var,
                                                              op0=ALU.mult, op1=ALU.subtract),
                      reads=[("pf", b2), "var"], writes=["var"])
                S.add("act", lambda e: e.activation(out=var, in_=var, func=AF.Ln, bias=EPS, scale=1.0),
                      reads=["var"], writes=["var"])
                S.add("act", lambda e: e.activation(out=rstdc, in_=var, func=AF.Exp, scale=-0.5),
                      reads=["var"], writes=["rstdc"])
                for cc in range(4):
                    S.add("dve", lambda e, cc=cc: e.tensor_tensor(out=zt, in0=acc[:, cc, :], in1=mean, op=ALU.subtract),
                          reads=[("acc", cc), "mean"], writes=["zt"])
                    S.add("dve", lambda e: e.tensor_tensor(out=zt, in0=zt, in1=rstdc, op=ALU.mult),
                          reads=["zt", "rstdc"], writes=["zt"])
                    S.add("dve", lambda e, cc=cc: e.tensor_scalar(
                        out=zt, in0=zt, scalar1=spT[:, SP_CNG + cc:SP_CNG + cc + 1],
                        scalar2=spT[:, SP_CNB + cc:SP_CNB + cc + 1], op0=ALU.mult, op1=ALU.add),
                        reads=["zt", "spT"], writes=["zt"])
                    S.add("act", lambda e: e.activation(out=et, in_=zt, func=AF.Tanh, scale=0.5),
                          reads=["zt"], writes=["et"])
                    S.add("dve", lambda e, cc=cc: e.scalar_tensor_tensor(
                        out=abr[:, cc, sl * 512:(sl + 1) * 512], in0=et, scalar=1.0, in1=zt, op0=ALU.add, op1=ALU.mult),
                        reads=["zt", "et"], writes=["abr"])

            convq = []
            for sl_ in range(2):
                for cpair in range(2):
                    for k in range(30, -1, -1):
                        for cc_ in (2 * cpair, 2 * cpair + 1):
                            convq.append((conv_tap, (sl_, cc_, k)))
                convq.append((ln_slot, (sl_,)))
            convq.reverse()

            def pull_conv(nitems):
                while nitems > 0 and convq:
                    f, a = convq.pop()
                    f(*a)
                    nitems -= 1

            S.barrier(A1)
            for sl in range(2):
                n = 4 * (2 * p + sl) + 4
                jref = p * 8 + 4 * sl + 2
                S.add("dve", lambda e, sl=sl, n=n, jref=jref: e.tensor_tensor(
                    out=biasT[:, sl, 0:n, :], in0=Pt[:, 0:n, :],
                    in1=EX[:, jref:jref + 1, :].broadcast_to([128, n, 8]), op=ALU.subtract),
                    reads=["Pt", "EX", A1], writes=["biasT"])
            for i in range(2):
                S.add("dve", lambda e, i=i: e.memset(Vb[i][:, :, 64:128], 1.0), reads=[A1], writes=[("V", i)])
            ptr = Rot(range(3))
            orot = Rot([2, 3])

            def attn_prep(hc):
                bi = hc % 2
                nk = nblk_total * 128
                S.add("sp", lambda e: e.dma_start(
                    out=KTb[bi][:, 0:nk], in_=KT_d[hc * 128:(hc + 1) * 128, 0:nk]),
                    reads=[("KTd", hc, pp, s_) for pp in range(p + 1) for s_ in range(2)] + [A1],
                    writes=[("KT", bi)], dma=f"kt{bi}")
                for e_ in range(2):
                    cs = hc * 128 + e_ * 64
                    for j0 in range(0, nblk_total, 8):
                        S.add("sp", lambda e, cs=cs, e_=e_, j0=j0: e.dma_start(
                            out=Vb[bi][:, j0:j0 + 8, e_ * 128:e_ * 128 + 64],
                            in_=V_d[j0 * 128:(j0 + 8) * 128, cs:cs + 64].rearrange("(j q) c -> q j c", q=128)),
                            reads=[("Vd", pp, t_) for pp in range(p + 1) for t_ in range(8)] + [A1],
                            writes=[("V", bi)], dma=f"v{bi}")
                ws = load_w(w_in_l[L][:, C_Q + hc * 128:C_Q + (hc + 1) * 128], 8, 128)
                for sl in range(2):
                    proj_fm(ws, 0, sl, 4)
                    qk_norm(4, qgs[:, 0:1], qT[bi][:, sl * 512:(sl + 1) * 512], [("qT", bi)], extra_reads=[A1], sbank=5)
                ws = load_w(w_in_l[L][:, C_CZ + hc * 128:C_CZ + (hc + 1) * 128], 8, 128)
                for sl in range(2):
                    bk = 4 + (sl % 2)
                    proj_fm(ws, 0, sl, bk)
                    S.add("act", lambda e, bk=bk: e.activation(out=lnt, in_=pf[bk], func=AF.Tanh, scale=0.5),
                          reads=[("pf", bk)], writes=["lnt"])
                    S.add("dve", lambda e, bk=bk, sl=sl: e.scalar_tensor_tensor(
                        out=scz[bi][:, sl * 512:(sl + 1) * 512], in0=lnt, scalar=1.0, in1=pf[bk],
                        op0=ALU.add, op1=ALU.mult),
                        reads=[("pf", bk), "lnt", A1], writes=[("scz", bi)])

            def attn_iter(hc, sl, e_):
                bi = hc % 2
                n = 4 * (2 * p + sl) + 4
                h = 2 * hc + e_
                rows = slice(64 * e_, 64 * e_ + 64)
                drows = slice(64 * (1 - e_), 64 * (1 - e_) + 64)
                ob = orot.next()
                qv = qT[bi][rows, sl * 512:(sl + 1) * 512]
                pts = {}

                def emit_s(j):
                    sbk = j % 2
                    diag = j >= n - 4
                    mm(pf[sbk], KTb[bi][rows, j * 128:(j + 1) * 128], qv, True, not diag,
                       [("KT", bi), ("qT", bi), A1], [("pf", sbk)])
                    if diag:
                        mm(pf[sbk], identb, maskU[:, j - (n - 4), :], False, True,
                           ["identb", "maskU"], [("pf", sbk)])
                    pt = ptr.next()
                    pts[j] = pt
                    S.add("act", lambda e: e.activation(
                        out=PT[pt], in_=pf[sbk], func=AF.Exp, bias=biasT[:, sl, j, h:h + 1], scale=1.0),
                        reads=[("pf", sbk), "biasT", A1], writes=[("PT", pt)])

                def emit_pv(j):
                    pt = pts[j]
                    mm(pf[ob], Vb[bi][:, j, e_ * 64:e_ * 64 + 128], PT[pt], j == 0, j == n - 1,
                       [("V", bi), ("PT", pt), A1], [("pf", ob)])

                emit_s(0)
                for j in range(n):
                    if j + 1 < n:
                        emit_s(j + 1)
                    emit_pv(j)
                tsl = slice(sl * 512, (sl + 1) * 512)
                S.add("dve", lambda e: e.reciprocal(out=rden[rows, :], in_=pf[ob][drows, :]),
                      reads=[("pf", ob), A1], writes=["rden"])
                S.add("dve", lambda e: e.tensor_tensor(
                    out=tmpO[rows, :], in0=pf[ob][rows, :], in1=scz[bi][rows, tsl], op=ALU.mult),
                    reads=[("pf", ob), ("scz", bi), A1], writes=["tmpO"])
                S.add("dve", lambda e: e.scalar_tensor_tensor(
                    out=cbr[rows, hc, tsl], in0=tmpO[rows, :], scalar=0.5, in1=rden[rows, :], op0=ALU.mult, op1=ALU.mult),
                    reads=["tmpO", "rden", A1], writes=["cbr"])

            attn_prep(0)
            for hc in range(4):
                if hc + 1 < 4:
                    attn_prep(hc + 1)
                for sl in range(2):
                    for e_ in range(2):
                        attn_iter(hc, sl, e_)
                        pull_conv(17)
            pull_conv(10 ** 6)
            S.barrier(A1)

            for u in range(2):
                ws = load_w(w_in_l[L][:, C_AZ + u * 256:C_AZ + (u + 1) * 256], 8, 256)
                for ci in range(2):
                    cc = u * 2 + ci
                    for sl in range(2):
                        bk = gen.next()
                        proj_fm(ws, ci * 128, sl, bk)
                        tsl = slice(sl * 512, (sl + 1) * 512)
                        silu_from_psum(bk, zt, "zt")
                        S.add("dve", lambda e, cc=cc, tsl=tsl: e.scalar_tensor_tensor(
                            out=abr[:, cc, tsl], in0=abr[:, cc, tsl], scalar=0.25, in1=zt, op0=ALU.mult, op1=ALU.mult),
                            reads=["abr", "zt"], writes=["abr"])

            for u in range(2):
                ws = load_w(w_in_l[L][:, C_BZ + u * 256:C_BZ + (u + 1) * 256], 8, 256)
                for ci in range(2):
                    for sl in range(2):
                        bk = gen.next()
                        proj_fm(ws, ci * 128, sl, bk)
                        silu_from_psum(bk, ew1[:, ci, sl * 512:(sl + 1) * 512], "ew1")
                ws = load_w(w_in_l[L][:, C_U + u * 256:C_U + (u + 1) * 256], 8, 256)
                for ci in range(2):
                    for sl in range(2):
                        bk = gen.next()
                        proj_fm(ws, ci * 128, sl, bk)
                        tsl = slice(sl * 512, (sl + 1) * 512)
                        S.add("dve", lambda e, bk=bk, ci=ci, tsl=tsl: e.tensor_tensor(
                            out=ew2[:, ci, tsl], in0=pf[bk], in1=ew1[:, ci, tsl], op=ALU.mult),
                            reads=[("pf", bk), "ew1"], writes=["ew2"])
                for ci in range(2):
                    g = u * 2 + ci
                    for sl in range(2):
                        bk = gen.next()
                        for t4 in range(4):
                            tb = sl * 4 + t4
                            mm(pf[bk][:, t4 * 128:(t4 + 1) * 128], vn[:, tb, g * 128:(g + 1) * 128], WsT[:, g, :],
                               True, False, ["vn", "WsT"], [("pf", bk)])
                            mm(pf[bk][:, t4 * 128:(t4 + 1) * 128], onesf[0:1, :], bsf[0:1, g, :],
                               False, True, ["onesf", "bsf"], [("pf", bk)])
                        tsl = slice(sl * 512, (sl + 1) * 512)
                        S.add("dve", lambda e, bk=bk, g=g, ci=ci, tsl=tsl: e.scalar_tensor_tensor(
                            out=bbr[:, g, tsl], in0=pf[bk], scalar=0.5, in1=ew2[:, ci, tsl], op0=ALU.mult, op1=ALU.mult),
                            reads=[("pf", bk), "ew2"], writes=["bbr"])

            brs = ((abr, "abr", w_a_d), (bbr, "bbr", w_b_d), (cbr, "cbr", w_c_d))
            for oc in range(8):
                for i, (br, brk, wd) in enumerate(brs):
                    gc = C_GATE + i * 1024 + oc * 128
                    wg = load_w(w_in_l[L][:, gc:gc + 128], 8, 128)
                    wy = load_w(wd[L, :, oc * 128:(oc + 1) * 128], 4, 128)
                    for sl in range(2):
                        tsl = slice(sl * 512, (sl + 1) * 512)
                        bg = gen.next()
                        proj_fm(wg, 0, sl, bg)
                        by = gen.next()
                        proj_fm(wy, 0, sl, by, nkc=4, src=br, srckey=brk)
                        S.add("act", lambda e, bg=bg, i=i, oc=oc: e.activation(
                            out=gden[i], in_=pf[bg], func=AF.Tanh, bias=nbg[:, i * 8 + oc:i * 8 + oc + 1], scale=0.5),
                            reads=[("pf", bg), "nbg", A1], writes=[("gden", i)])
                        if i == 0:
                            S.add("dve", lambda e, by=by, i=i, sl=sl: e.scalar_tensor_tensor(
                                out=macc[sl], in0=gden[i], scalar=1.0, in1=pf[by], op0=ALU.add, op1=ALU.mult),
                                reads=[("pf", by), ("gden", i), A1], writes=[("macc", sl)])
                        else:
                            S.add("dve", lambda e, by=by, i=i: e.scalar_tensor_tensor(
                                out=mt, in0=gden[i], scalar=1.0, in1=pf[by], op0=ALU.add, op1=ALU.mult),
                                reads=[("pf", by), ("gden", i), A1], writes=["mt"])
                            if i == 1:
                                S.add("dve", lambda e, sl=sl: e.tensor_tensor(out=macc[sl], in0=macc[sl], in1=mt,
                                                                               op=ALU.add),
                                      reads=[("macc", sl), "mt", A1], writes=[("macc", sl)])
                            else:
                                S.add("dve", lambda e, oc=oc, tsl=tsl, sl=sl: e.tensor_tensor(
                                    out=mergedT[:, oc, tsl], in0=macc[sl], in1=mt, op=ALU.add),
                                    reads=[("macc", sl), "mt", A1], writes=["mergedT"])

            for tb in range(8):
                r0 = tok0 + tb * 128
                S.add("sp", lambda e, tb=tb, r0=r0, xsrc=xsrc: e.dma_start(out=xh[tb % 2], in_=xsrc[r0:r0 + 128, :]),
                      reads=[(srckey, p, tb), A1], writes=[("xh", tb % 2)], dma=f"xh{tb % 2}")
                for half in range(2):
                    bk = gen.next()
                    for kc in range(8):
                        mm(pf[bk], mergedT[:, kc, tb * 128:(tb + 1) * 128], wout[:, kc, half * 512:(half + 1) * 512],
                           kc == 0, kc == 7, ["mergedT", ("wout", 2 * half), ("wout", 2 * half + 1), A1], [("pf", bk)])
                    S.add("dve", lambda e, bk=bk, tb=tb, half=half: e.scalar_tensor_tensor(
                        out=xo[tb % 2][:, half * 512:(half + 1) * 512], in0=pf[bk], scalar=0.5,
                        in1=xh[tb % 2][:, half * 512:(half + 1) * 512], op0=ALU.mult, op1=ALU.add),
                        reads=[("pf", bk), ("xh", tb % 2), A1], writes=[("xo", tb % 2)])
                S.add("sp", lambda e, tb=tb, r0=r0, xdst=xdst: e.dma_start(out=xdst[r0:r0 + 128, :], in_=xo[tb % 2]),
                      reads=[("xo", tb % 2), A1], writes=[(dstkey, p, tb)], dma=f"xo{tb % 2}")
            S.barrier(A1)

    print("MK ops:", len(S.ops), flush=True)
    fin = S.add("sp", None)
    S.ops[fin].deps = S.ops[fin].deps if S.ops[fin].fn is not None else {}
    for op in S.ops:
        if op.fn is not None and op.dma is not None and op.dma.startswith("xo"):
            S._dep(S.ops[fin], op.idx)
    S.emit()
    return nc


def _consts():
    ident = np.eye(128, dtype=np.float32)
    triu = np.triu(np.ones((128, 128), np.float32))
    tril = np.tril(np.ones((128, 128), np.float32))
    k = np.arange(128)[:, None, None]
    m = np.arange(4)[None, :, None]
    q = np.arange(512)[None, None, :]
    maskU = np.where(128 * m + k > q, -30000.0, 0.0).astype(np.float32)
    return ident, triu, tril, maskU


def _pack_small(inp):
    L = 2
    sp = np.zeros((L, 128, NSP), np.float32)
    for l in range(L):
        sp[l, :, SP_G:SP_G + 8] = inp["norm_g"][l].reshape(8, 128).T
        sp[l, :, SP_BG:SP_BG + 24] = inp["b_gate"][l].reshape(24, 128).T
        cw = inp["conv_w"][l]
        sp[l, :, SP_CW:SP_CW + 124] = cw.reshape(31, 4, 128).transpose(2, 1, 0).reshape(128, 124)
        sp[l, :, SP_CB:SP_CB + 4] = inp["conv_b"][l].reshape(4, 128).T
        sp[l, :, SP_CNG:SP_CNG + 4] = inp["conv_norm_g"][l].reshape(4, 128).T
        sp[l, :, SP_CNB:SP_CNB + 4] = inp["conv_norm_b"][l].reshape(4, 128).T
        sp[l, :, SP_QG] = np.tile(inp["q_norm_g"][l], 2)
        sp[l, :, SP_KG] = np.tile(inp["k_norm_g"][l], 2)
    return sp


_NC_CACHE = {}


def kernel(**inputs):
    inp = {k: np.asarray(v, dtype=np.float32) for k, v in inputs.items()}
    NL = int(os.environ.get("MK_NL", "2"))
    NP = int(os.environ.get("MK_NP", "4"))
    key = (NL, NP)
    if key not in _NC_CACHE:
        _NC_CACHE[key] = build(NL, NP)
    nc = _NC_CACHE[key]
    ident, triu, tril, maskU = _consts()
    spT = _pack_small(inp)
    shared = {
        "w_in0": np.ascontiguousarray(inp["w_in"][0]), "w_in1": np.ascontiguousarray(inp["w_in"][1]), "w_a": inp["w_a"], "w_b": inp["w_b"], "w_c": inp["w_c"], "w_out": inp["w_out"],
        "w_s": inp["w_s"], "b_s": inp["b_s"], "gmlp_norm_g": inp["gmlp_norm_g"], "b_f": inp["b_f"],
        "spT": spT, "ident": ident, "triu": triu, "tril": tril, "maskU": maskU,
    }
    in_maps = []
    for c in range(8):
        m = dict(shared)
        m["x"] = np.ascontiguousarray(inp["x"][c // 2])
        in_maps.append(m)
    res = run_bass_kernel_spmd(nc, in_maps, core_ids=list(range(8)))
    out = np.empty((4, SEQ, D), np.float32)
    for b in range(4):
        out[b, :SEQ // 2] = res.results[2 * b]["out"][:SEQ // 2]
        out[b, SEQ // 2:] = res.results[2 * b + 1]["out"][SEQ // 2:]
    return out
```

```python
import os
import numpy as np
import concourse.bass as bass
import concourse.mybir as mybir
from concourse.bass_utils import run_bass_kernel_spmd

F32 = mybir.dt.float32
BF16 = mybir.dt.bfloat16
ALU = mybir.AluOpType
AF = mybir.ActivationFunctionType

D = 1024
SEQ = 4096
NIN = 8200
EPS = 1e-6
PASS = 1024
NSP = 170
C_GATE, C_AV, C_AG, C_AZ, C_U, C_V, C_BZ, C_Q, C_K, C_VV, C_CZ, C_F = (
    0, 3072, 3584, 4096, 4608, 5120, 5632, 6144, 6656, 7168, 7680, 8192)
SP_G, SP_BG, SP_CW, SP_CB, SP_CNG, SP_CNB, SP_QG, SP_KG = 0, 8, 32, 156, 160, 164, 168, 169

ENGS = ("pe", "act", "dve", "pool", "sp")


class Op:
    __slots__ = ("eng", "fn", "deps", "dma", "signal", "count", "idx", "grp")

    def __init__(self, eng, fn, dma):
        self.eng = eng
        self.fn = fn
        self.dma = dma
        self.deps = {}
        self.signal = False
        self.count = 0
        self.grp = ("d:" + dma) if dma is not None else ("e:" + eng)


class Sched:
    def __init__(self, nc):
        self.nc = nc
        self.ops = []
        self.last_w = {}
        self.readers = {}
        self.maxops = int(os.environ.get("MK_MAXOPS", "0")) or None

    def _dep(self, op, d):
        src = self.ops[d]
        if src.fn is None:
            for g, i in src.deps.items():
                if op.deps.get(g, -1) < i:
                    op.deps[g] = i
            return
        if op.deps.get(src.grp, -1) < d:
            op.deps[src.grp] = d

    def add(self, eng, fn, reads=(), writes=(), dma=None):
        if self.maxops is not None and len(self.ops) >= self.maxops:
            return len(self.ops) - 1
        op = Op(eng, fn, dma)
        op.idx = len(self.ops)
        self.ops.append(op)
        excl = [b for b in reads if isinstance(b, tuple) and b[0] in ("pf", "pb")]
        if excl:
            reads = [b for b in reads if b not in excl]
            writes = list(writes) + excl
        for b in reads:
            w = self.last_w.get(b)
            if w is not None:
                self._dep(op, w)
        for b in writes:
            w = self.last_w.get(b)
            if w is not None:
                self._dep(op, w)
            for r in self.readers.get(b, ()):
                self._dep(op, r)
        for b in reads:
            self.readers.setdefault(b, []).append(op.idx)
        for b in writes:
            self.last_w[b] = op.idx
            self.readers[b] = []
        return op.idx

    def barrier(self, key):
        return self.add("sp", None, writes=[key])

    def emit(self):
        nc = self.nc
        ops = self.ops
        for op in ops:
            if op.fn is None:
                continue
            for g, d in op.deps.items():
                src = ops[d]
                if src.dma is None and src.eng == "pe" and op.eng == "pe" and op.dma is None:
                    continue
                src.signal = True
        counters = {}
        for op in ops:
            if op.fn is None:
                continue
            if op.dma is not None:
                counters[op.grp] = counters.get(op.grp, 0) + 16
                op.count = counters[op.grp]
            elif op.signal:
                counters[op.grp] = counters.get(op.grp, 0) + 1
                op.count = counters[op.grp]
        sems = {k: nc.alloc_semaphore(name="s_" + k.replace(":", "_")) for k in counters}
        per_eng = {e: [] for e in ENGS}
        for op in ops:
            per_eng[op.eng].append(op)

        def run(eng_name, eng):
            waited = {}
            for op in per_eng[eng_name]:
                if op.fn is None:
                    if op.deps and op is ops[-1]:
                        pass
                    else:
                        continue
                for g, d in op.deps.items():
                    src = ops[d]
                    if src.dma is None and src.eng == "pe" and eng_name == "pe" and op.dma is None:
                        continue
                    if waited.get(g, 0) >= src.count:
                        continue
                    eng.wait_ge(sems[g], src.count)
                    waited[g] = src.count
                if op.fn is None:
                    continue
                ins = op.fn(eng)
                if op.dma is not None:
                    ins.then_inc(sems[op.grp], 16)
                elif op.signal:
                    ins.then_inc(sems[op.grp], 1)

        with nc.allow_low_precision(reason="bf16 activations feeding bf16 matmuls"), nc.Block() as block:
            @block.tensor
            def _(e):
                run("pe", e)

            @block.scalar
            def _(e):
                run("act", e)

            @block.vector
            def _(e):
                run("dve", e)

            @block.gpsimd
            def _(e):
                run("pool", e)

            @block.sync
            def _(e):
                run("sp", e)


class Rot:
    def __init__(self, items):
        self.items = list(items)
        self.i = 0

    def next(self):
        it = self.items[self.i % len(self.items)]
        self.i += 1
        return it


def build(NL=2, NP=4, dbg=False):
    nc = bass.Bass("TRN2", target_bir_lowering=False)
    STOP = int(os.environ.get("MK_STOP", "9"))
    SUB = int(os.environ.get("MK_SUB", "9"))

    def din(name, shape, dt=F32):
        return nc.dram_tensor(name, list(shape), dt, kind="ExternalInput").ap()

    x_d = din("x", [SEQ, D])
    w_in_l = [din("w_in0", [D, NIN]), din("w_in1", [D, NIN])]
    w_a_d = din("w_a", [2, 512, D])
    w_b_d = din("w_b", [2, 512, D])
    w_c_d = din("w_c", [2, 512, D])
    w_out_d = din("w_out", [2, D, D])
    w_s_d = din("w_s", [2, 4, 128, 128])
    b_s_d = din("b_s", [2, 4, 128])
    gg_d = din("gmlp_norm_g", [2, 512])
    bf_d = din("b_f", [2, 8])
    sp_d = din("spT", [2, 128, NSP])
    ident_d = din("ident", [128, 128])
    triu_d = din("triu", [128, 128])
    tril_d = din("tril", [128, 128])
    maskU_d = din("maskU", [128, 4, 512])
    out_d = nc.dram_tensor("out", [SEQ, D], F32, kind="ExternalOutput").ap()
    xmid_d = nc.dram_tensor("xmid", [SEQ, D], F32, kind="ExternalOutput").ap()
    KT_d = nc.dram_tensor("KTd", [512, SEQ], BF16, kind="ExternalOutput").ap()
    V_d = nc.dram_tensor("Vd", [SEQ, 512], BF16, kind="ExternalOutput").ap()

    def sb(name, shape, dt):
        return nc.alloc_sbuf_tensor(name, list(shape), dt).ap()

    identb = sb("identb", [128, 128], BF16)
    onesf = sb("onesf", [128, 128], F32)
    blockones = sb("blockones", [128, 128], BF16)
    triuf = sb("triuf", [128, 128], F32)
    trilf = sb("trilf", [128, 128], F32)
    maskU = sb("maskU_s", [128, 4, 512], BF16)
    spT = sb("spT_s", [128, NSP], F32)
    qgs = sb("qgs", [128, 1], F32)
    nbg = sb("nbg", [128, 24], F32)
    cwh = sb("cwh", [128, 124], F32)
    ggbc = sb("ggbc", [128, 512], F32)
    bfbc = sb("bfbc", [128, 8], F32)
    bsf = sb("bsf", [1, 4, 128], F32)
    wsf = sb("wsf", [128, 4, 128], F32)
    wsm = sb("wsm", [128, 4, 128], BF16)
    WsT = sb("WsT", [128, 4, 128], BF16)
    wout = sb("wout", [128, 8, 1024], BF16)
    LF = sb("LF", [128, 32, 8], F32)
    TOTs = sb("TOTs", [128, 32, 8], F32)
    EX = sb("EX", [128, 32, 8], F32)
    Pt = sb("Pt", [128, 32, 8], F32)
    fpb = sb("fpb", [128, 64], F32)
    hT = sb("hT", [128, 8, PASS], BF16)
    wbuf = [sb(f"wbuf{i}", [128, 8, 256], BF16) for i in range(4)]
    cbr = sb("cbr", [128, 4, PASS], BF16)
    abr = sb("abr", [128, 4, PASS], BF16)
    bbr = sb("bbr", [128, 4, PASS], BF16)
    vn = sb("vn", [128, 8, 512], BF16)
    ssqv = sb("ssqv", [128, 8, 2], F32)
    ssqv1 = sb("ssqv1", [128, 8], F32)
    rstdv = sb("rstdv", [128, 8], F32)
    ssq = sb("ssq", [128, 8], F32)
    rstd = sb("rstd", [128, 8], F32)
    ksq = [sb(f"ksq{i}", [128, 512], BF16) for i in range(2)]
    lnt = sb("lnt", [128, 512], F32)
    rstdq = sb("rstdq", [128, 512], F32)
    kst = [sb(f"kst{i}", [128, 512], BF16) for i in range(2)]
    aT = sb("aT", [128, 4, 32 + PASS], BF16)
    acc = sb("acc", [128, 4, 512], F32)
    ysq = [sb(f"ysq{i}", [128, 512], F32) for i in range(2)]
    ew1 = sb("ew1", [128, 2, PASS], BF16)
    ew2 = sb("ew2", [128, 2, PASS], BF16)
    et = sb("et", [128, 512], F32)
    mean = sb("mean", [128, 512], F32)
    var = sb("var", [128, 512], F32)
    rstdc = sb("rstdc", [128, 512], F32)
    zt = sb("zt", [128, 512], F32)
    AR = 60 * 1024 // 2
    arena = sb("arena", [128, AR], BF16)

    def carve(off, nbytes, dt, shape=None):
        v = arena[:, off // 2:(off + nbytes) // 2]
        if dt == F32:
            v = v.bitcast(F32)
        if shape is not None and len(shape) == 3:
            v = v.rearrange("p (a b) -> p a b", a=shape[1])
        return v

    K = 1024
    xin = [carve(0, 4 * K, F32), carve(4 * K, 4 * K, F32)]
    xs = [carve(8 * K, 2 * K, BF16), carve(10 * K, 2 * K, BF16)]
    junk = carve(12 * K, 2 * K, BF16)
    vstage = carve(14 * K, 8 * K, BF16, [128, 8, 512])
    KTb = [carve(0, 8 * K, BF16), carve(8 * K, 8 * K, BF16)]
    Vb = [carve(16 * K, 12 * K, BF16, [128, 32, 192]), carve(28 * K, 12 * K, BF16, [128, 32, 192])]
    PT = [carve(40 * K + i * K, K, BF16) for i in range(3)]
    biasT = carve(43 * K, 2 * K, F32).rearrange("p (s j h) -> p s j h", s=2, j=32)
    rden = carve(45 * K, 2 * K, F32)
    tmpO = carve(47 * K, 2 * K, F32)
    qT = [carve(49 * K, 2 * K, BF16), carve(51 * K, 2 * K, BF16)]
    scz = [carve(53 * K, 2 * K, BF16), carve(55 * K, 2 * K, BF16)]
    mergedT = carve(0, 16 * K, BF16, [128, 8, PASS])
    gden = [carve(16 * K + i * 2 * K, 2 * K, F32) for i in range(3)]
    macc = [carve(22 * K, 2 * K, F32), carve(24 * K, 2 * K, F32)]
    mt = carve(26 * K, 2 * K, F32)
    xo = [carve(28 * K, 4 * K, F32), carve(32 * K, 4 * K, F32)]
    xh = [carve(36 * K, 4 * K, F32), carve(40 * K, 4 * K, F32)]

    pf = [nc.alloc_psum_tensor(f"pf{i}", [128, 512], F32).ap() for i in range(6)]
    pb = [nc.alloc_psum_tensor(f"pb{i}", [128, 1024], BF16).ap() for i in range(2)]
    pbf = [t.bitcast(F32) for t in pb]

    S = Sched(nc)
    A1 = "ar1"
    wrot = Rot(range(4))
    crot = Rot(range(4))
    gen = Rot(range(6))

    def mm(out, lhsT, rhs, start, stop, reads, writes):
        S.add("pe", lambda e: e.matmul(out, lhsT=lhsT, rhs=rhs, start=start, stop=stop), reads=reads, writes=writes)

    def cast_dma(dst, src, wkeys, grp):
        S.add("pool", lambda e: e.dma_start(out=dst, in_=src), writes=wkeys, dma=grp)

    def load_w(src2d, nkc, ncols):
        s = wrot.next()
        dst = wbuf[s][:, 0:nkc, 0:ncols]
        src = src2d.rearrange("(kc p) c -> p kc c", p=128)
        S.add("pool", lambda e: e.dma_start(out=dst, in_=src), writes=[("w", s)], dma=f"w{s}")
        return s

    def proj_fm(ws, col0, sl, bank, nkc=8, src=None, srckey="hT"):
        src = hT if src is None else src
        for kc in range(nkc):
            mm(pf[bank][:, :], wbuf[ws][:, kc, col0:col0 + 128], src[:, kc, sl * 512:(sl + 1) * 512],
               kc == 0, kc == nkc - 1, [("w", ws), srckey], [("pf", bank)])

    cast_dma(identb, ident_d, ["identb"], "cdid")
    cast_dma(maskU, maskU_d, ["maskU"], "cdmu")
    S.add("sp", lambda e: e.dma_start(out=triuf, in_=triu_d), writes=["triuf"], dma="c2")
    S.add("sp", lambda e: e.dma_start(out=trilf, in_=tril_d), writes=["trilf"], dma="c3")
    S.add("dve", lambda e: e.memset(onesf, 1.0), writes=["onesf"])
    S.add("dve", lambda e: e.memset(blockones, 0.0), writes=["blockones"])
    S.add("dve", lambda e: e.memset(blockones[0:64, 0:64], 1.0), writes=["blockones"])
    S.add("dve", lambda e: e.memset(blockones[64:128, 64:128], 1.0), writes=["blockones"])

    for L in range(NL):
        xsrc = x_d if L == 0 else xmid_d
        xdst = out_d if L == NL - 1 else xmid_d
        srckey = "xin_d" if L == 0 else "xmid"
        dstkey = "out_d" if L == NL - 1 else "xmid"
        S.add("sp", lambda e, L=L: e.dma_start(out=spT, in_=sp_d[L]), writes=["spT"], dma="p0")
        S.add("sp", lambda e, L=L: e.dma_start(out=ggbc, in_=gg_d[L:L + 1, :].broadcast_to([128, 512])),
              writes=["ggbc"], dma="p1")
        S.add("sp", lambda e, L=L: e.dma_start(out=bfbc, in_=bf_d[L:L + 1, :].broadcast_to([128, 8])),
              writes=["bfbc"], dma="p2")
        S.add("sp", lambda e, L=L: e.dma_start(out=bsf, in_=b_s_d[L:L + 1, :, :]), writes=["bsf"], dma="p3")
        S.add("sp", lambda e, L=L: e.dma_start(out=wsf, in_=w_s_d[L].rearrange("g t s -> t g s")),
              writes=["wsf"], dma="p4")
        S.add("dve", lambda e: e.tensor_scalar(out=qgs, in0=spT[:, SP_QG:SP_QG + 1], scalar1=0.125, scalar2=None,
                                               op0=ALU.mult), reads=["spT"], writes=["qgs"])
        S.add("dve", lambda e: e.tensor_scalar(out=nbg, in0=spT[:, SP_BG:SP_BG + 24], scalar1=0.5, scalar2=None,
                                               op0=ALU.mult), reads=["spT"], writes=["nbg"])
        S.add("dve", lambda e: e.tensor_scalar(out=cwh, in0=spT[:, SP_CW:SP_CW + 124], scalar1=0.5, scalar2=None,
                                               op0=ALU.mult), reads=["spT"], writes=["cwh"])
        S.add("dve", lambda e: e.tensor_tensor(out=wsm, in0=wsf, in1=trilf.unsqueeze(1).broadcast_to([128, 4, 128]),
                                               op=ALU.mult), reads=["wsf", "trilf"], writes=["wsm"])
        for g in range(4):
            S.add("pe", lambda e, g=g: e.transpose(pb[0][:, g * 128:(g + 1) * 128], wsm[:, g, :], identb),
                  reads=["wsm", "identb"], writes=[("pb", 0)])
        S.add("dve", lambda e: e.tensor_copy(out=WsT, in_=pb[0][:, 0:512].rearrange("p (g t) -> p g t", g=4)),
              reads=[("pb", 0)], writes=["WsT"])
        for u in range(4):
            cast_dma(wout[:, :, u * 256:(u + 1) * 256],
                     w_out_d[L, :, u * 256:(u + 1) * 256].rearrange("(kc p) c -> p kc c", p=128), [("wout", u)], f"wo{u}")

        for p in range(NP):
            tok0 = p * PASS
            nblk_total = (p + 1) * 8
            S.add("dve", lambda e: e.memset(ssq, 0.0), writes=["ssq"])
            for tb in range(8):
                r0 = tok0 + tb * 128
                xi = xin[tb % 2]
                xb = xs[tb % 2]
                S.add("sp", lambda e, xi=xi, r0=r0, xsrc=xsrc: e.dma_start(out=xi, in_=xsrc[r0:r0 + 128, :]),
                      reads=[(srckey, p, tb), A1], writes=[("xin", tb % 2)], dma=f"xin{tb % 2}")
                S.add("act", lambda e, xi=xi, tb=tb: e.activation(out=junk, in_=xi, func=AF.Square,
                                                                 accum_out=ssq[:, tb:tb + 1]),
                      reads=[("xin", tb % 2), A1], writes=["junk", "ssq"])
                S.add("act", lambda e, tb=tb: e.activation(out=rstd[:, tb:tb + 1], in_=ssq[:, tb:tb + 1], func=AF.Ln,
                                                           bias=EPS, scale=1.0 / D), reads=["ssq"], writes=["rstd"])
                S.add("act", lambda e, tb=tb: e.activation(out=rstd[:, tb:tb + 1], in_=rstd[:, tb:tb + 1], func=AF.Exp,
                                                           scale=-0.5), reads=["rstd"], writes=["rstd"])
                S.add("dve", lambda e, xi=xi, xb=xb, tb=tb: e.tensor_scalar(out=xb, in0=xi, scalar1=rstd[:, tb:tb + 1],
                                                                          scalar2=None, op0=ALU.mult),
                      reads=[("xin", tb % 2), "rstd", A1], writes=[("xs", tb % 2)])
                for kc in range(8):
                    S.add("pe", lambda e, xb=xb, kc=kc, tb=tb: e.transpose(
                        pb[tb % 2][:, kc * 128:(kc + 1) * 128], xb[:, kc * 128:(kc + 1) * 128], identb),
                        reads=[("xs", tb % 2), "identb", A1], writes=[("pb", tb % 2)])
                S.add("dve", lambda e, tb=tb: e.tensor_tensor(
                    out=hT[:, :, tb * 128:(tb + 1) * 128], in0=pb[tb % 2].rearrange("p (k t) -> p k t", k=8),
                    in1=spT[:, SP_G:SP_G + 8].unsqueeze(2).broadcast_to([128, 8, 128]), op=ALU.mult),
                    reads=[("pb", tb % 2), "spT"], writes=["hT"])

            if STOP <= 1:
                continue
            def silu_from_psum(bk, dst, dstkey):
                S.add("act", lambda e: e.activation(out=et, in_=pf[bk], func=AF.Tanh, scale=0.5),
                      reads=[("pf", bk)], writes=["et"])
                S.add("dve", lambda e: e.scalar_tensor_tensor(out=dst, in0=et, scalar=1.0, in1=pf[bk],
                                                              op0=ALU.add, op1=ALU.mult),
                      reads=[("pf", bk), "et"], writes=[dstkey])

            if p == 0:
                S.add("dve", lambda e: e.memset(aT[:, :, 0:32], 0.0), writes=["aT"])
            else:
                S.add("dve", lambda e: e.tensor_copy(out=aT[:, :, 0:32], in_=aT[:, :, PASS:PASS + 32]),
                      reads=["aT"], writes=["aT"])
            for u in range(2):
                ws = load_w(w_in_l[L][:, C_AG + u * 256:C_AG + (u + 1) * 256], 8, 256)
                for ci in range(2):
                    for sl in range(2):
                        bk = gen.next()
                        proj_fm(ws, ci * 128, sl, bk)
                        S.add("act", lambda e, bk=bk, ci=ci, sl=sl: e.activation(
                            out=ew1[:, ci, sl * 512:(sl + 1) * 512], in_=pf[bk], func=AF.Tanh, scale=0.5),
                            reads=[("pf", bk)], writes=["ew1"])
                ws = load_w(w_in_l[L][:, C_AV + u * 256:C_AV + (u + 1) * 256], 8, 256)
                for ci in range(2):
                    cc = u * 2 + ci
                    for sl in range(2):
                        bk = gen.next()
                        proj_fm(ws, ci * 128, sl, bk)
                        S.add("dve", lambda e, bk=bk, cc=cc, ci=ci, sl=sl: e.scalar_tensor_tensor(
                            out=aT[:, cc, 32 + sl * 512:32 + (sl + 1) * 512], in0=ew1[:, ci, sl * 512:(sl + 1) * 512],
                            scalar=1.0, in1=pf[bk], op0=ALU.add, op1=ALU.mult),
                            reads=[("pf", bk), "ew1"], writes=["aT"])

            def conv_tap(sl, cc, k):
                off = 32 + sl * 512 - (30 - k)
                src = aT[:, cc, off:off + 512]
                wcol = cwh[:, cc * 31 + k:cc * 31 + k + 1]
                if k == 30:
                    S.add("dve", lambda e: e.tensor_scalar(
                        out=acc[:, cc, :], in0=src, scalar1=wcol, scalar2=spT[:, SP_CB + cc:SP_CB + cc + 1],
                        op0=ALU.mult, op1=ALU.add), reads=["aT", "spT", "cwh"], writes=[("acc", cc)])
                else:
                    S.add("dve", lambda e: e.scalar_tensor_tensor(
                        out=acc[:, cc, :], in0=src, scalar=wcol, in1=acc[:, cc, :], op0=ALU.mult, op1=ALU.add),
                        reads=["aT", "cwh", ("acc", cc)], writes=[("acc", cc)])

            def ln_slot(sl):
                b1, b2 = 4, 5
                for cc in range(4):
                    mm(pf[b1], onesf, acc[:, cc, :], cc == 0, cc == 3, ["onesf", ("acc", cc)], [("pf", b1)])
                for cc in range(4):
                    yq = cc % 2
                    S.add("act", lambda e, cc=cc, yq=yq: e.activation(out=ysq[yq], in_=acc[:, cc, :], func=AF.Square),
                          reads=[("acc", cc)], writes=[("ysq", yq)])
                    mm(pf[b2], onesf, ysq[yq], cc == 0, cc == 3, ["onesf", ("ysq", yq)], [("pf", b2)])
                S.add("dve", lambda e: e.tensor_scalar(out=mean, in0=pf[b1], scalar1=1.0 / 512, scalar2=None,
                                                       op0=ALU.mult), reads=[("pf", b1)], writes=["mean"])
                S.add("dve", lambda e: e.tensor_tensor(out=var, in0=mean, in1=mean, op=ALU.mult),
                      reads=["mean"], writes=["var"])
                S.add("dve", lambda e: e.scalar_tensor_tensor(out=var, in0=pf[b2], scalar=1.0 / 512, in1=var,
                                                              op0=ALU.mult, op1=ALU.subtract),
                      reads=[("pf", b2), "var"], writes=["var"])
                S.add("act", lambda e: e.activation(out=var, in_=var, func=AF.Ln, bias=EPS, scale=1.0),
                      reads=["var"], writes=["var"])
                S.add("act", lambda e: e.activation(out=rstdc, in_=var, func=AF.Exp, scale=-0.5),
                      reads=["var"], writes=["rstdc"])
                for cc in range(4):
                    S.add("dve", lambda e, cc=cc: e.tensor_tensor(out=zt, in0=acc[:, cc, :], in1=mean, op=ALU.subtract),
                          reads=[("acc", cc), "mean"], writes=["zt"])
                    S.add("dve", lambda e: e.tensor_tensor(out=zt, in0=zt, in1=rstdc, op=ALU.mult),
                          reads=["zt", "rstdc"], writes=["zt"])
                    S.add("dve", lambda e, cc=cc: e.tensor_scalar(
                        out=zt, in0=zt, scalar1=spT[:, SP_CNG + cc:SP_CNG + cc + 1],
                        scalar2=spT[:, SP_CNB + cc:SP_CNB + cc + 1], op0=ALU.mult, op1=ALU.add),
                        reads=["zt", "spT"], writes=["zt"])
                    S.add("act", lambda e: e.activation(out=et, in_=zt, func=AF.Tanh, scale=0.5),
                          reads=["zt"], writes=["et"])
                    S.add("dve", lambda e, cc=cc: e.scalar_tensor_tensor(
                        out=abr[:, cc, sl * 512:(sl + 1) * 512], in0=et, scalar=1.0, in1=zt, op0=ALU.add, op1=ALU.mult),
                        reads=["zt", "et"], writes=["abr"])

            convq = []
            for sl_ in range(2):
                for cpair in range(2):
                    for k in range(30, -1, -1):
                        for cc_ in (2 * cpair, 2 * cpair + 1):
                            convq.append((conv_tap, (sl_, cc_, k)))
                convq.append((ln_slot, (sl_,)))
            convq.reverse()

            def pull_conv(nitems):
                while nitems > 0 and convq:
                    f, a = convq.pop()
                    f(*a)
                    nitems -= 1

            S.add("dve", lambda e: e.memset(ssqv, 0.0), writes=["ssqv"])
            for which, c0 in ((("v", C_V), ("V", C_VV)) if SUB > 1 else (("v", C_V),)):
                for u in range(2):
                    ws = load_w(w_in_l[L][:, c0 + u * 256: c0 + (u + 1) * 256], 8, 256)
                    for tb in range(8):
                        bk = gen.next()
                        for kc in range(8):
                            mm(pf[bk][:, 0:256], hT[:, kc, tb * 128:(tb + 1) * 128], wbuf[ws][:, kc, 0:256],
                               kc == 0, kc == 7, [("w", ws), "hT"], [("pf", bk)])
                        if which == "v":
                            S.add("act", lambda e, bk=bk, tb=tb, u=u: e.activation(
                                out=lnt[:, 0:256], in_=pf[bk][:, 0:256], func=AF.Square, accum_out=ssqv[:, tb, u:u + 1]),
                                reads=[("pf", bk)], writes=["lnt", "ssqv"])
                            S.add("dve", lambda e, bk=bk, tb=tb, u=u: e.tensor_copy(
                                out=vn[:, tb, u * 256:(u + 1) * 256], in_=pf[bk][:, 0:256]),
                                reads=[("pf", bk)], writes=["vn"])
                        else:
                            S.add("act", lambda e, bk=bk, tb=tb, u=u: e.copy(
                                out=vstage[:, tb, u * 256:(u + 1) * 256], in_=pf[bk][:, 0:256]),
                                reads=[("pf", bk), A1], writes=["vstage"])
                        pull_conv(3)
            if SUB <= 1:
                continue
            for tb in range(8):
                r0 = tok0 + tb * 128
                S.add("sp", lambda e, tb=tb, r0=r0: e.dma_start(out=V_d[r0:r0 + 128, :], in_=vstage[:, tb, :]),
                      reads=["vstage", A1], writes=[("Vd", p, tb)], dma="vst")
            if SUB <= 2:
                continue
            S.add("dve", lambda e: e.tensor_tensor(out=ssqv1, in0=ssqv[:, :, 0], in1=ssqv[:, :, 1], op=ALU.add),
                  reads=["ssqv"], writes=["ssqv1"])
            S.add("act", lambda e: e.activation(out=rstdv, in_=ssqv1, func=AF.Ln, bias=EPS, scale=1.0 / 512),
                  reads=["ssqv1"], writes=["rstdv"])
            S.add("act", lambda e: e.activation(out=rstdv, in_=rstdv, func=AF.Exp, scale=-0.5),
                  reads=["rstdv"], writes=["rstdv"])
            for tb in range(8):
                S.add("dve", lambda e, tb=tb: e.scalar_tensor_tensor(
                    out=vn[:, tb, :], in0=vn[:, tb, :], scalar=rstdv[:, tb:tb + 1], in1=ggbc, op0=ALU.mult, op1=ALU.mult),
                    reads=["vn", "rstdv", "ggbc"], writes=["vn"])
            if SUB <= 3:
                continue
            ws = load_w(w_in_l[L][:, C_F:C_F + 8], 8, 8)
            bk = gen.next()
            for tb in range(8):
                for kc in range(8):
                    mm(pf[bk][:, tb * 8:(tb + 1) * 8], hT[:, kc, tb * 128:(tb + 1) * 128], wbuf[ws][:, kc, 0:8],
                       kc == 0, kc == 7, [("w", ws), "hT"], [("pf", bk)])
            S.add("dve", lambda e, bk=bk: e.tensor_tensor(
                out=fpb.rearrange("p (t h) -> p t h", t=8), in0=pf[bk][:, 0:64].rearrange("p (t h) -> p t h", t=8),
                in1=bfbc.unsqueeze(1).broadcast_to([128, 8, 8]), op=ALU.add),
                reads=[("pf", bk), "bfbc"], writes=["fpb"])
            S.add("act", lambda e: e.activation(out=fpb, in_=fpb, func=AF.Exp, scale=-1.0), reads=["fpb"], writes=["fpb"])
            if SUB <= 4:
                continue
            lfp = LF[:, p * 8:(p + 1) * 8, :].rearrange("p j h -> p (j h)")
            S.add("act", lambda e, lfp=lfp: e.activation(out=lfp, in_=fpb, func=AF.Ln, bias=1.0, scale=1.0),
                  reads=["fpb"], writes=["LF"])
            bk = gen.next()
            mm(pf[bk][:, 0:64], triuf, lfp, True, True, ["triuf", "LF"], [("pf", bk)])
            mm(pf[bk][:, 64:128], onesf, lfp, True, True, ["onesf", "LF"], [("pf", bk)])
            S.add("dve", lambda e, bk=bk, p=p: e.tensor_copy(
                out=TOTs[:, p * 8:(p + 1) * 8, :].rearrange("p j h -> p (j h)"), in_=pf[bk][:, 64:128]),
                reads=[("pf", bk)], writes=["TOTs"])
            for j in range(p * 8, (p + 1) * 8):
                if j == 0:
                    S.add("dve", lambda e: e.memset(EX[:, 0, :], 0.0), writes=["EX"])
                else:
                    S.add("dve", lambda e, j=j: e.tensor_tensor(out=EX[:, j, :], in0=EX[:, j - 1, :],
                                                               in1=TOTs[:, j - 1, :], op=ALU.add),
                          reads=["EX", "TOTs"], writes=["EX"])
            S.add("dve", lambda e, bk=bk, p=p: e.tensor_tensor(
                out=Pt[:, p * 8:(p + 1) * 8, :].rearrange("p j h -> p (j h)"), in0=pf[bk][:, 0:64],
                in1=EX[:, p * 8:(p + 1) * 8, :].rearrange("p j h -> p (j h)"), op=ALU.add),
                reads=[("pf", bk), "EX"], writes=["Pt"])

            if STOP <= 2:
                continue
            def qk_sq(bk, kq):
                S.add("act", lambda e: e.activation(out=ksq[kq], in_=pf[bk], func=AF.Square),
                      reads=[("pf", bk)], writes=[("ksq", kq)])

            def qk_fin(bk, kq, gcol, dst, dstkeys, extra_reads=(), sview=None, skey=None):
                if sview is None:
                    sb_ = gen.next()
                    sview, skey = pf[sb_], ("pf", sb_)
                mm(sview, blockones, ksq[kq], True, True, ["blockones", ("ksq", kq)], [skey])
                S.add("act", lambda e: e.activation(out=lnt, in_=sview, func=AF.Ln, bias=EPS, scale=1.0 / 64),
                      reads=[skey], writes=["lnt"])
                S.add("act", lambda e: e.activation(out=rstdq, in_=lnt, func=AF.Exp, scale=-0.5),
                      reads=["lnt"], writes=["rstdq"])
                S.add("dve", lambda e: e.scalar_tensor_tensor(out=dst, in0=pf[bk], scalar=gcol, in1=rstdq,
                                                              op0=ALU.mult, op1=ALU.mult),
                      reads=[("pf", bk), "rstdq", "spT", "qgs"] + list(extra_reads), writes=dstkeys)

            kr = Rot(range(2))
            kqr = Rot(range(2))
            pend = None

            def k_finish(bk, kq, ks, hc, sl):
                qk_fin(bk, kq, spT[:, SP_KG:SP_KG + 1], kst[ks], [("kst", ks)])
                c0 = tok0 + sl * 512
                S.add("sp", lambda e: e.dma_start(
                    out=KT_d[hc * 128:(hc + 1) * 128, c0:c0 + 512], in_=kst[ks]),
                    reads=[("kst", ks)], writes=[("KTd", hc, p, sl)], dma=f"kst{ks}")

            for u in range(2):
                ws = load_w(w_in_l[L][:, C_K + u * 256:C_K + (u + 1) * 256], 8, 256)
                for ci in range(2):
                    hc = u * 2 + ci
                    for sl in range(2):
                        bk = gen.next()
                        proj_fm(ws, ci * 128, sl, bk)
                        kq = kqr.next()
                        qk_sq(bk, kq)
                        if pend is not None:
                            k_finish(*pend)
                        pend = (bk, kq, kr.next(), hc, sl)
                        pull_conv(4)
            k_finish(*pend)

            S.barrier(A1)
            for sl in range(2):
                n = 4 * (2 * p + sl) + 4
                jref = p * 8 + 4 * sl + 2
                S.add("dve", lambda e, sl=sl, n=n, jref=jref: e.tensor_tensor(
                    out=biasT[:, sl, 0:n, :], in0=Pt[:, 0:n, :],
                    in1=EX[:, jref:jref + 1, :].broadcast_to([128, n, 8]), op=ALU.subtract),
                    reads=["Pt", "EX", A1], writes=["biasT"])
            for i in range(2):
                S.add("dve", lambda e, i=i: e.memset(Vb[i][:, :, 64:128], 1.0), reads=[A1], writes=[("V", i)])
            ptr = Rot(range(3))
            orot = Rot([2, 3])

            def attn_prep(hc):
                bi = hc % 2
                nk = nblk_total * 128
                S.add("sp", lambda e: e.dma_start(
                    out=KTb[bi][:, 0:nk], in_=KT_d[hc * 128:(hc + 1) * 128, 0:nk]),
                    reads=[("KTd", hc, pp, s_) for pp in range(p + 1) for s_ in range(2)] + [A1],
                    writes=[("KT", bi)], dma=f"kt{bi}")
                for e_ in range(2):
                    cs = hc * 128 + e_ * 64
                    for j0 in range(0, nblk_total, 8):
                        S.add("sp", lambda e, cs=cs, e_=e_, j0=j0: e.dma_start(
                            out=Vb[bi][:, j0:j0 + 8, e_ * 128:e_ * 128 + 64],
                            in_=V_d[j0 * 128:(j0 + 8) * 128, cs:cs + 64].rearrange("(j q) c -> q j c", q=128)),
                            reads=[("Vd", pp, t_) for pp in range(p + 1) for t_ in range(8)] + [A1],
                            writes=[("V", bi)], dma=f"v{bi}")
                ws = load_w(w_in_l[L][:, C_Q + hc * 128:C_Q + (hc + 1) * 128], 8, 128)
                for sl in range(2):
                    proj_fm(ws, 0, sl, 4 + sl)
                    qk_sq(4 + sl, sl)
                for sl in range(2):
                    qk_fin(4 + sl, sl, qgs[:, 0:1], qT[bi][:, sl * 512:(sl + 1) * 512], [("qT", bi)],
                           extra_reads=[A1], sview=pbf[sl], skey=("pb", sl))
                ws = load_w(w_in_l[L][:, C_CZ + hc * 128:C_CZ + (hc + 1) * 128], 8, 128)
                for sl in range(2):
                    bk = 4 + (sl % 2)
                    proj_fm(ws, 0, sl, bk)
                    S.add("act", lambda e, bk=bk: e.activation(out=lnt, in_=pf[bk], func=AF.Tanh, scale=0.5),
                          reads=[("pf", bk)], writes=["lnt"])
                    S.add("dve", lambda e, bk=bk, sl=sl: e.scalar_tensor_tensor(
                        out=scz[bi][:, sl * 512:(sl + 1) * 512], in0=lnt, scalar=1.0, in1=pf[bk],
                        op0=ALU.add, op1=ALU.mult),
                        reads=[("pf", bk), "lnt", A1], writes=[("scz", bi)])

            def attn_iter(hc, sl, e_):
                bi = hc % 2
                n = 4 * (2 * p + sl) + 4
                h = 2 * hc + e_
                rows = slice(64 * e_, 64 * e_ + 64)
                drows = slice(64 * (1 - e_), 64 * (1 - e_) + 64)
                ob = orot.next()
                qv = qT[bi][rows, sl * 512:(sl + 1) * 512]
                pts = {}

                def emit_s(j):
                    sbk = j % 2
                    diag = j >= n - 4
                    mm(pf[sbk], KTb[bi][rows, j * 128:(j + 1) * 128], qv, True, not diag,
                       [("KT", bi), ("qT", bi), A1], [("pf", sbk)])
                    if diag:
                        mm(pf[sbk], identb, maskU[:, j - (n - 4), :], False, True,
                           ["identb", "maskU"], [("pf", sbk)])
                    pt = ptr.next()
                    pts[j] = pt
                    S.add("act", lambda e: e.activation(
                        out=PT[pt], in_=pf[sbk], func=AF.Exp, bias=biasT[:, sl, j, h:h + 1], scale=1.0),
                        reads=[("pf", sbk), "biasT", A1], writes=[("PT", pt)])

                def emit_pv(j):
                    pt = pts[j]
                    mm(pf[ob], Vb[bi][:, j, e_ * 64:e_ * 64 + 128], PT[pt], j == 0, j == n - 1,
                       [("V", bi), ("PT", pt), A1], [("pf", ob)])

                emit_s(0)
                for j in range(n):
                    if j + 1 < n:
                        emit_s(j + 1)
                    emit_pv(j)
                tsl = slice(sl * 512, (sl + 1) * 512)
                S.add("dve", lambda e: e.reciprocal(out=rden[rows, :], in_=pf[ob][drows, :]),
                      reads=[("pf", ob), A1], writes=["rden"])
                S.add("dve", lambda e: e.tensor_tensor(
                    out=tmpO[rows, :], in0=pf[ob][rows, :], in1=scz[bi][rows, tsl], op=ALU.mult),
                    reads=[("pf", ob), ("scz", bi), A1], writes=["tmpO"])
                S.add("dve", lambda e: e.scalar_tensor_tensor(
                    out=cbr[rows, hc, tsl], in0=tmpO[rows, :], scalar=0.5, in1=rden[rows, :], op0=ALU.mult, op1=ALU.mult),
                    reads=["tmpO", "rden", A1], writes=["cbr"])

            attn_prep(0)
            for hc in range(4):
                if hc + 1 < 4:
                    attn_prep(hc + 1)
                for sl in range(2):
                    for e_ in range(2):
                        attn_iter(hc, sl, e_)
                        pull_conv(17)
            pull_conv(10 ** 6)
            S.barrier(A1)

            for u in range(2):
                ws = load_w(w_in_l[L][:, C_AZ + u * 256:C_AZ + (u + 1) * 256], 8, 256)
                for ci in range(2):
                    cc = u * 2 + ci
                    for sl in range(2):
                        bk = gen.next()
                        proj_fm(ws, ci * 128, sl, bk)
                        tsl = slice(sl * 512, (sl + 1) * 512)
                        silu_from_psum(bk, zt, "zt")
                        S.add("dve", lambda e, cc=cc, tsl=tsl: e.scalar_tensor_tensor(
                            out=abr[:, cc, tsl], in0=abr[:, cc, tsl], scalar=0.25, in1=zt, op0=ALU.mult, op1=ALU.mult),
                            reads=["abr", "zt"], writes=["abr"])

            for u in range(2):
                ws = load_w(w_in_l[L][:, C_BZ + u * 256:C_BZ + (u + 1) * 256], 8, 256)
                for ci in range(2):
                    for sl in range(2):
                        bk = gen.next()
                        proj_fm(ws, ci * 128, sl, bk)
                        silu_from_psum(bk, ew1[:, ci, sl * 512:(sl + 1) * 512], "ew1")
                ws = load_w(w_in_l[L][:, C_U + u * 256:C_U + (u + 1) * 256], 8, 256)
                for ci in range(2):
                    for sl in range(2):
                        bk = gen.next()
                        proj_fm(ws, ci * 128, sl, bk)
                        tsl = slice(sl * 512, (sl + 1) * 512)
                        S.add("dve", lambda e, bk=bk, ci=ci, tsl=tsl: e.tensor_tensor(
                            out=ew2[:, ci, tsl], in0=pf[bk], in1=ew1[:, ci, tsl], op=ALU.mult),
                            reads=[("pf", bk), "ew1"], writes=["ew2"])
                for ci in range(2):
                    g = u * 2 + ci
                    for sl in range(2):
                        bk = gen.next()
                        for t4 in range(4):
                            tb = sl * 4 + t4
                            mm(pf[bk][:, t4 * 128:(t4 + 1) * 128], vn[:, tb, g * 128:(g + 1) * 128], WsT[:, g, :],
                               True, False, ["vn", "WsT"], [("pf", bk)])
                            mm(pf[bk][:, t4 * 128:(t4 + 1) * 128], onesf[0:1, :], bsf[0:1, g, :],
                               False, True, ["onesf", "bsf"], [("pf", bk)])
                        tsl = slice(sl * 512, (sl + 1) * 512)
                        S.add("dve", lambda e, bk=bk, g=g, ci=ci, tsl=tsl: e.scalar_tensor_tensor(
                            out=bbr[:, g, tsl], in0=pf[bk], scalar=0.5, in1=ew2[:, ci, tsl], op0=ALU.mult, op1=ALU.mult),
                            reads=[("pf", bk), "ew2"], writes=["bbr"])

            brs = ((abr, "abr", w_a_d), (bbr, "bbr", w_b_d), (cbr, "cbr", w_c_d))
            for oc in range(8):
                for i, (br, brk, wd) in enumerate(brs):
                    gc = C_GATE + i * 1024 + oc * 128
                    wg = wrot.next()
                    S.add("pool", lambda e, wg=wg, gc=gc, L=L: e.dma_start(
                        out=wbuf[wg][:, 0:8, 0:128], in_=w_in_l[L][:, gc:gc + 128].rearrange("(kc p) c -> p kc c", p=128)),
                        writes=[("w", wg)], dma=f"w{wg}")
                    S.add("pool", lambda e, wg=wg, wd=wd, oc=oc, L=L: e.dma_start(
                        out=wbuf[wg][:, 0:4, 128:256],
                        in_=wd[L, :, oc * 128:(oc + 1) * 128].rearrange("(kc p) c -> p kc c", p=128)),
                        writes=[("w", wg)], dma=f"w{wg}")
                    for sl in range(2):
                        tsl = slice(sl * 512, (sl + 1) * 512)
                        bg = gen.next()
                        proj_fm(wg, 0, sl, bg)
                        by = gen.next()
                        proj_fm(wg, 128, sl, by, nkc=4, src=br, srckey=brk)
                        S.add("act", lambda e, bg=bg, i=i, oc=oc: e.activation(
                            out=gden[i], in_=pf[bg], func=AF.Tanh, bias=nbg[:, i * 8 + oc:i * 8 + oc + 1], scale=0.5),
                            reads=[("pf", bg), "nbg", A1], writes=[("gden", i)])
                        if i == 0:
                            S.add("dve", lambda e, by=by, i=i, sl=sl: e.scalar_tensor_tensor(
                                out=macc[sl], in0=gden[i], scalar=1.0, in1=pf[by], op0=ALU.add, op1=ALU.mult),
                                reads=[("pf", by), ("gden", i), A1], writes=[("macc", sl)])
                        else:
                            S.add("dve", lambda e, by=by, i=i: e.scalar_tensor_tensor(
                                out=mt, in0=gden[i], scalar=1.0, in1=pf[by], op0=ALU.add, op1=ALU.mult),
                                reads=[("pf", by), ("gden", i), A1], writes=["mt"])
                            if i == 1:
                                S.add("dve", lambda e, sl=sl: e.tensor_tensor(out=macc[sl], in0=macc[sl], in1=mt,
                                                                               op=ALU.add),
                                      reads=[("macc", sl), "mt", A1], writes=[("macc", sl)])
                            else:
                                S.add("dve", lambda e, oc=oc, tsl=tsl, sl=sl: e.tensor_tensor(
                                    out=mergedT[:, oc, tsl], in0=macc[sl], in1=mt, op=ALU.add),
                                    reads=[("macc", sl), "mt", A1], writes=["mergedT"])

            for tb in range(8):
                r0 = tok0 + tb * 128
                S.add("sp", lambda e, tb=tb, r0=r0, xsrc=xsrc: e.dma_start(out=xh[tb % 2], in_=xsrc[r0:r0 + 128, :]),
                      reads=[(srckey, p, tb), A1], writes=[("xh", tb % 2)], dma=f"xh{tb % 2}")
                for half in range(2):
                    bk = gen.next()
                    for kc in range(8):
                        mm(pf[bk], mergedT[:, kc, tb * 128:(tb + 1) * 128], wout[:, kc, half * 512:(half + 1) * 512],
                           kc == 0, kc == 7, ["mergedT", ("wout", 2 * half), ("wout", 2 * half + 1), A1], [("pf", bk)])
                    S.add("dve", lambda e, bk=bk, tb=tb, half=half: e.scalar_tensor_tensor(
                        out=xo[tb % 2][:, half * 512:(half + 1) * 512], in0=pf[bk], scalar=0.5,
                        in1=xh[tb % 2][:, half * 512:(half + 1) * 512], op0=ALU.mult, op1=ALU.add),
                        reads=[("pf", bk), ("xh", tb % 2), A1], writes=[("xo", tb % 2)])
                S.add("sp", lambda e, tb=tb, r0=r0, xdst=xdst: e.dma_start(out=xdst[r0:r0 + 128, :], in_=xo[tb % 2]),
                      reads=[("xo", tb % 2), A1], writes=[(dstkey, p, tb)], dma=f"xo{tb % 2}")
            S.barrier(A1)

    print("MK ops:", len(S.ops), flush=True)
    fin = S.add("sp", None)
    S.ops[fin].deps = S.ops[fin].deps if S.ops[fin].fn is not None else {}
    for op in S.ops:
        if op.fn is not None and op.dma is not None and op.dma.startswith("xo"):
            S._dep(S.ops[fin], op.idx)
    S.emit()
    return nc


def _consts():
    ident = np.eye(128, dtype=np.float32)
    triu = np.triu(np.ones((128, 128), np.float32))
    tril = np.tril(np.ones((128, 128), np.float32))
    k = np.arange(128)[:, None, None]
    m = np.arange(4)[None, :, None]
    q = np.arange(512)[None, None, :]
    maskU = np.where(128 * m + k > q, -30000.0, 0.0).astype(np.float32)
    return ident, triu, tril, maskU


def _pack_small(inp):
    L = 2
    sp = np.zeros((L, 128, NSP), np.float32)
    for l in range(L):
        sp[l, :, SP_G:SP_G + 8] = inp["norm_g"][l].reshape(8, 128).T
        sp[l, :, SP_BG:SP_BG + 24] = inp["b_gate"][l].reshape(24, 128).T
        cw = inp["conv_w"][l]
        sp[l, :, SP_CW:SP_CW + 124] = cw.reshape(31, 4, 128).transpose(2, 1, 0).reshape(128, 124)
        sp[l, :, SP_CB:SP_CB + 4] = inp["conv_b"][l].reshape(4, 128).T
        sp[l, :, SP_CNG:SP_CNG + 4] = inp["conv_norm_g"][l].reshape(4, 128).T
        sp[l, :, SP_CNB:SP_CNB + 4] = inp["conv_norm_b"][l].reshape(4, 128).T
        sp[l, :, SP_QG] = np.tile(inp["q_norm_g"][l], 2)
        sp[l, :, SP_KG] = np.tile(inp["k_norm_g"][l], 2)
    return sp


_NC_CACHE = {}


def kernel(**inputs):
    inp = {k: np.asarray(v, dtype=np.float32) for k, v in inputs.items()}
    NL = int(os.environ.get("MK_NL", "2"))
    NP = int(os.environ.get("MK_NP", "4"))
    key = (NL, NP)
    if key not in _NC_CACHE:
        _NC_CACHE[key] = build(NL, NP)
    nc = _NC_CACHE[key]
    ident, triu, tril, maskU = _consts()
    spT = _pack_small(inp)
    shared = {
        "w_in0": np.ascontiguousarray(inp["w_in"][0]), "w_in1": np.ascontiguousarray(inp["w_in"][1]), "w_a": inp["w_a"], "w_b": inp["w_b"], "w_c": inp["w_c"], "w_out": inp["w_out"],
        "w_s": inp["w_s"], "b_s": inp["b_s"], "gmlp_norm_g": inp["gmlp_norm_g"], "b_f": inp["b_f"],
        "spT": spT, "ident": ident, "triu": triu, "tril": tril, "maskU": maskU,
    }
    in_maps = []
    for c in range(8):
        m = dict(shared)
        m["x"] = np.ascontiguousarray(inp["x"][c // 2])
        in_maps.append(m)
    res = run_bass_kernel_spmd(nc, in_maps, core_ids=list(range(8)))
    out = np.empty((4, SEQ, D), np.float32)
    for b in range(4):
        out[b, :SEQ // 2] = res.results[2 * b]["out"][:SEQ // 2]
        out[b, SEQ // 2:] = res.results[2 * b + 1]["out"][SEQ // 2:]
    return out
```

```python
import os
import numpy as np
import concourse.bass as bass
import concourse.mybir as mybir
from concourse.bass_utils import run_bass_kernel_spmd

F32 = mybir.dt.float32
BF16 = mybir.dt.bfloat16
ALU = mybir.AluOpType
AF = mybir.ActivationFunctionType

D = 1024
SEQ = 4096
NIN = 8200
EPS = 1e-6
PASS = 1024
NSP = 170
C_GATE, C_AV, C_AG, C_AZ, C_U, C_V, C_BZ, C_Q, C_K, C_VV, C_CZ, C_F = (
    0, 3072, 3584, 4096, 4608, 5120, 5632, 6144, 6656, 7168, 7680, 8192)
SP_G, SP_BG, SP_CW, SP_CB, SP_CNG, SP_CNB, SP_QG, SP_KG = 0, 8, 32, 156, 160, 164, 168, 169

ENGS = ("pe", "act", "dve", "pool", "sp")


class Op:
    __slots__ = ("eng", "fn", "deps", "dma", "signal", "count", "idx", "grp")

    def __init__(self, eng, fn, dma):
        self.eng = eng
        self.fn = fn
        self.dma = dma
        self.deps = {}
        self.signal = False
        self.count = 0
        self.grp = ("d:" + dma) if dma is not None else ("e:" + eng)


class Sched:
    def __init__(self, nc):
        self.nc = nc
        self.ops = []
        self.last_w = {}
        self.readers = {}
        self.maxops = int(os.environ.get("MK_MAXOPS", "0")) or None

    def _dep(self, op, d):
        src = self.ops[d]
        if src.fn is None:
            for g, i in src.deps.items():
                if op.deps.get(g, -1) < i:
                    op.deps[g] = i
            return
        if op.deps.get(src.grp, -1) < d:
            op.deps[src.grp] = d

    def add(self, eng, fn, reads=(), writes=(), dma=None):
        if self.maxops is not None and len(self.ops) >= self.maxops:
            return len(self.ops) - 1
        op = Op(eng, fn, dma)
        op.idx = len(self.ops)
        self.ops.append(op)
        excl = [b for b in reads if isinstance(b, tuple) and b[0] in ("pf", "pb")]
        if excl:
            reads = [b for b in reads if b not in excl]
            writes = list(writes) + excl
        for b in reads:
            w = self.last_w.get(b)
            if w is not None:
                self._dep(op, w)
        for b in writes:
            w = self.last_w.get(b)
            if w is not None:
                self._dep(op, w)
            for r in self.readers.get(b, ()):
                self._dep(op, r)
        for b in reads:
            self.readers.setdefault(b, []).append(op.idx)
        for b in writes:
            self.last_w[b] = op.idx
            self.readers[b] = []
        return op.idx

    def barrier(self, key):
        return self.add("sp", None, writes=[key])

    def emit(self):
        nc = self.nc
        ops = self.ops
        for op in ops:
            if op.fn is None:
                continue
            for g, d in op.deps.items():
                src = ops[d]
                if src.dma is None and src.eng == "pe" and op.eng == "pe" and op.dma is None:
                    continue
                src.signal = True
        counters = {}
        for op in ops:
            if op.fn is None:
                continue
            if op.dma is not None:
                counters[op.grp] = counters.get(op.grp, 0) + 16
                op.count = counters[op.grp]
            elif op.signal:
                counters[op.grp] = counters.get(op.grp, 0) + 1
                op.count = counters[op.grp]
        sems = {k: nc.alloc_semaphore(name="s_" + k.replace(":", "_")) for k in counters}
        per_eng = {e: [] for e in ENGS}
        for op in ops:
            per_eng[op.eng].append(op)

        def run(eng_name, eng):
            waited = {}
            for op in per_eng[eng_name]:
                if op.fn is None:
                    if op.deps and op is ops[-1]:
                        pass
                    else:
                        continue
                for g, d in op.deps.items():
                    src = ops[d]
                    if src.dma is None and src.eng == "pe" and eng_name == "pe" and op.dma is None:
                        continue
                    if waited.get(g, 0) >= src.count:
                        continue
                    eng.wait_ge(sems[g], src.count)
                    waited[g] = src.count
                if op.fn is None:
                    continue
                ins = op.fn(eng)
                if op.dma is not None:
                    ins.then_inc(sems[op.grp], 16)
                elif op.signal:
                    ins.then_inc(sems[op.grp], 1)

        with nc.allow_low_precision(reason="bf16 activations feeding bf16 matmuls"), nc.Block() as block:
            @block.tensor
            def _(e):
                run("pe", e)

            @block.scalar
            def _(e):
                run("act", e)

            @block.vector
            def _(e):
                run("dve", e)

            @block.gpsimd
            def _(e):
                run("pool", e)

            @block.sync
            def _(e):
                run("sp", e)


class Rot:
    def __init__(self, items):
        self.items = list(items)
        self.i = 0

    def next(self):
        it = self.items[self.i % len(self.items)]
        self.i += 1
        return it


def build(NL=2, NP=4, dbg=False):
    nc = bass.Bass("TRN2", target_bir_lowering=False)
    STOP = int(os.environ.get("MK_STOP", "9"))
    SUB = int(os.environ.get("MK_SUB", "9"))

    def din(name, shape, dt=F32):
        return nc.dram_tensor(name, list(shape), dt, kind="ExternalInput").ap()

    x_d = din("x", [SEQ, D])
    w_in_l = [din("w_in0", [D, NIN]), din("w_in1", [D, NIN])]
    w_a_d = din("w_a", [2, 512, D])
    w_b_d = din("w_b", [2, 512, D])
    w_c_d = din("w_c", [2, 512, D])
    w_out_d = din("w_out", [2, D, D])
    w_s_d = din("w_s", [2, 4, 128, 128])
    b_s_d = din("b_s", [2, 4, 128])
    gg_d = din("gmlp_norm_g", [2, 512])
    bf_d = din("b_f", [2, 8])
    sp_d = din("spT", [2, 128, NSP])
    ident_d = din("ident", [128, 128])
    triu_d = din("triu", [128, 128])
    tril_d = din("tril", [128, 128])
    maskU_d = din("maskU", [128, 4, 512])
    out_d = nc.dram_tensor("out", [SEQ, D], F32, kind="ExternalOutput").ap()
    xmid_d = nc.dram_tensor("xmid", [SEQ, D], F32, kind="ExternalOutput").ap()
    KT_d = nc.dram_tensor("KTd", [512, SEQ], BF16, kind="ExternalOutput").ap()
    V_d = nc.dram_tensor("Vd", [SEQ, 512], BF16, kind="ExternalOutput").ap()

    def sb(name, shape, dt):
        return nc.alloc_sbuf_tensor(name, list(shape), dt).ap()

    identb = sb("identb", [128, 128], BF16)
    onesf = sb("onesf", [128, 128], F32)
    blockones = sb("blockones", [128, 128], BF16)
    triuf = sb("triuf", [128, 128], F32)
    trilf = sb("trilf", [128, 128], F32)
    maskU = sb("maskU_s", [128, 4, 512], BF16)
    spT = sb("spT_s", [128, NSP], F32)
    qgs = sb("qgs", [128, 1], F32)
    nbg = sb("nbg", [128, 24], F32)
    cwh = sb("cwh", [128, 124], F32)
    ggbc = sb("ggbc", [128, 512], F32)
    bfbc = sb("bfbc", [128, 8], F32)
    bsf = sb("bsf", [1, 4, 128], F32)
    wsf = sb("wsf", [128, 4, 128], F32)
    wsm = sb("wsm", [128, 4, 128], BF16)
    WsT = sb("WsT", [128, 4, 128], BF16)
    wout = sb("wout", [128, 8, 1024], BF16)
    LF = sb("LF", [128, 32, 8], F32)
    TOTs = sb("TOTs", [128, 32, 8], F32)
    EX = sb("EX", [128, 32, 8], F32)
    Pt = sb("Pt", [128, 32, 8], F32)
    fpb = sb("fpb", [128, 64], F32)
    hT = sb("hT", [128, 8, PASS], BF16)
    wbuf = [sb(f"wbuf{i}", [128, 8, 256], BF16) for i in range(4)]
    cbr = sb("cbr", [128, 4, PASS], BF16)
    abr = sb("abr", [128, 4, PASS], BF16)
    bbr = sb("bbr", [128, 4, PASS], BF16)
    vn = sb("vn", [128, 8, 512], BF16)
    ssqv = sb("ssqv", [128, 8, 2], F32)
    ssqv1 = sb("ssqv1", [128, 8], F32)
    rstdv = sb("rstdv", [128, 8], F32)
    ssq = sb("ssq", [128, 8], F32)
    rstd = sb("rstd", [128, 8], F32)
    ksq = [sb(f"ksq{i}", [128, 512], BF16) for i in range(2)]
    lnt = sb("lnt", [128, 512], F32)
    rstdq = sb("rstdq", [128, 512], F32)
    kst = [sb(f"kst{i}", [128, 512], BF16) for i in range(2)]
    aT = sb("aT", [128, 4, 32 + PASS], BF16)
    acc = sb("acc", [128, 4, 512], F32)
    ysq = [sb(f"ysq{i}", [128, 512], F32) for i in range(2)]
    ew1 = sb("ew1", [128, 2, PASS], BF16)
    ew2 = sb("ew2", [128, 2, PASS], BF16)
    et = sb("et", [128, 512], F32)
    mean = sb("mean", [128, 512], F32)
    var = sb("var", [128, 512], F32)
    rstdc = sb("rstdc", [128, 512], F32)
    zt = sb("zt", [128, 512], F32)
    AR = 60 * 1024 // 2
    arena = sb("arena", [128, AR], BF16)

    def carve(off, nbytes, dt, shape=None):
        v = arena[:, off // 2:(off + nbytes) // 2]
        if dt == F32:
            v = v.bitcast(F32)
        if shape is not None and len(shape) == 3:
            v = v.rearrange("p (a b) -> p a b", a=shape[1])
        return v

    K = 1024
    xin = [carve(0, 4 * K, F32), carve(4 * K, 4 * K, F32)]
    xs = [carve(8 * K, 2 * K, BF16), carve(10 * K, 2 * K, BF16)]
    junk = carve(12 * K, 2 * K, BF16)
    vstage = carve(14 * K, 8 * K, BF16, [128, 8, 512])
    KTb = [carve(0, 8 * K, BF16), carve(8 * K, 8 * K, BF16)]
    Vb = [carve(16 * K, 12 * K, BF16, [128, 32, 192]), carve(28 * K, 12 * K, BF16, [128, 32, 192])]
    PT = [carve(40 * K + i * K, K, BF16) for i in range(3)]
    biasT = carve(43 * K, 2 * K, F32).rearrange("p (s j h) -> p s j h", s=2, j=32)
    rden = carve(45 * K, 2 * K, F32)
    tmpO = carve(47 * K, 2 * K, F32)
    qT = [carve(49 * K, 2 * K, BF16), carve(51 * K, 2 * K, BF16)]
    scz = [carve(53 * K, 2 * K, BF16), carve(55 * K, 2 * K, BF16)]
    mergedT = carve(0, 16 * K, BF16, [128, 8, PASS])
    gden = [carve(16 * K + i * 2 * K, 2 * K, F32) for i in range(3)]
    macc = [carve(22 * K, 2 * K, F32), carve(24 * K, 2 * K, F32)]
    mt = carve(26 * K, 2 * K, F32)
    xo = [carve(28 * K, 4 * K, F32), carve(32 * K, 4 * K, F32)]
    xh = [carve(36 * K, 4 * K, F32), carve(40 * K, 4 * K, F32)]

    pf = [nc.alloc_psum_tensor(f"pf{i}", [128, 512], F32).ap() for i in range(6)]
    pb = [nc.alloc_psum_tensor(f"pb{i}", [128, 1024], BF16).ap() for i in range(2)]
    pbf = [t.bitcast(F32) for t in pb]

    S = Sched(nc)
    A1 = "ar1"
    wrot = Rot(range(4))
    crot = Rot(range(4))
    gen = Rot(range(6))

    def mm(out, lhsT, rhs, start, stop, reads, writes):
        S.add("pe", lambda e: e.matmul(out, lhsT=lhsT, rhs=rhs, start=start, stop=stop), reads=reads, writes=writes)

    def cast_dma(dst, src, wkeys, grp):
        S.add("pool", lambda e: e.dma_start(out=dst, in_=src), writes=wkeys, dma=grp)

    def load_w(src2d, nkc, ncols):
        s = wrot.next()
        dst = wbuf[s][:, 0:nkc, 0:ncols]
        src = src2d.rearrange("(kc p) c -> p kc c", p=128)
        S.add("pool", lambda e: e.dma_start(out=dst, in_=src), writes=[("w", s)], dma=f"w{s}")
        return s

    def proj_fm(ws, col0, sl, bank, nkc=8, src=None, srckey="hT"):
        src = hT if src is None else src
        for kc in range(nkc):
            mm(pf[bank][:, :], wbuf[ws][:, kc, col0:col0 + 128], src[:, kc, sl * 512:(sl + 1) * 512],
               kc == 0, kc == nkc - 1, [("w", ws), srckey], [("pf", bank)])

    cast_dma(identb, ident_d, ["identb"], "cdid")
    cast_dma(maskU, maskU_d, ["maskU"], "cdmu")
    S.add("sp", lambda e: e.dma_start(out=triuf, in_=triu_d), writes=["triuf"], dma="c2")
    S.add("sp", lambda e: e.dma_start(out=trilf, in_=tril_d), writes=["trilf"], dma="c3")
    S.add("dve", lambda e: e.memset(onesf, 1.0), writes=["onesf"])
    S.add("dve", lambda e: e.memset(blockones, 0.0), writes=["blockones"])
    S.add("dve", lambda e: e.memset(blockones[0:64, 0:64], 1.0), writes=["blockones"])
    S.add("dve", lambda e: e.memset(blockones[64:128, 64:128], 1.0), writes=["blockones"])

    for L in range(NL):
        xsrc = x_d if L == 0 else xmid_d
        xdst = out_d if L == NL - 1 else xmid_d
        srckey = "xin_d" if L == 0 else "xmid"
        dstkey = "out_d" if L == NL - 1 else "xmid"
        S.add("sp", lambda e, L=L: e.dma_start(out=spT, in_=sp_d[L]), writes=["spT"], dma="p0")
        S.add("sp", lambda e, L=L: e.dma_start(out=ggbc, in_=gg_d[L:L + 1, :].broadcast_to([128, 512])),
              writes=["ggbc"], dma="p1")
        S.add("sp", lambda e, L=L: e.dma_start(out=bfbc, in_=bf_d[L:L + 1, :].broadcast_to([128, 8])),
              writes=["bfbc"], dma="p2")
        S.add("sp", lambda e, L=L: e.dma_start(out=bsf, in_=b_s_d[L:L + 1, :, :]), writes=["bsf"], dma="p3")
        S.add("sp", lambda e, L=L: e.dma_start(out=wsf, in_=w_s_d[L].rearrange("g t s -> t g s")),
              writes=["wsf"], dma="p4")
        S.add("dve", lambda e: e.tensor_scalar(out=qgs, in0=spT[:, SP_QG:SP_QG + 1], scalar1=0.125, scalar2=None,
                                               op0=ALU.mult), reads=["spT"], writes=["qgs"])
        S.add("dve", lambda e: e.tensor_scalar(out=nbg, in0=spT[:, SP_BG:SP_BG + 24], scalar1=0.5, scalar2=None,
                                               op0=ALU.mult), reads=["spT"], writes=["nbg"])
        S.add("dve", lambda e: e.tensor_scalar(out=cwh, in0=spT[:, SP_CW:SP_CW + 124], scalar1=0.5, scalar2=None,
                                               op0=ALU.mult), reads=["spT"], writes=["cwh"])
        S.add("dve", lambda e: e.tensor_tensor(out=wsm, in0=wsf, in1=trilf.unsqueeze(1).broadcast_to([128, 4, 128]),
                                               op=ALU.mult), reads=["wsf", "trilf"], writes=["wsm"])
        for g in range(4):
            S.add("pe", lambda e, g=g: e.transpose(pb[0][:, g * 128:(g + 1) * 128], wsm[:, g, :], identb),
                  reads=["wsm", "identb"], writes=[("pb", 0)])
        S.add("dve", lambda e: e.tensor_copy(out=WsT, in_=pb[0][:, 0:512].rearrange("p (g t) -> p g t", g=4)),
              reads=[("pb", 0)], writes=["WsT"])
        for u in range(4):
            cast_dma(wout[:, :, u * 256:(u + 1) * 256],
                     w_out_d[L, :, u * 256:(u + 1) * 256].rearrange("(kc p) c -> p kc c", p=128), [("wout", u)], f"wo{u}")

        for p in range(NP):
            tok0 = p * PASS
            nblk_total = (p + 1) * 8
            S.add("dve", lambda e: e.memset(ssq, 0.0), writes=["ssq"])
            for tb in range(8):
                r0 = tok0 + tb * 128
                xi = xin[tb % 2]
                xb = xs[tb % 2]
                S.add("sp", lambda e, xi=xi, r0=r0, xsrc=xsrc: e.dma_start(out=xi, in_=xsrc[r0:r0 + 128, :]),
                      reads=[(srckey, p, tb), A1], writes=[("xin", tb % 2)], dma=f"xin{tb % 2}")
                S.add("act", lambda e, xi=xi, tb=tb: e.activation(out=junk, in_=xi, func=AF.Square,
                                                                 accum_out=ssq[:, tb:tb + 1]),
                      reads=[("xin", tb % 2), A1], writes=["junk", "ssq"])
                S.add("act", lambda e, tb=tb: e.activation(out=rstd[:, tb:tb + 1], in_=ssq[:, tb:tb + 1], func=AF.Ln,
                                                           bias=EPS, scale=1.0 / D), reads=["ssq"], writes=["rstd"])
                S.add("act", lambda e, tb=tb: e.activation(out=rstd[:, tb:tb + 1], in_=rstd[:, tb:tb + 1], func=AF.Exp,
                                                           scale=-0.5), reads=["rstd"], writes=["rstd"])
                S.add("dve", lambda e, xi=xi, xb=xb, tb=tb: e.tensor_scalar(out=xb, in0=xi, scalar1=rstd[:, tb:tb + 1],
                                                                          scalar2=None, op0=ALU.mult),
                      reads=[("xin", tb % 2), "rstd", A1], writes=[("xs", tb % 2)])
                for kc in range(8):
                    S.add("pe", lambda e, xb=xb, kc=kc, tb=tb: e.transpose(
                        pb[tb % 2][:, kc * 128:(kc + 1) * 128], xb[:, kc * 128:(kc + 1) * 128], identb),
                        reads=[("xs", tb % 2), "identb", A1], writes=[("pb", tb % 2)])
                S.add("dve", lambda e, tb=tb: e.tensor_tensor(
                    out=hT[:, :, tb * 128:(tb + 1) * 128], in0=pb[tb % 2].rearrange("p (k t) -> p k t", k=8),
                    in1=spT[:, SP_G:SP_G + 8].unsqueeze(2).broadcast_to([128, 8, 128]), op=ALU.mult),
                    reads=[("pb", tb % 2), "spT"], writes=["hT"])

            if STOP <= 1:
                continue
            S.add("dve", lambda e: e.memset(ssqv, 0.0), writes=["ssqv"])
            for which, c0 in ((("v", C_V), ("V", C_VV)) if SUB > 1 else (("v", C_V),)):
                for u in range(2):
                    ws = load_w(w_in_l[L][:, c0 + u * 256: c0 + (u + 1) * 256], 8, 256)
                    for tb in range(8):
                        bk = gen.next()
                        for kc in range(8):
                            mm(pf[bk][:, 0:256], hT[:, kc, tb * 128:(tb + 1) * 128], wbuf[ws][:, kc, 0:256],
                               kc == 0, kc == 7, [("w", ws), "hT"], [("pf", bk)])
                        if which == "v":
                            S.add("act", lambda e, bk=bk, tb=tb, u=u: e.activation(
                                out=lnt[:, 0:256], in_=pf[bk][:, 0:256], func=AF.Square, accum_out=ssqv[:, tb, u:u + 1]),
                                reads=[("pf", bk)], writes=["lnt", "ssqv"])
                            S.add("dve", lambda e, bk=bk, tb=tb, u=u: e.tensor_copy(
                                out=vn[:, tb, u * 256:(u + 1) * 256], in_=pf[bk][:, 0:256]),
                                reads=[("pf", bk)], writes=["vn"])
                        else:
                            S.add("act", lambda e, bk=bk, tb=tb, u=u: e.copy(
                                out=vstage[:, tb, u * 256:(u + 1) * 256], in_=pf[bk][:, 0:256]),
                                reads=[("pf", bk), A1], writes=["vstage"])
            if SUB <= 1:
                continue
            for tb in range(8):
                r0 = tok0 + tb * 128
                S.add("sp", lambda e, tb=tb, r0=r0: e.dma_start(out=V_d[r0:r0 + 128, :], in_=vstage[:, tb, :]),
                      reads=["vstage", A1], writes=[("Vd", p, tb)], dma="vst")
            if SUB <= 2:
                continue
            S.add("dve", lambda e: e.tensor_tensor(out=ssqv1, in0=ssqv[:, :, 0], in1=ssqv[:, :, 1], op=ALU.add),
                  reads=["ssqv"], writes=["ssqv1"])
            S.add("act", lambda e: e.activation(out=rstdv, in_=ssqv1, func=AF.Ln, bias=EPS, scale=1.0 / 512),
                  reads=["ssqv1"], writes=["rstdv"])
            S.add("act", lambda e: e.activation(out=rstdv, in_=rstdv, func=AF.Exp, scale=-0.5),
                  reads=["rstdv"], writes=["rstdv"])
            for tb in range(8):
                S.add("dve", lambda e, tb=tb: e.scalar_tensor_tensor(
                    out=vn[:, tb, :], in0=vn[:, tb, :], scalar=rstdv[:, tb:tb + 1], in1=ggbc, op0=ALU.mult, op1=ALU.mult),
                    reads=["vn", "rstdv", "ggbc"], writes=["vn"])
            if SUB <= 3:
                continue
            ws = load_w(w_in_l[L][:, C_F:C_F + 8], 8, 8)
            bk = gen.next()
            for tb in range(8):
                for kc in range(8):
                    mm(pf[bk][:, tb * 8:(tb + 1) * 8], hT[:, kc, tb * 128:(tb + 1) * 128], wbuf[ws][:, kc, 0:8],
                       kc == 0, kc == 7, [("w", ws), "hT"], [("pf", bk)])
            S.add("dve", lambda e, bk=bk: e.tensor_tensor(
                out=fpb.rearrange("p (t h) -> p t h", t=8), in0=pf[bk][:, 0:64].rearrange("p (t h) -> p t h", t=8),
                in1=bfbc.unsqueeze(1).broadcast_to([128, 8, 8]), op=ALU.add),
                reads=[("pf", bk), "bfbc"], writes=["fpb"])
            S.add("act", lambda e: e.activation(out=fpb, in_=fpb, func=AF.Exp, scale=-1.0), reads=["fpb"], writes=["fpb"])
            if SUB <= 4:
                continue
            lfp = LF[:, p * 8:(p + 1) * 8, :].rearrange("p j h -> p (j h)")
            S.add("act", lambda e, lfp=lfp: e.activation(out=lfp, in_=fpb, func=AF.Ln, bias=1.0, scale=1.0),
                  reads=["fpb"], writes=["LF"])
            bk = gen.next()
            mm(pf[bk][:, 0:64], triuf, lfp, True, True, ["triuf", "LF"], [("pf", bk)])
            mm(pf[bk][:, 64:128], onesf, lfp, True, True, ["onesf", "LF"], [("pf", bk)])
            S.add("dve", lambda e, bk=bk, p=p: e.tensor_copy(
                out=TOTs[:, p * 8:(p + 1) * 8, :].rearrange("p j h -> p (j h)"), in_=pf[bk][:, 64:128]),
                reads=[("pf", bk)], writes=["TOTs"])
            for j in range(p * 8, (p + 1) * 8):
                if j == 0:
                    S.add("dve", lambda e: e.memset(EX[:, 0, :], 0.0), writes=["EX"])
                else:
                    S.add("dve", lambda e, j=j: e.tensor_tensor(out=EX[:, j, :], in0=EX[:, j - 1, :],
                                                               in1=TOTs[:, j - 1, :], op=ALU.add),
                          reads=["EX", "TOTs"], writes=["EX"])
            S.add("dve", lambda e, bk=bk, p=p: e.tensor_tensor(
                out=Pt[:, p * 8:(p + 1) * 8, :].rearrange("p j h -> p (j h)"), in0=pf[bk][:, 0:64],
                in1=EX[:, p * 8:(p + 1) * 8, :].rearrange("p j h -> p (j h)"), op=ALU.add),
                reads=[("pf", bk), "EX"], writes=["Pt"])

            if STOP <= 2:
                continue
            def qk_sq(bk, kq):
                S.add("act", lambda e: e.activation(out=ksq[kq], in_=pf[bk], func=AF.Square),
                      reads=[("pf", bk)], writes=[("ksq", kq)])

            def qk_fin(bk, kq, gcol, dst, dstkeys, extra_reads=(), sview=None, skey=None):
                if sview is None:
                    sb_ = gen.next()
                    sview, skey = pf[sb_], ("pf", sb_)
                mm(sview, blockones, ksq[kq], True, True, ["blockones", ("ksq", kq)], [skey])
                S.add("act", lambda e: e.activation(out=lnt, in_=sview, func=AF.Ln, bias=EPS, scale=1.0 / 64),
                      reads=[skey], writes=["lnt"])
                S.add("act", lambda e: e.activation(out=rstdq, in_=lnt, func=AF.Exp, scale=-0.5),
                      reads=["lnt"], writes=["rstdq"])
                S.add("dve", lambda e: e.scalar_tensor_tensor(out=dst, in0=pf[bk], scalar=gcol, in1=rstdq,
                                                              op0=ALU.mult, op1=ALU.mult),
                      reads=[("pf", bk), "rstdq", "spT", "qgs"] + list(extra_reads), writes=dstkeys)

            kr = Rot(range(2))
            kqr = Rot(range(2))
            pend = None

            def k_finish(bk, kq, ks, hc, sl):
                qk_fin(bk, kq, spT[:, SP_KG:SP_KG + 1], kst[ks], [("kst", ks)])
                c0 = tok0 + sl * 512
                S.add("sp", lambda e: e.dma_start(
                    out=KT_d[hc * 128:(hc + 1) * 128, c0:c0 + 512], in_=kst[ks]),
                    reads=[("kst", ks)], writes=[("KTd", hc, p, sl)], dma=f"kst{ks}")

            for u in range(2):
                ws = load_w(w_in_l[L][:, C_K + u * 256:C_K + (u + 1) * 256], 8, 256)
                for ci in range(2):
                    hc = u * 2 + ci
                    for sl in range(2):
                        bk = gen.next()
                        proj_fm(ws, ci * 128, sl, bk)
                        kq = kqr.next()
                        qk_sq(bk, kq)
                        if pend is not None:
                            k_finish(*pend)
                        pend = (bk, kq, kr.next(), hc, sl)
            k_finish(*pend)

            def silu_from_psum(bk, dst, dstkey):
                S.add("act", lambda e: e.activation(out=et, in_=pf[bk], func=AF.Tanh, scale=0.5),
                      reads=[("pf", bk)], writes=["et"])
                S.add("dve", lambda e: e.scalar_tensor_tensor(out=dst, in0=et, scalar=1.0, in1=pf[bk],
                                                              op0=ALU.add, op1=ALU.mult),
                      reads=[("pf", bk), "et"], writes=[dstkey])

            if p == 0:
                S.add("dve", lambda e: e.memset(aT[:, :, 0:32], 0.0), writes=["aT"])
            else:
                S.add("dve", lambda e: e.tensor_copy(out=aT[:, :, 0:32], in_=aT[:, :, PASS:PASS + 32]),
                      reads=["aT"], writes=["aT"])
            for u in range(2):
                ws = load_w(w_in_l[L][:, C_AG + u * 256:C_AG + (u + 1) * 256], 8, 256)
                for ci in range(2):
                    for sl in range(2):
                        bk = gen.next()
                        proj_fm(ws, ci * 128, sl, bk)
                        S.add("act", lambda e, bk=bk, ci=ci, sl=sl: e.activation(
                            out=ew1[:, ci, sl * 512:(sl + 1) * 512], in_=pf[bk], func=AF.Tanh, scale=0.5),
                            reads=[("pf", bk)], writes=["ew1"])
                ws = load_w(w_in_l[L][:, C_AV + u * 256:C_AV + (u + 1) * 256], 8, 256)
                for ci in range(2):
                    cc = u * 2 + ci
                    for sl in range(2):
                        bk = gen.next()
                        proj_fm(ws, ci * 128, sl, bk)
                        S.add("dve", lambda e, bk=bk, cc=cc, ci=ci, sl=sl: e.scalar_tensor_tensor(
                            out=aT[:, cc, 32 + sl * 512:32 + (sl + 1) * 512], in0=ew1[:, ci, sl * 512:(sl + 1) * 512],
                            scalar=1.0, in1=pf[bk], op0=ALU.add, op1=ALU.mult),
                            reads=[("pf", bk), "ew1"], writes=["aT"])

            def conv_tap(sl, cc, k):
                off = 32 + sl * 512 - (30 - k)
                src = aT[:, cc, off:off + 512]
                wcol = cwh[:, cc * 31 + k:cc * 31 + k + 1]
                if k == 30:
                    S.add("dve", lambda e: e.tensor_scalar(
                        out=acc[:, cc, :], in0=src, scalar1=wcol, scalar2=spT[:, SP_CB + cc:SP_CB + cc + 1],
                        op0=ALU.mult, op1=ALU.add), reads=["aT", "spT", "cwh"], writes=[("acc", cc)])
                else:
                    S.add("dve", lambda e: e.scalar_tensor_tensor(
                        out=acc[:, cc, :], in0=src, scalar=wcol, in1=acc[:, cc, :], op0=ALU.mult, op1=ALU.add),
                        reads=["aT", "cwh", ("acc", cc)], writes=[("acc", cc)])

            def ln_slot(sl):
                b1, b2 = 4, 5
                for cc in range(4):
                    mm(pf[b1], onesf, acc[:, cc, :], cc == 0, cc == 3, ["onesf", ("acc", cc)], [("pf", b1)])
                for cc in range(4):
                    yq = cc % 2
                    S.add("act", lambda e, cc=cc, yq=yq: e.activation(out=ysq[yq], in_=acc[:, cc, :], func=AF.Square),
                          reads=[("acc", cc)], writes=[("ysq", yq)])
                    mm(pf[b2], onesf, ysq[yq], cc == 0, cc == 3, ["onesf", ("ysq", yq)], [("pf", b2)])
                S.add("dve", lambda e: e.tensor_scalar(out=mean, in0=pf[b1], scalar1=1.0 / 512, scalar2=None,
                                                       op0=ALU.mult), reads=[("pf", b1)], writes=["mean"])
                S.add("dve", lambda e: e.tensor_tensor(out=var, in0=mean, in1=mean, op=ALU.mult),
                      reads=["mean"], writes=["var"])
                S.add("dve", lambda e: e.scalar_tensor_tensor(out=var, in0=pf[b2], scalar=1.0 / 512, in1=var,
                                                              op0=ALU.mult, op1=ALU.subtract),
                      reads=[("pf", b2), "var"], writes=["var"])
                S.add("act", lambda e: e.activation(out=var, in_=var, func=AF.Ln, bias=EPS, scale=1.0),
                      reads=["var"], writes=["var"])
                S.add("act", lambda e: e.activation(out=rstdc, in_=var, func=AF.Exp, scale=-0.5),
                      reads=["var"], writes=["rstdc"])
                for cc in range(4):
                    S.add("dve", lambda e, cc=cc: e.tensor_tensor(out=zt, in0=acc[:, cc, :], in1=mean, op=ALU.subtract),
                          reads=[("acc", cc), "mean"], writes=["zt"])
                    S.add("dve", lambda e: e.tensor_tensor(out=zt, in0=zt, in1=rstdc, op=ALU.mult),
                          reads=["zt", "rstdc"], writes=["zt"])
                    S.add("dve", lambda e, cc=cc: e.tensor_scalar(
                        out=zt, in0=zt, scalar1=spT[:, SP_CNG + cc:SP_CNG + cc + 1],
                        scalar2=spT[:, SP_CNB + cc:SP_CNB + cc + 1], op0=ALU.mult, op1=ALU.add),
                        reads=["zt", "spT"], writes=["zt"])
                    S.add("act", lambda e: e.activation(out=et, in_=zt, func=AF.Tanh, scale=0.5),
                          reads=["zt"], writes=["et"])
                    S.add("dve", lambda e, cc=cc: e.scalar_tensor_tensor(
                        out=abr[:, cc, sl * 512:(sl + 1) * 512], in0=et, scalar=1.0, in1=zt, op0=ALU.add, op1=ALU.mult),
                        reads=["zt", "et"], writes=["abr"])

            convq = []
            for sl_ in range(2):
                for cpair in range(2):
                    for k in range(30, -1, -1):
                        for cc_ in (2 * cpair, 2 * cpair + 1):
                            convq.append((conv_tap, (sl_, cc_, k)))
                convq.append((ln_slot, (sl_,)))
            convq.reverse()

            def pull_conv(nitems):
                while nitems > 0 and convq:
                    f, a = convq.pop()
                    f(*a)
                    nitems -= 1

            S.barrier(A1)
            for sl in range(2):
                n = 4 * (2 * p + sl) + 4
                jref = p * 8 + 4 * sl + 2
                S.add("dve", lambda e, sl=sl, n=n, jref=jref: e.tensor_tensor(
                    out=biasT[:, sl, 0:n, :], in0=Pt[:, 0:n, :],
                    in1=EX[:, jref:jref + 1, :].broadcast_to([128, n, 8]), op=ALU.subtract),
                    reads=["Pt", "EX", A1], writes=["biasT"])
            for i in range(2):
                S.add("dve", lambda e, i=i: e.memset(Vb[i][:, :, 64:128], 1.0), reads=[A1], writes=[("V", i)])
            ptr = Rot(range(3))
            orot = Rot([2, 3])

            def attn_prep(hc):
                bi = hc % 2
                nk = nblk_total * 128
                S.add("sp", lambda e: e.dma_start(
                    out=KTb[bi][:, 0:nk], in_=KT_d[hc * 128:(hc + 1) * 128, 0:nk]),
                    reads=[("KTd", hc, pp, s_) for pp in range(p + 1) for s_ in range(2)] + [A1],
                    writes=[("KT", bi)], dma=f"kt{bi}")
                for e_ in range(2):
                    cs = hc * 128 + e_ * 64
                    for j0 in range(0, nblk_total, 8):
                        S.add("sp", lambda e, cs=cs, e_=e_, j0=j0: e.dma_start(
                            out=Vb[bi][:, j0:j0 + 8, e_ * 128:e_ * 128 + 64],
                            in_=V_d[j0 * 128:(j0 + 8) * 128, cs:cs + 64].rearrange("(j q) c -> q j c", q=128)),
                            reads=[("Vd", pp, t_) for pp in range(p + 1) for t_ in range(8)] + [A1],
                            writes=[("V", bi)], dma=f"v{bi}")
                ws = load_w(w_in_l[L][:, C_Q + hc * 128:C_Q + (hc + 1) * 128], 8, 128)
                for sl in range(2):
                    proj_fm(ws, 0, sl, 4 + sl)
                    qk_sq(4 + sl, sl)
                for sl in range(2):
                    qk_fin(4 + sl, sl, qgs[:, 0:1], qT[bi][:, sl * 512:(sl + 1) * 512], [("qT", bi)],
                           extra_reads=[A1], sview=pbf[sl], skey=("pb", sl))
                ws = load_w(w_in_l[L][:, C_CZ + hc * 128:C_CZ + (hc + 1) * 128], 8, 128)
                for sl in range(2):
                    bk = 4 + (sl % 2)
                    proj_fm(ws, 0, sl, bk)
                    S.add("act", lambda e, bk=bk: e.activation(out=lnt, in_=pf[bk], func=AF.Tanh, scale=0.5),
                          reads=[("pf", bk)], writes=["lnt"])
                    S.add("dve", lambda e, bk=bk, sl=sl: e.scalar_tensor_tensor(
                        out=scz[bi][:, sl * 512:(sl + 1) * 512], in0=lnt, scalar=1.0, in1=pf[bk],
                        op0=ALU.add, op1=ALU.mult),
                        reads=[("pf", bk), "lnt", A1], writes=[("scz", bi)])

            def attn_iter(hc, sl, e_):
                bi = hc % 2
                n = 4 * (2 * p + sl) + 4
                h = 2 * hc + e_
                rows = slice(64 * e_, 64 * e_ + 64)
                drows = slice(64 * (1 - e_), 64 * (1 - e_) + 64)
                ob = orot.next()
                qv = qT[bi][rows, sl * 512:(sl + 1) * 512]
                pts = {}

                def emit_s(j):
                    sbk = j % 2
                    diag = j >= n - 4
                    mm(pf[sbk], KTb[bi][rows, j * 128:(j + 1) * 128], qv, True, not diag,
                       [("KT", bi), ("qT", bi), A1], [("pf", sbk)])
                    if diag:
                        mm(pf[sbk], identb, maskU[:, j - (n - 4), :], False, True,
                           ["identb", "maskU"], [("pf", sbk)])
                    pt = ptr.next()
                    pts[j] = pt
                    S.add("act", lambda e: e.activation(
                        out=PT[pt], in_=pf[sbk], func=AF.Exp, bias=biasT[:, sl, j, h:h + 1], scale=1.0),
                        reads=[("pf", sbk), "biasT", A1], writes=[("PT", pt)])

                def emit_pv(j):
                    pt = pts[j]
                    mm(pf[ob], Vb[bi][:, j, e_ * 64:e_ * 64 + 128], PT[pt], j == 0, j == n - 1,
                       [("V", bi), ("PT", pt), A1], [("pf", ob)])

                emit_s(0)
                for j in range(n):
                    if j + 1 < n:
                        emit_s(j + 1)
                    emit_pv(j)
                tsl = slice(sl * 512, (sl + 1) * 512)
                S.add("dve", lambda e: e.reciprocal(out=rden[rows, :], in_=pf[ob][drows, :]),
                      reads=[("pf", ob), A1], writes=["rden"])
                S.add("dve", lambda e: e.tensor_tensor(
                    out=tmpO[rows, :], in0=pf[ob][rows, :], in1=scz[bi][rows, tsl], op=ALU.mult),
                    reads=[("pf", ob), ("scz", bi), A1], writes=["tmpO"])
                S.add("dve", lambda e: e.scalar_tensor_tensor(
                    out=cbr[rows, hc, tsl], in0=tmpO[rows, :], scalar=0.5, in1=rden[rows, :], op0=ALU.mult, op1=ALU.mult),
                    reads=["tmpO", "rden", A1], writes=["cbr"])

            attn_prep(0)
            for hc in range(4):
                if hc + 1 < 4:
                    attn_prep(hc + 1)
                for sl in range(2):
                    for e_ in range(2):
                        attn_iter(hc, sl, e_)
                        pull_conv(17)
            pull_conv(10 ** 6)
            S.barrier(A1)

            for u in range(2):
                ws = load_w(w_in_l[L][:, C_AZ + u * 256:C_AZ + (u + 1) * 256], 8, 256)
                for ci in range(2):
                    cc = u * 2 + ci
                    for sl in range(2):
                        bk = gen.next()
                        proj_fm(ws, ci * 128, sl, bk)
                        tsl = slice(sl * 512, (sl + 1) * 512)
                        silu_from_psum(bk, zt, "zt")
                        S.add("dve", lambda e, cc=cc, tsl=tsl: e.scalar_tensor_tensor(
                            out=abr[:, cc, tsl], in0=abr[:, cc, tsl], scalar=0.25, in1=zt, op0=ALU.mult, op1=ALU.mult),
                            reads=["abr", "zt"], writes=["abr"])

            for u in range(2):
                ws = load_w(w_in_l[L][:, C_BZ + u * 256:C_BZ + (u + 1) * 256], 8, 256)
                for ci in range(2):
                    for sl in range(2):
                        bk = gen.next()
                        proj_fm(ws, ci * 128, sl, bk)
                        silu_from_psum(bk, ew1[:, ci, sl * 512:(sl + 1) * 512], "ew1")
                ws = load_w(w_in_l[L][:, C_U + u * 256:C_U + (u + 1) * 256], 8, 256)
                for ci in range(2):
                    for sl in range(2):
                        bk = gen.next()
                        proj_fm(ws, ci * 128, sl, bk)
                        tsl = slice(sl * 512, (sl + 1) * 512)
                        S.add("dve", lambda e, bk=bk, ci=ci, tsl=tsl: e.tensor_tensor(
                            out=ew2[:, ci, tsl], in0=pf[bk], in1=ew1[:, ci, tsl], op=ALU.mult),
                            reads=[("pf", bk), "ew1"], writes=["ew2"])
                for ci in range(2):
                    g = u * 2 + ci
                    for sl in range(2):
                        bk = gen.next()
                        for t4 in range(4):
                            tb = sl * 4 + t4
                            mm(pf[bk][:, t4 * 128:(t4 + 1) * 128], vn[:, tb, g * 128:(g + 1) * 128], WsT[:, g, :],
                               True, False, ["vn", "WsT"], [("pf", bk)])
                            mm(pf[bk][:, t4 * 128:(t4 + 1) * 128], onesf[0:1, :], bsf[0:1, g, :],
                               False, True, ["onesf", "bsf"], [("pf", bk)])
                        tsl = slice(sl * 512, (sl + 1) * 512)
                        S.add("dve", lambda e, bk=bk, g=g, ci=ci, tsl=tsl: e.scalar_tensor_tensor(
                            out=bbr[:, g, tsl], in0=pf[bk], scalar=0.5, in1=ew2[:, ci, tsl], op0=ALU.mult, op1=ALU.mult),
                            reads=[("pf", bk), "ew2"], writes=["bbr"])

            brs = ((abr, "abr", w_a_d), (bbr, "bbr", w_b_d), (cbr, "cbr", w_c_d))
            for oc in range(8):
                for i, (br, brk, wd) in enumerate(brs):
                    gc = C_GATE + i * 1024 + oc * 128
                    wg = wrot.next()
                    S.add("pool", lambda e, wg=wg, gc=gc, L=L: e.dma_start(
                        out=wbuf[wg][:, 0:8, 0:128], in_=w_in_l[L][:, gc:gc + 128].rearrange("(kc p) c -> p kc c", p=128)),
                        writes=[("w", wg)], dma=f"w{wg}")
                    S.add("pool", lambda e, wg=wg, wd=wd, oc=oc, L=L: e.dma_start(
                        out=wbuf[wg][:, 0:4, 128:256],
                        in_=wd[L, :, oc * 128:(oc + 1) * 128].rearrange("(kc p) c -> p kc c", p=128)),
                        writes=[("w", wg)], dma=f"w{wg}")
                    for sl in range(2):
                        tsl = slice(sl * 512, (sl + 1) * 512)
                        bg = gen.next()
                        proj_fm(wg, 0, sl, bg)
                        by = gen.next()
                        proj_fm(wg, 128, sl, by, nkc=4, src=br, srckey=brk)
                        S.add("act", lambda e, bg=bg, i=i, oc=oc: e.activation(
                            out=gden[i], in_=pf[bg], func=AF.Tanh, bias=nbg[:, i * 8 + oc:i * 8 + oc + 1], scale=0.5),
                            reads=[("pf", bg), "nbg", A1], writes=[("gden", i)])
                        if i == 0:
                            S.add("dve", lambda e, by=by, i=i, sl=sl: e.scalar_tensor_tensor(
                                out=macc[sl], in0=gden[i], scalar=1.0, in1=pf[by], op0=ALU.add, op1=ALU.mult),
                                reads=[("pf", by), ("gden", i), A1], writes=[("macc", sl)])
                        else:
                            S.add("dve", lambda e, by=by, i=i: e.scalar_tensor_tensor(
                                out=mt, in0=gden[i], scalar=1.0, in1=pf[by], op0=ALU.add, op1=ALU.mult),
                                reads=[("pf", by), ("gden", i), A1], writes=["mt"])
                            if i == 1:
                                S.add("dve", lambda e, sl=sl: e.tensor_tensor(out=macc[sl], in0=macc[sl], in1=mt,
                                                                               op=ALU.add),
                                      reads=[("macc", sl), "mt", A1], writes=[("macc", sl)])
                            else:
                                S.add("dve", lambda e, oc=oc, tsl=tsl, sl=sl: e.tensor_tensor(
                                    out=mergedT[:, oc, tsl], in0=macc[sl], in1=mt, op=ALU.add),
                                    reads=[("macc", sl), "mt", A1], writes=["mergedT"])

            for tb in range(8):
                r0 = tok0 + tb * 128
                S.add("sp", lambda e, tb=tb, r0=r0, xsrc=xsrc: e.dma_start(out=xh[tb % 2], in_=xsrc[r0:r0 + 128, :]),
                      reads=[(srckey, p, tb), A1], writes=[("xh", tb % 2)], dma=f"xh{tb % 2}")
                for half in range(2):
                    bk = gen.next()
                    for kc in range(8):
                        mm(pf[bk], mergedT[:, kc, tb * 128:(tb + 1) * 128], wout[:, kc, half * 512:(half + 1) * 512],
                           kc == 0, kc == 7, ["mergedT", ("wout", 2 * half), ("wout", 2 * half + 1), A1], [("pf", bk)])
                    S.add("dve", lambda e, bk=bk, tb=tb, half=half: e.scalar_tensor_tensor(
                        out=xo[tb % 2][:, half * 512:(half + 1) * 512], in0=pf[bk], scalar=0.5,
                        in1=xh[tb % 2][:, half * 512:(half + 1) * 512], op0=ALU.mult, op1=ALU.add),
                        reads=[("pf", bk), ("xh", tb % 2), A1], writes=[("xo", tb % 2)])
                S.add("sp", lambda e, tb=tb, r0=r0, xdst=xdst: e.dma_start(out=xdst[r0:r0 + 128, :], in_=xo[tb % 2]),
                      reads=[("xo", tb % 2), A1], writes=[(dstkey, p, tb)], dma=f"xo{tb % 2}")
            S.barrier(A1)

    print("MK ops:", len(S.ops), flush=True)
    fin = S.add("sp", None)
    S.ops[fin].deps = S.ops[fin].deps if S.ops[fin].fn is not None else {}
    for op in S.ops:
        if op.fn is not None and op.dma is not None and op.dma.startswith("xo"):
            S._dep(S.ops[fin], op.idx)
    S.emit()
    return nc


def _consts():
    ident = np.eye(128, dtype=np.float32)
    triu = np.triu(np.ones((128, 128), np.float32))
    tril = np.tril(np.ones((128, 128), np.float32))
    k = np.arange(128)[:, None, None]
    m = np.arange(4)[None, :, None]
    q = np.arange(512)[None, None, :]
    maskU = np.where(128 * m + k > q, -30000.0, 0.0).astype(np.float32)
    return ident, triu, tril, maskU


def _pack_small(inp):
    L = 2
    sp = np.zeros((L, 128, NSP), np.float32)
    for l in range(L):
        sp[l, :, SP_G:SP_G + 8] = inp["norm_g"][l].reshape(8, 128).T
        sp[l, :, SP_BG:SP_BG + 24] = inp["b_gate"][l].reshape(24, 128).T
        cw = inp["conv_w"][l]
        sp[l, :, SP_CW:SP_CW + 124] = cw.reshape(31, 4, 128).transpose(2, 1, 0).reshape(128, 124)
        sp[l, :, SP_CB:SP_CB + 4] = inp["conv_b"][l].reshape(4, 128).T
        sp[l, :, SP_CNG:SP_CNG + 4] = inp["conv_norm_g"][l].reshape(4, 128).T
        sp[l, :, SP_CNB:SP_CNB + 4] = inp["conv_norm_b"][l].reshape(4, 128).T
        sp[l, :, SP_QG] = np.tile(inp["q_norm_g"][l], 2)
        sp[l, :, SP_KG] = np.tile(inp["k_norm_g"][l], 2)
    return sp


_NC_CACHE = {}


def kernel(**inputs):
    inp = {k: np.asarray(v, dtype=np.float32) for k, v in inputs.items()}
    NL = int(os.environ.get("MK_NL", "2"))
    NP = int(os.environ.get("MK_NP", "4"))
    key = (NL, NP)
    if key not in _NC_CACHE:
        _NC_CACHE[key] = build(NL, NP)
    nc = _NC_CACHE[key]
    ident, triu, tril, maskU = _consts()
    spT = _pack_small(inp)
    shared = {
        "w_in0": np.ascontiguousarray(inp["w_in"][0]), "w_in1": np.ascontiguousarray(inp["w_in"][1]), "w_a": inp["w_a"], "w_b": inp["w_b"], "w_c": inp["w_c"], "w_out": inp["w_out"],
        "w_s": inp["w_s"], "b_s": inp["b_s"], "gmlp_norm_g": inp["gmlp_norm_g"], "b_f": inp["b_f"],
        "spT": spT, "ident": ident, "triu": triu, "tril": tril, "maskU": maskU,
    }
    in_maps = []
    for c in range(8):
        m = dict(shared)
        m["x"] = np.ascontiguousarray(inp["x"][c // 2])
        in_maps.append(m)
    res = run_bass_kernel_spmd(nc, in_maps, core_ids=list(range(8)))
    out = np.empty((4, SEQ, D), np.float32)
    for b in range(4):
        out[b, :SEQ // 2] = res.results[2 * b]["out"][:SEQ // 2]
        out[b, SEQ // 2:] = res.results[2 * b + 1]["out"][SEQ // 2:]
    return out
```

```python
import os
import numpy as np
import concourse.bass as bass
import concourse.mybir as mybir
from concourse.bass_utils import run_bass_kernel_spmd

F32 = mybir.dt.float32
BF16 = mybir.dt.bfloat16
ALU = mybir.AluOpType
AF = mybir.ActivationFunctionType

D = 1024
SEQ = 4096
NIN = 8200
EPS = 1e-6
PASS = 1024
NSP = 170
C_GATE, C_AV, C_AG, C_AZ, C_U, C_V, C_BZ, C_Q, C_K, C_VV, C_CZ, C_F = (
    0, 3072, 3584, 4096, 4608, 5120, 5632, 6144, 6656, 7168, 7680, 8192)
SP_G, SP_BG, SP_CW, SP_CB, SP_CNG, SP_CNB, SP_QG, SP_KG = 0, 8, 32, 156, 160, 164, 168, 169

ENGS = ("pe", "act", "dve", "pool", "sp")


class Op:
    __slots__ = ("eng", "fn", "deps", "dma", "signal", "count", "idx", "grp")

    def __init__(self, eng, fn, dma):
        self.eng = eng
        self.fn = fn
        self.dma = dma
        self.deps = {}
        self.signal = False
        self.count = 0
        self.grp = ("d:" + dma) if dma is not None else ("e:" + eng)


class Sched:
    def __init__(self, nc):
        self.nc = nc
        self.ops = []
        self.last_w = {}
        self.readers = {}
        self.maxops = int(os.environ.get("MK_MAXOPS", "0")) or None

    def _dep(self, op, d):
        src = self.ops[d]
        if src.fn is None:
            for g, i in src.deps.items():
                if op.deps.get(g, -1) < i:
                    op.deps[g] = i
            return
        if op.deps.get(src.grp, -1) < d:
            op.deps[src.grp] = d

    def add(self, eng, fn, reads=(), writes=(), dma=None):
        if self.maxops is not None and len(self.ops) >= self.maxops:
            return len(self.ops) - 1
        op = Op(eng, fn, dma)
        op.idx = len(self.ops)
        self.ops.append(op)
        excl = [b for b in reads if isinstance(b, tuple) and b[0] in ("pf", "pb")]
        if excl:
            reads = [b for b in reads if b not in excl]
            writes = list(writes) + excl
        for b in reads:
            w = self.last_w.get(b)
            if w is not None:
                self._dep(op, w)
        for b in writes:
            w = self.last_w.get(b)
            if w is not None:
                self._dep(op, w)
            for r in self.readers.get(b, ()):
                self._dep(op, r)
        for b in reads:
            self.readers.setdefault(b, []).append(op.idx)
        for b in writes:
            self.last_w[b] = op.idx
            self.readers[b] = []
        return op.idx

    def barrier(self, key):
        return self.add("sp", None, writes=[key])

    def emit(self):
        nc = self.nc
        ops = self.ops
        for op in ops:
            if op.fn is None:
                continue
            for g, d in op.deps.items():
                src = ops[d]
                if src.dma is None and src.eng == "pe" and op.eng == "pe" and op.dma is None:
                    continue
                src.signal = True
        counters = {}
        for op in ops:
            if op.fn is None:
                continue
            if op.dma is not None:
                counters[op.grp] = counters.get(op.grp, 0) + 16
                op.count = counters[op.grp]
            elif op.signal:
                counters[op.grp] = counters.get(op.grp, 0) + 1
                op.count = counters[op.grp]
        sems = {k: nc.alloc_semaphore(name="s_" + k.replace(":", "_")) for k in counters}
        per_eng = {e: [] for e in ENGS}
        for op in ops:
            per_eng[op.eng].append(op)

        def run(eng_name, eng):
            waited = {}
            for op in per_eng[eng_name]:
                if op.fn is None:
                    if op.deps and op is ops[-1]:
                        pass
                    else:
                        continue
                for g, d in op.deps.items():
                    src = ops[d]
                    if src.dma is None and src.eng == "pe" and eng_name == "pe" and op.dma is None:
                        continue
                    if waited.get(g, 0) >= src.count:
                        continue
                    eng.wait_ge(sems[g], src.count)
                    waited[g] = src.count
                if op.fn is None:
                    continue
                ins = op.fn(eng)
                if op.dma is not None:
                    ins.then_inc(sems[op.grp], 16)
                elif op.signal:
                    ins.then_inc(sems[op.grp], 1)

        with nc.allow_low_precision(reason="bf16 activations feeding bf16 matmuls"), nc.Block() as block:
            @block.tensor
            def _(e):
                run("pe", e)

            @block.scalar
            def _(e):
                run("act", e)

            @block.vector
            def _(e):
                run("dve", e)

            @block.gpsimd
            def _(e):
                run("pool", e)

            @block.sync
            def _(e):
                run("sp", e)


class Rot:
    def __init__(self, items):
        self.items = list(items)
        self.i = 0

    def next(self):
        it = self.items[self.i % len(self.items)]
        self.i += 1
        return it


def build(NL=2, NP=4, dbg=False):
    nc = bass.Bass("TRN2", target_bir_lowering=False)
    STOP = int(os.environ.get("MK_STOP", "9"))
    SUB = int(os.environ.get("MK_SUB", "9"))

    def din(name, shape, dt=F32):
        return nc.dram_tensor(name, list(shape), dt, kind="ExternalInput").ap()

    x_d = din("x", [SEQ, D])
    w_in_l = [din("w_in0", [D, NIN]), din("w_in1", [D, NIN])]
    w_a_d = din("w_a", [2, 512, D])
    w_b_d = din("w_b", [2, 512, D])
    w_c_d = din("w_c", [2, 512, D])
    w_out_d = din("w_out", [2, D, D])
    w_s_d = din("w_s", [2, 4, 128, 128])
    b_s_d = din("b_s", [2, 4, 128])
    gg_d = din("gmlp_norm_g", [2, 512])
    bf_d = din("b_f", [2, 8])
    sp_d = din("spT", [2, 128, NSP])
    ident_d = din("ident", [128, 128])
    triu_d = din("triu", [128, 128])
    tril_d = din("tril", [128, 128])
    maskU_d = din("maskU", [128, 4, 512])
    out_d = nc.dram_tensor("out", [SEQ, D], F32, kind="ExternalOutput").ap()
    xmid_d = nc.dram_tensor("xmid", [SEQ, D], F32, kind="ExternalOutput").ap()
    KT_d = nc.dram_tensor("KTd", [512, SEQ], BF16, kind="ExternalOutput").ap()
    V_d = nc.dram_tensor("Vd", [SEQ, 512], BF16, kind="ExternalOutput").ap()

    def sb(name, shape, dt):
        return nc.alloc_sbuf_tensor(name, list(shape), dt).ap()

    identb = sb("identb", [128, 128], BF16)
    onesf = sb("onesf", [128, 128], F32)
    blockones = sb("blockones", [128, 128], BF16)
    triuf = sb("triuf", [128, 128], F32)
    trilf = sb("trilf", [128, 128], F32)
    maskU = sb("maskU_s", [128, 4, 512], BF16)
    spT = sb("spT_s", [128, NSP], F32)
    qgs = sb("qgs", [128, 1], F32)
    nbg = sb("nbg", [128, 24], F32)
    cwh = sb("cwh", [128, 124], F32)
    ggbc = sb("ggbc", [128, 512], F32)
    bfbc = sb("bfbc", [128, 8], F32)
    bsf = sb("bsf", [1, 4, 128], F32)
    wsf = sb("wsf", [128, 4, 128], F32)
    wsm = sb("wsm", [128, 4, 128], BF16)
    WsT = sb("WsT", [128, 4, 128], BF16)
    wout = sb("wout", [128, 8, 1024], BF16)
    LF = sb("LF", [128, 32, 8], F32)
    TOTs = sb("TOTs", [128, 32, 8], F32)
    EX = sb("EX", [128, 32, 8], F32)
    Pt = sb("Pt", [128, 32, 8], F32)
    fpb = sb("fpb", [128, 64], F32)
    hT = sb("hT", [128, 8, PASS], BF16)
    wbuf = [sb(f"wbuf{i}", [128, 8, 256], BF16) for i in range(4)]
    cbr = sb("cbr", [128, 4, PASS], BF16)
    abr = sb("abr", [128, 4, PASS], BF16)
    bbr = sb("bbr", [128, 4, PASS], BF16)
    vn = sb("vn", [128, 8, 512], BF16)
    ssqv = sb("ssqv", [128, 8, 2], F32)
    ssqv1 = sb("ssqv1", [128, 8], F32)
    rstdv = sb("rstdv", [128, 8], F32)
    ssq = sb("ssq", [128, 8], F32)
    rstd = sb("rstd", [128, 8], F32)
    ksq = [sb(f"ksq{i}", [128, 512], BF16) for i in range(2)]
    lnt = sb("lnt", [128, 512], F32)
    rstdq = sb("rstdq", [128, 512], F32)
    kst = [sb(f"kst{i}", [128, 512], BF16) for i in range(2)]
    aT = sb("aT", [128, 4, 32 + PASS], BF16)
    acc = sb("acc", [128, 4, 512], F32)
    ysq = [sb(f"ysq{i}", [128, 512], BF16) for i in range(2)]
    onesb = sb("onesb", [128, 128], BF16)
    ew1 = sb("ew1", [128, 2, PASS], BF16)
    ew2 = sb("ew2", [128, 2, PASS], BF16)
    et = sb("et", [128, 512], F32)
    mean = sb("mean", [128, 512], F32)
    var = sb("var", [128, 512], F32)
    rstdc = sb("rstdc", [128, 512], F32)
    zt = sb("zt", [128, 512], F32)
    AR = 60 * 1024 // 2
    arena = sb("arena", [128, AR], BF16)

    def carve(off, nbytes, dt, shape=None):
        v = arena[:, off // 2:(off + nbytes) // 2]
        if dt == F32:
            v = v.bitcast(F32)
        if shape is not None and len(shape) == 3:
            v = v.rearrange("p (a b) -> p a b", a=shape[1])
        return v

    K = 1024
    xin = [carve(44 * K, 4 * K, F32), carve(48 * K, 4 * K, F32)]
    xs = [carve(52 * K, 2 * K, BF16), carve(54 * K, 2 * K, BF16)]
    junk = carve(56 * K, 2 * K, BF16)
    vstage = carve(14 * K, 8 * K, BF16, [128, 8, 512])
    KTb = [carve(0, 8 * K, BF16), carve(8 * K, 8 * K, BF16)]
    Vb = [carve(16 * K, 12 * K, BF16, [128, 32, 192]), carve(28 * K, 12 * K, BF16, [128, 32, 192])]
    PT = [carve(40 * K + i * K, K, BF16) for i in range(3)]
    biasT = carve(43 * K, 2 * K, F32).rearrange("p (s j h) -> p s j h", s=2, j=32)
    rden = carve(45 * K, 2 * K, F32)
    tmpO = carve(47 * K, 2 * K, F32)
    qT = [carve(49 * K, 2 * K, BF16), carve(51 * K, 2 * K, BF16)]
    scz = [carve(53 * K, 2 * K, BF16), carve(55 * K, 2 * K, BF16)]
    mergedT = carve(0, 16 * K, BF16, [128, 8, PASS])
    gden = [carve(16 * K + i * 2 * K, 2 * K, F32) for i in range(3)]
    macc = [carve(22 * K, 2 * K, F32), carve(24 * K, 2 * K, F32)]
    mt = carve(26 * K, 2 * K, F32)
    xo = [carve(28 * K, 4 * K, F32), carve(32 * K, 4 * K, F32)]
    xh = [carve(36 * K, 4 * K, F32), carve(40 * K, 4 * K, F32)]

    pf = [nc.alloc_psum_tensor(f"pf{i}", [128, 512], F32).ap() for i in range(6)]
    pb = [nc.alloc_psum_tensor(f"pb{i}", [128, 1024], BF16).ap() for i in range(2)]
    pbf = [t.bitcast(F32) for t in pb]

    S = Sched(nc)
    A1 = "ar1"
    wrot = Rot(range(4))
    crot = Rot(range(4))
    gen = Rot(range(6))

    def mm(out, lhsT, rhs, start, stop, reads, writes):
        S.add("pe", lambda e: e.matmul(out, lhsT=lhsT, rhs=rhs, start=start, stop=stop), reads=reads, writes=writes)

    def cast_dma(dst, src, wkeys, grp):
        S.add("pool", lambda e: e.dma_start(out=dst, in_=src), writes=wkeys, dma=grp)

    def load_w(src2d, nkc, ncols):
        s = wrot.next()
        dst = wbuf[s][:, 0:nkc, 0:ncols]
        src = src2d.rearrange("(kc p) c -> p kc c", p=128)
        S.add("pool", lambda e: e.dma_start(out=dst, in_=src), writes=[("w", s)], dma=f"w{s}")
        return s

    def proj_fm(ws, col0, sl, bank, nkc=8, src=None, srckey="hT"):
        src = hT if src is None else src
        for kc in range(nkc):
            mm(pf[bank][:, :], wbuf[ws][:, kc, col0:col0 + 128], src[:, kc, sl * 512:(sl + 1) * 512],
               kc == 0, kc == nkc - 1, [("w", ws), srckey], [("pf", bank)])

    cast_dma(identb, ident_d, ["identb"], "cdid")
    cast_dma(maskU, maskU_d, ["maskU"], "cdmu")
    S.add("sp", lambda e: e.dma_start(out=triuf, in_=triu_d), writes=["triuf"], dma="c2")
    S.add("sp", lambda e: e.dma_start(out=trilf, in_=tril_d), writes=["trilf"], dma="c3")
    S.add("dve", lambda e: e.memset(onesf, 1.0), writes=["onesf"])
    S.add("dve", lambda e: e.memset(onesb, 1.0), writes=["onesb"])
    S.add("dve", lambda e: e.memset(blockones, 0.0), writes=["blockones"])
    S.add("dve", lambda e: e.memset(blockones[0:64, 0:64], 1.0), writes=["blockones"])
    S.add("dve", lambda e: e.memset(blockones[64:128, 64:128], 1.0), writes=["blockones"])

    for L in range(NL):
        xsrc = x_d if L == 0 else xmid_d
        xdst = out_d if L == NL - 1 else xmid_d
        srckey = "xin_d" if L == 0 else "xmid"
        dstkey = "out_d" if L == NL - 1 else "xmid"
        S.add("sp", lambda e, L=L: e.dma_start(out=spT, in_=sp_d[L]), writes=["spT"], dma="p0")
        S.add("sp", lambda e, L=L: e.dma_start(out=ggbc, in_=gg_d[L:L + 1, :].broadcast_to([128, 512])),
              writes=["ggbc"], dma="p1")
        S.add("sp", lambda e, L=L: e.dma_start(out=bfbc, in_=bf_d[L:L + 1, :].broadcast_to([128, 8])),
              writes=["bfbc"], dma="p2")
        S.add("sp", lambda e, L=L: e.dma_start(out=bsf, in_=b_s_d[L:L + 1, :, :]), writes=["bsf"], dma="p3")
        S.add("sp", lambda e, L=L: e.dma_start(out=wsf, in_=w_s_d[L].rearrange("g t s -> t g s")),
              writes=["wsf"], dma="p4")
        S.add("dve", lambda e: e.tensor_scalar(out=qgs, in0=spT[:, SP_QG:SP_QG + 1], scalar1=0.125, scalar2=None,
                                               op0=ALU.mult), reads=["spT"], writes=["qgs"])
        S.add("dve", lambda e: e.tensor_scalar(out=nbg, in0=spT[:, SP_BG:SP_BG + 24], scalar1=0.5, scalar2=None,
                                               op0=ALU.mult), reads=["spT"], writes=["nbg"])
        S.add("dve", lambda e: e.tensor_scalar(out=cwh, in0=spT[:, SP_CW:SP_CW + 124], scalar1=0.5, scalar2=None,
                                               op0=ALU.mult), reads=["spT"], writes=["cwh"])
        S.add("dve", lambda e: e.tensor_tensor(out=wsm, in0=wsf, in1=trilf.unsqueeze(1).broadcast_to([128, 4, 128]),
                                               op=ALU.mult), reads=["wsf", "trilf"], writes=["wsm"])
        for g in range(4):
            S.add("pe", lambda e, g=g: e.transpose(pb[0][:, g * 128:(g + 1) * 128], wsm[:, g, :], identb),
                  reads=["wsm", "identb"], writes=[("pb", 0)])
        S.add("dve", lambda e: e.tensor_copy(out=WsT, in_=pb[0][:, 0:512].rearrange("p (g t) -> p g t", g=4)),
              reads=[("pb", 0)], writes=["WsT"])
        for u in range(4):
            cast_dma(wout[:, :, u * 256:(u + 1) * 256],
                     w_out_d[L, :, u * 256:(u + 1) * 256].rearrange("(kc p) c -> p kc c", p=128), [("wout", u)], f"wo{u}")

        def stage_A(p):
            tok0 = p * PASS
            S.add("dve", lambda e: e.memset(ssq, 0.0), writes=["ssq"])
            for tb in range(8):
                r0 = tok0 + tb * 128
                xi = xin[tb % 2]
                xb = xs[tb % 2]
                S.add("sp", lambda e, xi=xi, r0=r0, xsrc=xsrc: e.dma_start(out=xi, in_=xsrc[r0:r0 + 128, :]),
                      reads=[(srckey, p, tb), A1], writes=[("xin", tb % 2)], dma=f"xin{tb % 2}")
                S.add("act", lambda e, xi=xi, tb=tb: e.activation(out=junk, in_=xi, func=AF.Square,
                                                                 accum_out=ssq[:, tb:tb + 1]),
                      reads=[("xin", tb % 2), A1], writes=["junk", "ssq"])
                S.add("act", lambda e, tb=tb: e.activation(out=rstd[:, tb:tb + 1], in_=ssq[:, tb:tb + 1], func=AF.Ln,
                                                           bias=EPS, scale=1.0 / D), reads=["ssq"], writes=["rstd"])
                S.add("act", lambda e, tb=tb: e.activation(out=rstd[:, tb:tb + 1], in_=rstd[:, tb:tb + 1], func=AF.Exp,
                                                           scale=-0.5), reads=["rstd"], writes=["rstd"])
                S.add("dve", lambda e, xi=xi, xb=xb, tb=tb: e.tensor_scalar(out=xb, in0=xi, scalar1=rstd[:, tb:tb + 1],
                                                                          scalar2=None, op0=ALU.mult),
                      reads=[("xin", tb % 2), "rstd", A1], writes=[("xs", tb % 2)])
                for kc in range(8):
                    S.add("pe", lambda e, xb=xb, kc=kc, tb=tb: e.transpose(
                        pb[tb % 2][:, kc * 128:(kc + 1) * 128], xb[:, kc * 128:(kc + 1) * 128], identb),
                        reads=[("xs", tb % 2), "identb", A1], writes=[("pb", tb % 2)])
                S.add("dve", lambda e, tb=tb: e.tensor_tensor(
                    out=hT[:, :, tb * 128:(tb + 1) * 128], in0=pb[tb % 2].rearrange("p (k t) -> p k t", k=8),
                    in1=spT[:, SP_G:SP_G + 8].unsqueeze(2).broadcast_to([128, 8, 128]), op=ALU.mult),
                    reads=[("pb", tb % 2), "spT"], writes=["hT"])


        for p in range(NP):
            tok0 = p * PASS
            nblk_total = (p + 1) * 8
            if p == 0:
                stage_A(0)
            if STOP <= 1:
                continue
            S.add("dve", lambda e: e.memset(ssqv, 0.0), writes=["ssqv"])
            for which, c0 in ((("v", C_V), ("V", C_VV)) if SUB > 1 else (("v", C_V),)):
                for u in range(2):
                    ws = load_w(w_in_l[L][:, c0 + u * 256: c0 + (u + 1) * 256], 8, 256)
                    for tb in range(8):
                        bk = gen.next()
                        for kc in range(8):
                            mm(pf[bk][:, 0:256], hT[:, kc, tb * 128:(tb + 1) * 128], wbuf[ws][:, kc, 0:256],
                               kc == 0, kc == 7, [("w", ws), "hT"], [("pf", bk)])
                        if which == "v":
                            S.add("act", lambda e, bk=bk, tb=tb, u=u: e.activation(
                                out=lnt[:, 0:256], in_=pf[bk][:, 0:256], func=AF.Square, accum_out=ssqv[:, tb, u:u + 1]),
                                reads=[("pf", bk)], writes=["lnt", "ssqv"])
                            S.add("dve", lambda e, bk=bk, tb=tb, u=u: e.tensor_copy(
                                out=vn[:, tb, u * 256:(u + 1) * 256], in_=pf[bk][:, 0:256]),
                                reads=[("pf", bk)], writes=["vn"])
                        else:
                            S.add("act", lambda e, bk=bk, tb=tb, u=u: e.copy(
                                out=vstage[:, tb, u * 256:(u + 1) * 256], in_=pf[bk][:, 0:256]),
                                reads=[("pf", bk), A1], writes=["vstage"])
            if SUB <= 1:
                continue
            for tb in range(8):
                r0 = tok0 + tb * 128
                S.add("sp", lambda e, tb=tb, r0=r0: e.dma_start(out=V_d[r0:r0 + 128, :], in_=vstage[:, tb, :]),
                      reads=["vstage", A1], writes=[("Vd", p, tb)], dma="vst")
            if SUB <= 2:
                continue
            S.add("dve", lambda e: e.tensor_tensor(out=ssqv1, in0=ssqv[:, :, 0], in1=ssqv[:, :, 1], op=ALU.add),
                  reads=["ssqv"], writes=["ssqv1"])
            S.add("act", lambda e: e.activation(out=rstdv, in_=ssqv1, func=AF.Ln, bias=EPS, scale=1.0 / 512),
                  reads=["ssqv1"], writes=["rstdv"])
            S.add("act", lambda e: e.activation(out=rstdv, in_=rstdv, func=AF.Exp, scale=-0.5),
                  reads=["rstdv"], writes=["rstdv"])
            for tb in range(8):
                S.add("dve", lambda e, tb=tb: e.scalar_tensor_tensor(
                    out=vn[:, tb, :], in0=vn[:, tb, :], scalar=rstdv[:, tb:tb + 1], in1=ggbc, op0=ALU.mult, op1=ALU.mult),
                    reads=["vn", "rstdv", "ggbc"], writes=["vn"])
            if SUB <= 3:
                continue
            ws = load_w(w_in_l[L][:, C_F:C_F + 8], 8, 8)
            bk = gen.next()
            for tb in range(8):
                for kc in range(8):
                    mm(pf[bk][:, tb * 8:(tb + 1) * 8], hT[:, kc, tb * 128:(tb + 1) * 128], wbuf[ws][:, kc, 0:8],
                       kc == 0, kc == 7, [("w", ws), "hT"], [("pf", bk)])
            S.add("dve", lambda e, bk=bk: e.tensor_tensor(
                out=fpb.rearrange("p (t h) -> p t h", t=8), in0=pf[bk][:, 0:64].rearrange("p (t h) -> p t h", t=8),
                in1=bfbc.unsqueeze(1).broadcast_to([128, 8, 8]), op=ALU.add),
                reads=[("pf", bk), "bfbc"], writes=["fpb"])
            S.add("act", lambda e: e.activation(out=fpb, in_=fpb, func=AF.Exp, scale=-1.0), reads=["fpb"], writes=["fpb"])
            if SUB <= 4:
                continue
            lfp = LF[:, p * 8:(p + 1) * 8, :].rearrange("p j h -> p (j h)")
            S.add("act", lambda e, lfp=lfp: e.activation(out=lfp, in_=fpb, func=AF.Ln, bias=1.0, scale=1.0),
                  reads=["fpb"], writes=["LF"])
            bk = gen.next()
            mm(pf[bk][:, 0:64], triuf, lfp, True, True, ["triuf", "LF"], [("pf", bk)])
            mm(pf[bk][:, 64:128], onesf, lfp, True, True, ["onesf", "LF"], [("pf", bk)])
            S.add("dve", lambda e, bk=bk, p=p: e.tensor_copy(
                out=TOTs[:, p * 8:(p + 1) * 8, :].rearrange("p j h -> p (j h)"), in_=pf[bk][:, 64:128]),
                reads=[("pf", bk)], writes=["TOTs"])
            for j in range(p * 8, (p + 1) * 8):
                if j == 0:
                    S.add("dve", lambda e: e.memset(EX[:, 0, :], 0.0), writes=["EX"])
                else:
                    S.add("dve", lambda e, j=j: e.tensor_tensor(out=EX[:, j, :], in0=EX[:, j - 1, :],
                                                               in1=TOTs[:, j - 1, :], op=ALU.add),
                          reads=["EX", "TOTs"], writes=["EX"])
            S.add("dve", lambda e, bk=bk, p=p: e.tensor_tensor(
                out=Pt[:, p * 8:(p + 1) * 8, :].rearrange("p j h -> p (j h)"), in0=pf[bk][:, 0:64],
                in1=EX[:, p * 8:(p + 1) * 8, :].rearrange("p j h -> p (j h)"), op=ALU.add),
                reads=[("pf", bk), "EX"], writes=["Pt"])

            if STOP <= 2:
                continue
            def qk_sq(bk, kq):
                S.add("act", lambda e: e.activation(out=ksq[kq], in_=pf[bk], func=AF.Square),
                      reads=[("pf", bk)], writes=[("ksq", kq)])

            def qk_fin(bk, kq, gcol, dst, dstkeys, extra_reads=(), sview=None, skey=None):
                if sview is None:
                    sb_ = gen.next()
                    sview, skey = pf[sb_], ("pf", sb_)
                mm(sview, blockones, ksq[kq], True, True, ["blockones", ("ksq", kq)], [skey])
                S.add("act", lambda e: e.activation(out=lnt, in_=sview, func=AF.Ln, bias=EPS, scale=1.0 / 64),
                      reads=[skey], writes=["lnt"])
                S.add("act", lambda e: e.activation(out=rstdq, in_=lnt, func=AF.Exp, scale=-0.5),
                      reads=["lnt"], writes=["rstdq"])
                S.add("dve", lambda e: e.scalar_tensor_tensor(out=dst, in0=pf[bk], scalar=gcol, in1=rstdq,
                                                              op0=ALU.mult, op1=ALU.mult),
                      reads=[("pf", bk), "rstdq", "spT", "qgs"] + list(extra_reads), writes=dstkeys)

            kr = Rot(range(2))
            kqr = Rot(range(2))
            pend = None

            def k_finish(bk, kq, ks, hc, sl):
                qk_fin(bk, kq, spT[:, SP_KG:SP_KG + 1], kst[ks], [("kst", ks)])
                c0 = tok0 + sl * 512
                S.add("sp", lambda e: e.dma_start(
                    out=KT_d[hc * 128:(hc + 1) * 128, c0:c0 + 512], in_=kst[ks]),
                    reads=[("kst", ks)], writes=[("KTd", hc, p, sl)], dma=f"kst{ks}")

            for u in range(2):
                ws = load_w(w_in_l[L][:, C_K + u * 256:C_K + (u + 1) * 256], 8, 256)
                for ci in range(2):
                    hc = u * 2 + ci
                    for sl in range(2):
                        bk = gen.next()
                        proj_fm(ws, ci * 128, sl, bk)
                        kq = kqr.next()
                        qk_sq(bk, kq)
                        if pend is not None:
                            k_finish(*pend)
                        pend = (bk, kq, kr.next(), hc, sl)
            k_finish(*pend)

            def silu_from_psum(bk, dst, dstkey):
                S.add("act", lambda e: e.activation(out=et, in_=pf[bk], func=AF.Tanh, scale=0.5),
                      reads=[("pf", bk)], writes=["et"])
                S.add("dve", lambda e: e.scalar_tensor_tensor(out=dst, in0=et, scalar=1.0, in1=pf[bk],
                                                              op0=ALU.add, op1=ALU.mult),
                      reads=[("pf", bk), "et"], writes=[dstkey])

            if p == 0:
                S.add("dve", lambda e: e.memset(aT[:, :, 0:32], 0.0), writes=["aT"])
            else:
                S.add("dve", lambda e: e.tensor_copy(out=aT[:, :, 0:32], in_=aT[:, :, PASS:PASS + 32]),
                      reads=["aT"], writes=["aT"])
            for u in range(2):
                ws = load_w(w_in_l[L][:, C_AG + u * 256:C_AG + (u + 1) * 256], 8, 256)
                for ci in range(2):
                    for sl in range(2):
                        bk = gen.next()
                        proj_fm(ws, ci * 128, sl, bk)
                        S.add("act", lambda e, bk=bk, ci=ci, sl=sl: e.activation(
                            out=ew1[:, ci, sl * 512:(sl + 1) * 512], in_=pf[bk], func=AF.Tanh, scale=0.5),
                            reads=[("pf", bk)], writes=["ew1"])
                ws = load_w(w_in_l[L][:, C_AV + u * 256:C_AV + (u + 1) * 256], 8, 256)
                for ci in range(2):
                    cc = u * 2 + ci
                    for sl in range(2):
                        bk = gen.next()
                        proj_fm(ws, ci * 128, sl, bk)
                        S.add("dve", lambda e, bk=bk, cc=cc, ci=ci, sl=sl: e.scalar_tensor_tensor(
                            out=aT[:, cc, 32 + sl * 512:32 + (sl + 1) * 512], in0=ew1[:, ci, sl * 512:(sl + 1) * 512],
                            scalar=1.0, in1=pf[bk], op0=ALU.add, op1=ALU.mult),
                            reads=[("pf", bk), "ew1"], writes=["aT"])

            def conv_tap(sl, cc, k):
                off = 32 + sl * 512 - (30 - k)
                src = aT[:, cc, off:off + 512]
                wcol = cwh[:, cc * 31 + k:cc * 31 + k + 1]
                if k == 30:
                    S.add("dve", lambda e: e.tensor_scalar(
                        out=acc[:, cc, :], in0=src, scalar1=wcol, scalar2=spT[:, SP_CB + cc:SP_CB + cc + 1],
                        op0=ALU.mult, op1=ALU.add), reads=["aT", "spT", "cwh"], writes=[("acc", cc)])
                elif k == 0:
                    S.add("dve", lambda e: e.scalar_tensor_tensor(
                        out=abr[:, cc, sl * 512:(sl + 1) * 512], in0=src, scalar=wcol, in1=acc[:, cc, :],
                        op0=ALU.mult, op1=ALU.add),
                        reads=["aT", "cwh", ("acc", cc)], writes=["abr"])
                else:
                    S.add("dve", lambda e: e.scalar_tensor_tensor(
                        out=acc[:, cc, :], in0=src, scalar=wcol, in1=acc[:, cc, :], op0=ALU.mult, op1=ALU.add),
                        reads=["aT", "cwh", ("acc", cc)], writes=[("acc", cc)])

            def ln_slot(sl):
                b1 = gen.next()
                b2 = gen.next()
                tsl_ = slice(sl * 512, (sl + 1) * 512)
                for cc in range(4):
                    mm(pf[b1], onesb, abr[:, cc, tsl_], cc == 0, cc == 3, ["onesb", "abr"], [("pf", b1)])
                for cc in range(4):
                    yq = cc % 2
                    S.add("act", lambda e, cc=cc, yq=yq: e.activation(out=ysq[yq], in_=abr[:, cc, tsl_], func=AF.Square),
                          reads=["abr"], writes=[("ysq", yq)])
                    mm(pf[b2], onesb, ysq[yq], cc == 0, cc == 3, ["onesb", ("ysq", yq)], [("pf", b2)])
                S.add("dve", lambda e: e.tensor_scalar(out=mean, in0=pf[b1], scalar1=1.0 / 512, scalar2=None,
                                                       op0=ALU.mult), reads=[("pf", b1)], writes=["mean"])
                S.add("dve", lambda e: e.tensor_tensor(out=var, in0=mean, in1=mean, op=ALU.mult),
                      reads=["mean"], writes=["var"])
                S.add("dve", lambda e: e.scalar_tensor_tensor(out=var, in0=pf[b2], scalar=1.0 / 512, in1=var,
                                                              op0=ALU.mult, op1=ALU.subtract),
                      reads=[("pf", b2), "var"], writes=["var"])
                S.add("act", lambda e: e.activation(out=var, in_=var, func=AF.Ln, bias=EPS, scale=1.0),
                      reads=["var"], writes=["var"])
                S.add("act", lambda e: e.activation(out=rstdc, in_=var, func=AF.Exp, scale=-0.5),
                      reads=["var"], writes=["rstdc"])
                for cc in range(4):
                    S.add("dve", lambda e, cc=cc: e.tensor_tensor(out=zt, in0=abr[:, cc, sl * 512:(sl + 1) * 512],
                                                                  in1=mean, op=ALU.subtract),
                          reads=["abr", "mean"], writes=["zt"])
                    S.add("dve", lambda e: e.tensor_tensor(out=zt, in0=zt, in1=rstdc, op=ALU.mult),
                          reads=["zt", "rstdc"], writes=["zt"])
                    S.add("dve", lambda e, cc=cc: e.tensor_scalar(
                        out=zt, in0=zt, scalar1=spT[:, SP_CNG + cc:SP_CNG + cc + 1],
                        scalar2=spT[:, SP_CNB + cc:SP_CNB + cc + 1], op0=ALU.mult, op1=ALU.add),
                        reads=["zt", "spT"], writes=["zt"])
                    S.add("act", lambda e: e.activation(out=et, in_=zt, func=AF.Tanh, scale=0.5),
                          reads=["zt"], writes=["et"])
                    S.add("dve", lambda e, cc=cc: e.scalar_tensor_tensor(
                        out=abr[:, cc, sl * 512:(sl + 1) * 512], in0=et, scalar=1.0, in1=zt, op0=ALU.add, op1=ALU.mult),
                        reads=["zt", "et"], writes=["abr"])

            convq = []
            for sl_ in range(2):
                for cpair in range(2):
                    for k in range(30, -1, -1):
                        for cc_ in (2 * cpair, 2 * cpair + 1):
                            convq.append((conv_tap, (sl_, cc_, k)))
            convq.reverse()

            def pull_conv(nitems):
                while nitems > 0 and convq:
                    f, a = convq.pop()
                    f(*a)
                    nitems -= 1

            S.barrier(A1)
            for sl in range(2):
                n = 4 * (2 * p + sl) + 4
                jref = p * 8 + 4 * sl + 2
                S.add("dve", lambda e, sl=sl, n=n, jref=jref: e.tensor_tensor(
                    out=biasT[:, sl, 0:n, :], in0=Pt[:, 0:n, :],
                    in1=EX[:, jref:jref + 1, :].broadcast_to([128, n, 8]), op=ALU.subtract),
                    reads=["Pt", "EX", A1], writes=["biasT"])
            for i in range(2):
                S.add("dve", lambda e, i=i: e.memset(Vb[i][:, :, 64:128], 1.0), reads=[A1], writes=[("V", i)])
            ptr = Rot(range(3))
            orot = Rot([2, 3])

            def attn_prep(hc):
                bi = hc % 2
                nk = nblk_total * 128
                S.add("sp", lambda e: e.dma_start(
                    out=KTb[bi][:, 0:nk], in_=KT_d[hc * 128:(hc + 1) * 128, 0:nk]),
                    reads=[("KTd", hc, pp, s_) for pp in range(p + 1) for s_ in range(2)] + [A1],
                    writes=[("KT", bi)], dma=f"kt{bi}")
                for e_ in range(2):
                    cs = hc * 128 + e_ * 64
                    for j0 in range(0, nblk_total, 8):
                        S.add("sp", lambda e, cs=cs, e_=e_, j0=j0: e.dma_start(
                            out=Vb[bi][:, j0:j0 + 8, e_ * 128:e_ * 128 + 64],
                            in_=V_d[j0 * 128:(j0 + 8) * 128, cs:cs + 64].rearrange("(j q) c -> q j c", q=128)),
                            reads=[("Vd", pp, t_) for pp in range(p + 1) for t_ in range(8)] + [A1],
                            writes=[("V", bi)], dma=f"v{bi}")
                ws = load_w(w_in_l[L][:, C_Q + hc * 128:C_Q + (hc + 1) * 128], 8, 128)
                for sl in range(2):
                    proj_fm(ws, 0, sl, 4 + sl)
                    qk_sq(4 + sl, sl)
                for sl in range(2):
                    qk_fin(4 + sl, sl, qgs[:, 0:1], qT[bi][:, sl * 512:(sl + 1) * 512], [("qT", bi)],
                           extra_reads=[A1], sview=pbf[sl], skey=("pb", sl))
                ws = load_w(w_in_l[L][:, C_CZ + hc * 128:C_CZ + (hc + 1) * 128], 8, 128)
                for sl in range(2):
                    bk = 4 + (sl % 2)
                    proj_fm(ws, 0, sl, bk)
                    S.add("act", lambda e, bk=bk: e.activation(out=lnt, in_=pf[bk], func=AF.Tanh, scale=0.5),
                          reads=[("pf", bk)], writes=["lnt"])
                    S.add("dve", lambda e, bk=bk, sl=sl: e.scalar_tensor_tensor(
                        out=scz[bi][:, sl * 512:(sl + 1) * 512], in0=lnt, scalar=1.0, in1=pf[bk],
                        op0=ALU.add, op1=ALU.mult),
                        reads=[("pf", bk), "lnt", A1], writes=[("scz", bi)])

            def attn_iter(hc, sl, e_):
                bi = hc % 2
                n = 4 * (2 * p + sl) + 4
                h = 2 * hc + e_
                rows = slice(64 * e_, 64 * e_ + 64)
                drows = slice(64 * (1 - e_), 64 * (1 - e_) + 64)
                ob = orot.next()
                qv = qT[bi][rows, sl * 512:(sl + 1) * 512]
                pts = {}

                def emit_s(j):
                    sbk = j % 2
                    diag = j >= n - 4
                    mm(pf[sbk], KTb[bi][rows, j * 128:(j + 1) * 128], qv, True, not diag,
                       [("KT", bi), ("qT", bi), A1], [("pf", sbk)])
                    if diag:
                        mm(pf[sbk], identb, maskU[:, j - (n - 4), :], False, True,
                           ["identb", "maskU"], [("pf", sbk)])
                    pt = ptr.next()
                    pts[j] = pt
                    S.add("act", lambda e: e.activation(
                        out=PT[pt], in_=pf[sbk], func=AF.Exp, bias=biasT[:, sl, j, h:h + 1], scale=1.0),
                        reads=[("pf", sbk), "biasT", A1], writes=[("PT", pt)])

                def emit_pv(j):
                    pt = pts[j]
                    mm(pf[ob], Vb[bi][:, j, e_ * 64:e_ * 64 + 128], PT[pt], j == 0, j == n - 1,
                       [("V", bi), ("PT", pt), A1], [("pf", ob)])

                emit_s(0)
                for j in range(n):
                    if j + 1 < n:
                        emit_s(j + 1)
                    emit_pv(j)
                tsl = slice(sl * 512, (sl + 1) * 512)
                S.add("dve", lambda e: e.reciprocal(out=rden[rows, :], in_=pf[ob][drows, :]),
                      reads=[("pf", ob), A1], writes=["rden"])
                S.add("dve", lambda e: e.tensor_tensor(
                    out=tmpO[rows, :], in0=pf[ob][rows, :], in1=scz[bi][rows, tsl], op=ALU.mult),
                    reads=[("pf", ob), ("scz", bi), A1], writes=["tmpO"])
                S.add("dve", lambda e: e.scalar_tensor_tensor(
                    out=cbr[rows, hc, tsl], in0=tmpO[rows, :], scalar=0.5, in1=rden[rows, :], op0=ALU.mult, op1=ALU.mult),
                    reads=["tmpO", "rden", A1], writes=["cbr"])

            attn_prep(0)
            for hc in range(4):
                if hc + 1 < 4:
                    attn_prep(hc + 1)
                for sl in range(2):
                    for e_ in range(2):
                        attn_iter(hc, sl, e_)
                        pull_conv(17)
            S.barrier(A1)

            for u in range(2):
                ws = load_w(w_in_l[L][:, C_BZ + u * 256:C_BZ + (u + 1) * 256], 8, 256)
                for ci in range(2):
                    for sl in range(2):
                        bk = gen.next()
                        proj_fm(ws, ci * 128, sl, bk)
                        silu_from_psum(bk, ew1[:, ci, sl * 512:(sl + 1) * 512], "ew1")
                ws = load_w(w_in_l[L][:, C_U + u * 256:C_U + (u + 1) * 256], 8, 256)
                for ci in range(2):
                    for sl in range(2):
                        bk = gen.next()
                        proj_fm(ws, ci * 128, sl, bk)
                        tsl = slice(sl * 512, (sl + 1) * 512)
                        S.add("dve", lambda e, bk=bk, ci=ci, tsl=tsl: e.tensor_tensor(
                            out=ew2[:, ci, tsl], in0=pf[bk], in1=ew1[:, ci, tsl], op=ALU.mult),
                            reads=[("pf", bk), "ew1"], writes=["ew2"])
                for ci in range(2):
                    g = u * 2 + ci
                    for sl in range(2):
                        bk = gen.next()
                        for t4 in range(4):
                            tb = sl * 4 + t4
                            mm(pf[bk][:, t4 * 128:(t4 + 1) * 128], vn[:, tb, g * 128:(g + 1) * 128], WsT[:, g, :],
                               True, False, ["vn", "WsT"], [("pf", bk)])
                            mm(pf[bk][:, t4 * 128:(t4 + 1) * 128], onesf[0:1, :], bsf[0:1, g, :],
                               False, True, ["onesf", "bsf"], [("pf", bk)])
                        tsl = slice(sl * 512, (sl + 1) * 512)
                        S.add("dve", lambda e, bk=bk, g=g, ci=ci, tsl=tsl: e.scalar_tensor_tensor(
                            out=bbr[:, g, tsl], in0=pf[bk], scalar=0.5, in1=ew2[:, ci, tsl], op0=ALU.mult, op1=ALU.mult),
                            reads=[("pf", bk), "ew2"], writes=["bbr"])

            pull_conv(10 ** 6)
            ln_slot(0)
            ln_slot(1)
            for u in range(2):
                ws = load_w(w_in_l[L][:, C_AZ + u * 256:C_AZ + (u + 1) * 256], 8, 256)
                for ci in range(2):
                    cc = u * 2 + ci
                    for sl in range(2):
                        bk = gen.next()
                        proj_fm(ws, ci * 128, sl, bk)
                        tsl = slice(sl * 512, (sl + 1) * 512)
                        silu_from_psum(bk, zt, "zt")
                        S.add("dve", lambda e, cc=cc, tsl=tsl: e.scalar_tensor_tensor(
                            out=abr[:, cc, tsl], in0=abr[:, cc, tsl], scalar=0.25, in1=zt, op0=ALU.mult, op1=ALU.mult),
                            reads=["abr", "zt"], writes=["abr"])

            brs = ((abr, "abr", w_a_d), (bbr, "bbr", w_b_d), (cbr, "cbr", w_c_d))
            for oc in range(8):
                for i, (br, brk, wd) in enumerate(brs):
                    gc = C_GATE + i * 1024 + oc * 128
                    wg = wrot.next()
                    S.add("pool", lambda e, wg=wg, gc=gc, L=L: e.dma_start(
                        out=wbuf[wg][:, 0:8, 0:128], in_=w_in_l[L][:, gc:gc + 128].rearrange("(kc p) c -> p kc c", p=128)),
                        writes=[("w", wg)], dma=f"w{wg}")
                    S.add("pool", lambda e, wg=wg, wd=wd, oc=oc, L=L: e.dma_start(
                        out=wbuf[wg][:, 0:4, 128:256],
                        in_=wd[L, :, oc * 128:(oc + 1) * 128].rearrange("(kc p) c -> p kc c", p=128)),
                        writes=[("w", wg)], dma=f"w{wg}")
                    for sl in range(2):
                        tsl = slice(sl * 512, (sl + 1) * 512)
                        bg = gen.next()
                        proj_fm(wg, 0, sl, bg)
                        by = gen.next()
                        proj_fm(wg, 128, sl, by, nkc=4, src=br, srckey=brk)
                        S.add("act", lambda e, bg=bg, i=i, oc=oc: e.activation(
                            out=gden[i], in_=pf[bg], func=AF.Tanh, bias=nbg[:, i * 8 + oc:i * 8 + oc + 1], scale=0.5),
                            reads=[("pf", bg), "nbg", A1], writes=[("gden", i)])
                        if i == 0:
                            S.add("dve", lambda e, by=by, i=i, sl=sl: e.scalar_tensor_tensor(
                                out=macc[sl], in0=gden[i], scalar=1.0, in1=pf[by], op0=ALU.add, op1=ALU.mult),
                                reads=[("pf", by), ("gden", i), A1], writes=[("macc", sl)])
                        else:
                            S.add("dve", lambda e, by=by, i=i: e.scalar_tensor_tensor(
                                out=mt, in0=gden[i], scalar=1.0, in1=pf[by], op0=ALU.add, op1=ALU.mult),
                                reads=[("pf", by), ("gden", i), A1], writes=["mt"])
                            if i == 1:
                                S.add("dve", lambda e, sl=sl: e.tensor_tensor(out=macc[sl], in0=macc[sl], in1=mt,
                                                                               op=ALU.add),
                                      reads=[("macc", sl), "mt", A1], writes=[("macc", sl)])
                            else:
                                S.add("dve", lambda e, oc=oc, tsl=tsl, sl=sl: e.tensor_tensor(
                                    out=mergedT[:, oc, tsl], in0=macc[sl], in1=mt, op=ALU.add),
                                    reads=[("macc", sl), "mt", A1], writes=["mergedT"])

            if p + 1 < NP:
                stage_A(p + 1)
            for tb in range(8):
                r0 = tok0 + tb * 128
                S.add("sp", lambda e, tb=tb, r0=r0, xsrc=xsrc: e.dma_start(out=xh[tb % 2], in_=xsrc[r0:r0 + 128, :]),
                      reads=[(srckey, p, tb), A1], writes=[("xh", tb % 2)], dma=f"xh{tb % 2}")
                for half in range(2):
                    bk = gen.next()
                    for kc in range(8):
                        mm(pf[bk], mergedT[:, kc, tb * 128:(tb + 1) * 128], wout[:, kc, half * 512:(half + 1) * 512],
                           kc == 0, kc == 7, ["mergedT", ("wout", 2 * half), ("wout", 2 * half + 1), A1], [("pf", bk)])
                    S.add("dve", lambda e, bk=bk, tb=tb, half=half: e.scalar_tensor_tensor(
                        out=xo[tb % 2][:, half * 512:(half + 1) * 512], in0=pf[bk], scalar=0.5,
                        in1=xh[tb % 2][:, half * 512:(half + 1) * 512], op0=ALU.mult, op1=ALU.add),
                        reads=[("pf", bk), ("xh", tb % 2), A1], writes=[("xo", tb % 2)])
                S.add("sp", lambda e, tb=tb, r0=r0, xdst=xdst: e.dma_start(out=xdst[r0:r0 + 128, :], in_=xo[tb % 2]),
                      reads=[("xo", tb % 2), A1], writes=[(dstkey, p, tb)], dma=f"xo{tb % 2}")
            S.barrier(A1)

    print("MK ops:", len(S.ops), flush=True)
    fin = S.add("sp", None)
    S.ops[fin].deps = S.ops[fin].deps if S.ops[fin].fn is not None else {}
    for op in S.ops:
        if op.fn is not None and op.dma is not None and op.dma.startswith("xo"):
            S._dep(S.ops[fin], op.idx)
    S.emit()
    return nc


def _consts():
    ident = np.eye(128, dtype=np.float32)
    triu = np.triu(np.ones((128, 128), np.float32))
    tril = np.tril(np.ones((128, 128), np.float32))
    k = np.arange(128)[:, None, None]
    m = np.arange(4)[None, :, None]
    q = np.arange(512)[None, None, :]
    maskU = np.where(128 * m + k > q, -30000.0, 0.0).astype(np.float32)
    return ident, triu, tril, maskU


def _pack_small(inp):
    L = 2
    sp = np.zeros((L, 128, NSP), np.float32)
    for l in range(L):
        sp[l, :, SP_G:SP_G + 8] = inp["norm_g"][l].reshape(8, 128).T
        sp[l, :, SP_BG:SP_BG + 24] = inp["b_gate"][l].reshape(24, 128).T
        cw = inp["conv_w"][l]
        sp[l, :, SP_CW:SP_CW + 124] = cw.reshape(31, 4, 128).transpose(2, 1, 0).reshape(128, 124)
        sp[l, :, SP_CB:SP_CB + 4] = inp["conv_b"][l].reshape(4, 128).T
        sp[l, :, SP_CNG:SP_CNG + 4] = inp["conv_norm_g"][l].reshape(4, 128).T
        sp[l, :, SP_CNB:SP_CNB + 4] = inp["conv_norm_b"][l].reshape(4, 128).T
        sp[l, :, SP_QG] = np.tile(inp["q_norm_g"][l], 2)
        sp[l, :, SP_KG] = np.tile(inp["k_norm_g"][l], 2)
    return sp


_NC_CACHE = {}


def kernel(**inputs):
    inp = {k: np.asarray(v, dtype=np.float32) for k, v in inputs.items()}
    NL = int(os.environ.get("MK_NL", "2"))
    NP = int(os.environ.get("MK_NP", "4"))
    key = (NL, NP)
    if key not in _NC_CACHE:
        _NC_CACHE[key] = build(NL, NP)
    nc = _NC_CACHE[key]
    ident, triu, tril, maskU = _consts()
    spT = _pack_small(inp)
    shared = {
        "w_in0": np.ascontiguousarray(inp["w_in"][0]), "w_in1": np.ascontiguousarray(inp["w_in"][1]), "w_a": inp["w_a"], "w_b": inp["w_b"], "w_c": inp["w_c"], "w_out": inp["w_out"],
        "w_s": inp["w_s"], "b_s": inp["b_s"], "gmlp_norm_g": inp["gmlp_norm_g"], "b_f": inp["b_f"],
        "spT": spT, "ident": ident, "triu": triu, "tril": tril, "maskU": maskU,
    }
    in_maps = []
    for c in range(8):
        m = dict(shared)
        m["x"] = np.ascontiguousarray(inp["x"][c // 2])
        in_maps.append(m)
    res = run_bass_kernel_spmd(nc, in_maps, core_ids=list(range(8)))
    out = np.empty((4, SEQ, D), np.float32)
    for b in range(4):
        out[b, :SEQ // 2] = res.results[2 * b]["out"][:SEQ // 2]
        out[b, SEQ // 2:] = res.results[2 * b + 1]["out"][SEQ // 2:]
    return out
```

```python
import os
import numpy as np
import concourse.bass as bass
import concourse.mybir as mybir
from concourse.bass_utils import run_bass_kernel_spmd

F32 = mybir.dt.float32
BF16 = mybir.dt.bfloat16
ALU = mybir.AluOpType
AF = mybir.ActivationFunctionType

D = 1024
SEQ = 4096
NIN = 8200
EPS = 1e-6
PASS = 1024
NSP = 170
C_GATE, C_AV, C_AG, C_AZ, C_U, C_V, C_BZ, C_Q, C_K, C_VV, C_CZ, C_F = (
    0, 3072, 3584, 4096, 4608, 5120, 5632, 6144, 6656, 7168, 7680, 8192)
SP_G, SP_BG, SP_CW, SP_CB, SP_CNG, SP_CNB, SP_QG, SP_KG = 0, 8, 32, 156, 160, 164, 168, 169

ENGS = ("pe", "act", "dve", "pool", "sp")


class Op:
    __slots__ = ("eng", "fn", "deps", "dma", "signal", "count", "idx", "grp")

    def __init__(self, eng, fn, dma):
        self.eng = eng
        self.fn = fn
        self.dma = dma
        self.deps = {}
        self.signal = False
        self.count = 0
        self.grp = ("d:" + dma) if dma is not None else ("e:" + eng)


class Sched:
    def __init__(self, nc):
        self.nc = nc
        self.ops = []
        self.last_w = {}
        self.readers = {}
        self.maxops = int(os.environ.get("MK_MAXOPS", "0")) or None

    def _dep(self, op, d):
        src = self.ops[d]
        if src.fn is None:
            for g, i in src.deps.items():
                if op.deps.get(g, -1) < i:
                    op.deps[g] = i
            return
        if op.deps.get(src.grp, -1) < d:
            op.deps[src.grp] = d

    def add(self, eng, fn, reads=(), writes=(), dma=None):
        if self.maxops is not None and len(self.ops) >= self.maxops:
            return len(self.ops) - 1
        op = Op(eng, fn, dma)
        op.idx = len(self.ops)
        self.ops.append(op)
        excl = [b for b in reads if isinstance(b, tuple) and b[0] in ("pf", "pb")]
        if excl:
            reads = [b for b in reads if b not in excl]
            writes = list(writes) + excl
        for b in reads:
            w = self.last_w.get(b)
            if w is not None:
                self._dep(op, w)
        for b in writes:
            w = self.last_w.get(b)
            if w is not None:
                self._dep(op, w)
            for r in self.readers.get(b, ()):
                self._dep(op, r)
        for b in reads:
            self.readers.setdefault(b, []).append(op.idx)
        for b in writes:
            self.last_w[b] = op.idx
            self.readers[b] = []
        return op.idx

    def barrier(self, key):
        return self.add("sp", None, writes=[key])

    def emit(self):
        nc = self.nc
        ops = self.ops
        for op in ops:
            if op.fn is None:
                continue
            for g, d in op.deps.items():
                src = ops[d]
                if src.dma is None and src.eng == "pe" and op.eng == "pe" and op.dma is None:
                    continue
                src.signal = True
        counters = {}
        for op in ops:
            if op.fn is None:
                continue
            if op.dma is not None:
                counters[op.grp] = counters.get(op.grp, 0) + 16
                op.count = counters[op.grp]
            elif op.signal:
                counters[op.grp] = counters.get(op.grp, 0) + 1
                op.count = counters[op.grp]
        sems = {k: nc.alloc_semaphore(name="s_" + k.replace(":", "_")) for k in counters}
        per_eng = {e: [] for e in ENGS}
        for op in ops:
            per_eng[op.eng].append(op)

        def run(eng_name, eng):
            waited = {}
            for op in per_eng[eng_name]:
                if op.fn is None:
                    if op.deps and op is ops[-1]:
                        pass
                    else:
                        continue
                for g, d in op.deps.items():
                    src = ops[d]
                    if src.dma is None and src.eng == "pe" and eng_name == "pe" and op.dma is None:
                        continue
                    if waited.get(g, 0) >= src.count:
                        continue
                    eng.wait_ge(sems[g], src.count)
                    waited[g] = src.count
                if op.fn is None:
                    continue
                ins = op.fn(eng)
                if op.dma is not None:
                    ins.then_inc(sems[op.grp], 16)
                elif op.signal:
                    ins.then_inc(sems[op.grp], 1)

        with nc.allow_low_precision(reason="bf16 activations feeding bf16 matmuls"), nc.Block() as block:
            @block.tensor
            def _(e):
                run("pe", e)

            @block.scalar
            def _(e):
                run("act", e)

            @block.vector
            def _(e):
                run("dve", e)

            @block.gpsimd
            def _(e):
                run("pool", e)

            @block.sync
            def _(e):
                run("sp", e)


class Rot:
    def __init__(self, items):
        self.items = list(items)
        self.i = 0

    def next(self):
        it = self.items[self.i % len(self.items)]
        self.i += 1
        return it


def build(NL=2, NP=4, dbg=False):
    nc = bass.Bass("TRN2", target_bir_lowering=False)
    STOP = int(os.environ.get("MK_STOP", "9"))
    SUB = int(os.environ.get("MK_SUB", "9"))

    def din(name, shape, dt=F32):
        return nc.dram_tensor(name, list(shape), dt, kind="ExternalInput").ap()

    x_d = din("x", [SEQ, D])
    w_in_l = [din("w_in0", [D, NIN]), din("w_in1", [D, NIN])]
    w_a_d = din("w_a", [2, 512, D])
    w_b_d = din("w_b", [2, 512, D])
    w_c_d = din("w_c", [2, 512, D])
    w_out_d = din("w_out", [2, D, D])
    w_s_d = din("w_s", [2, 4, 128, 128])
    b_s_d = din("b_s", [2, 4, 128])
    gg_d = din("gmlp_norm_g", [2, 512])
    bf_d = din("b_f", [2, 8])
    sp_d = din("spT", [2, 128, NSP])
    ident_d = din("ident", [128, 128])
    triu_d = din("triu", [128, 128])
    tril_d = din("tril", [128, 128])
    maskU_d = din("maskU", [128, 4, 512])
    out_d = nc.dram_tensor("out", [SEQ, D], F32, kind="ExternalOutput").ap()
    xmid_d = nc.dram_tensor("xmid", [SEQ, D], F32, kind="ExternalOutput").ap()
    KT_d = nc.dram_tensor("KTd", [512, SEQ], BF16, kind="ExternalOutput").ap()
    V_d = nc.dram_tensor("Vd", [SEQ, 512], BF16, kind="ExternalOutput").ap()

    def sb(name, shape, dt):
        return nc.alloc_sbuf_tensor(name, list(shape), dt).ap()

    identb = sb("identb", [128, 128], BF16)
    onesf = sb("onesf", [128, 128], F32)
    blockones = sb("blockones", [128, 128], BF16)
    triuf = sb("triuf", [128, 128], F32)
    trilf = sb("trilf", [128, 128], F32)
    maskU = sb("maskU_s", [128, 4, 512], BF16)
    spT = sb("spT_s", [128, NSP], F32)
    qgs = sb("qgs", [128, 1], F32)
    nbg = sb("nbg", [128, 24], F32)
    cwh = sb("cwh", [128, 124], F32)
    ggbc = sb("ggbc", [128, 512], F32)
    bfbc = sb("bfbc", [128, 8], F32)
    bsf = sb("bsf", [1, 4, 128], F32)
    wsf = sb("wsf", [128, 4, 128], F32)
    wsm = sb("wsm", [128, 4, 128], BF16)
    WsT = sb("WsT", [128, 4, 128], BF16)
    wout = sb("wout", [128, 8, 1024], BF16)
    LF = sb("LF", [128, 32, 8], F32)
    TOTs = sb("TOTs", [128, 32, 8], F32)
    EX = sb("EX", [128, 32, 8], F32)
    Pt = sb("Pt", [128, 32, 8], F32)
    fpb = sb("fpb", [128, 64], F32)
    hT = sb("hT", [128, 8, PASS], BF16)
    wbuf = [sb(f"wbuf{i}", [128, 8, 256], BF16) for i in range(4)]
    cbr = sb("cbr", [128, 4, PASS], BF16)
    abr = sb("abr", [128, 4, PASS], BF16)
    bbr = sb("bbr", [128, 4, PASS], BF16)
    vn = sb("vn", [128, 8, 512], BF16)
    ssqv = sb("ssqv", [128, 8, 2], F32)
    ssqv1 = sb("ssqv1", [128, 8], F32)
    rstdv = sb("rstdv", [128, 8], F32)
    ssq = sb("ssq", [128, 8], F32)
    rstd = sb("rstd", [128, 8], F32)
    ksq = [sb(f"ksq{i}", [128, 512], BF16) for i in range(2)]
    lnt = sb("lnt", [128, 512], F32)
    rstdq = sb("rstdq", [128, 512], F32)
    kst = [sb(f"kst{i}", [128, 512], BF16) for i in range(2)]
    aT = sb("aT", [128, 4, 32 + PASS], BF16)
    acc = sb("acc", [128, 4, 512], F32)
    ysq = [sb(f"ysq{i}", [128, 512], BF16) for i in range(2)]
    onesb = sb("onesb", [128, 128], BF16)
    ew1 = sb("ew1", [128, 2, PASS], BF16)
    ew2 = sb("ew2", [128, 2, PASS], BF16)
    et = sb("et", [128, 512], F32)
    mean = sb("mean", [128, 512], F32)
    var = sb("var", [128, 512], F32)
    rstdc = sb("rstdc", [128, 512], F32)
    zt = sb("zt", [128, 512], F32)
    AR = 60 * 1024 // 2
    arena = sb("arena", [128, AR], BF16)

    def carve(off, nbytes, dt, shape=None):
        v = arena[:, off // 2:(off + nbytes) // 2]
        if dt == F32:
            v = v.bitcast(F32)
        if shape is not None and len(shape) == 3:
            v = v.rearrange("p (a b) -> p a b", a=shape[1])
        return v

    K = 1024
    xin = [carve(44 * K, 4 * K, F32), carve(48 * K, 4 * K, F32)]
    xs = [carve(52 * K, 2 * K, BF16), carve(54 * K, 2 * K, BF16)]
    junk = carve(56 * K, 2 * K, BF16)
    vstage = carve(14 * K, 8 * K, BF16, [128, 8, 512])
    KTb = [carve(0, 8 * K, BF16), carve(8 * K, 8 * K, BF16)]
    Vb = [carve(16 * K, 12 * K, BF16, [128, 32, 192]), carve(28 * K, 12 * K, BF16, [128, 32, 192])]
    PT = [carve(40 * K + i * K, K, BF16) for i in range(3)]
    biasT = carve(43 * K, 2 * K, F32).rearrange("p (s j h) -> p s j h", s=2, j=32)
    rden = carve(45 * K, 2 * K, F32)
    tmpO = carve(47 * K, 2 * K, F32)
    qT = [carve(49 * K, 2 * K, BF16), carve(51 * K, 2 * K, BF16)]
    scz = [carve(53 * K, 2 * K, BF16), carve(55 * K, 2 * K, BF16)]
    mergedT = carve(0, 16 * K, BF16, [128, 8, PASS])
    gden = [carve(16 * K + i * 2 * K, 2 * K, F32) for i in range(3)]
    macc = [carve(22 * K, 2 * K, F32), carve(24 * K, 2 * K, F32)]
    mt = carve(26 * K, 2 * K, F32)
    xo = [carve(28 * K, 4 * K, F32), carve(32 * K, 4 * K, F32)]
    xh = [carve(36 * K, 4 * K, F32), carve(40 * K, 4 * K, F32)]

    pf = [nc.alloc_psum_tensor(f"pf{i}", [128, 512], F32).ap() for i in range(6)]
    pb = [nc.alloc_psum_tensor(f"pb{i}", [128, 1024], BF16).ap() for i in range(2)]
    pbf = [t.bitcast(F32) for t in pb]

    S = Sched(nc)
    A1 = "ar1"
    wrot = Rot(range(4))
    crot = Rot(range(4))
    gen = Rot(range(6))

    def mm(out, lhsT, rhs, start, stop, reads, writes):
        S.add("pe", lambda e: e.matmul(out, lhsT=lhsT, rhs=rhs, start=start, stop=stop), reads=reads, writes=writes)

    def cast_dma(dst, src, wkeys, grp):
        S.add("pool", lambda e: e.dma_start(out=dst, in_=src), writes=wkeys, dma=grp)

    def load_w(src2d, nkc, ncols):
        s = wrot.next()
        dst = wbuf[s][:, 0:nkc, 0:ncols]
        src = src2d.rearrange("(kc p) c -> p kc c", p=128)
        S.add("pool", lambda e: e.dma_start(out=dst, in_=src), writes=[("w", s)], dma=f"w{s}")
        return s

    def proj_fm(ws, col0, sl, bank, nkc=8, src=None, srckey="hT"):
        src = hT if src is None else src
        for kc in range(nkc):
            mm(pf[bank][:, :], wbuf[ws][:, kc, col0:col0 + 128], src[:, kc, sl * 512:(sl + 1) * 512],
               kc == 0, kc == nkc - 1, [("w", ws), srckey], [("pf", bank)])

    cast_dma(identb, ident_d, ["identb"], "cdid")
    cast_dma(maskU, maskU_d, ["maskU"], "cdmu")
    S.add("sp", lambda e: e.dma_start(out=triuf, in_=triu_d), writes=["triuf"], dma="c2")
    S.add("sp", lambda e: e.dma_start(out=trilf, in_=tril_d), writes=["trilf"], dma="c3")
    S.add("dve", lambda e: e.memset(onesf, 1.0), writes=["onesf"])
    S.add("dve", lambda e: e.memset(onesb, 1.0), writes=["onesb"])
    S.add("dve", lambda e: e.memset(blockones, 0.0), writes=["blockones"])
    S.add("dve", lambda e: e.memset(blockones[0:64, 0:64], 1.0), writes=["blockones"])
    S.add("dve", lambda e: e.memset(blockones[64:128, 64:128], 1.0), writes=["blockones"])

    for L in range(NL):
        xsrc = x_d if L == 0 else xmid_d
        xdst = out_d if L == NL - 1 else xmid_d
        srckey = "xin_d" if L == 0 else "xmid"
        dstkey = "out_d" if L == NL - 1 else "xmid"
        S.add("sp", lambda e, L=L: e.dma_start(out=spT, in_=sp_d[L]), writes=["spT"], dma="p0")
        S.add("sp", lambda e, L=L: e.dma_start(out=ggbc, in_=gg_d[L:L + 1, :].broadcast_to([128, 512])),
              writes=["ggbc"], dma="p1")
        S.add("sp", lambda e, L=L: e.dma_start(out=bfbc, in_=bf_d[L:L + 1, :].broadcast_to([128, 8])),
              writes=["bfbc"], dma="p2")
        S.add("sp", lambda e, L=L: e.dma_start(out=bsf, in_=b_s_d[L:L + 1, :, :]), writes=["bsf"], dma="p3")
        S.add("sp", lambda e, L=L: e.dma_start(out=wsf, in_=w_s_d[L].rearrange("g t s -> t g s")),
              writes=["wsf"], dma="p4")
        S.add("dve", lambda e: e.tensor_scalar(out=qgs, in0=spT[:, SP_QG:SP_QG + 1], scalar1=0.125, scalar2=None,
                                               op0=ALU.mult), reads=["spT"], writes=["qgs"])
        S.add("dve", lambda e: e.tensor_scalar(out=nbg, in0=spT[:, SP_BG:SP_BG + 24], scalar1=0.5, scalar2=None,
                                               op0=ALU.mult), reads=["spT"], writes=["nbg"])
        S.add("dve", lambda e: e.tensor_scalar(out=cwh, in0=spT[:, SP_CW:SP_CW + 124], scalar1=0.5, scalar2=None,
                                               op0=ALU.mult), reads=["spT"], writes=["cwh"])
        S.add("dve", lambda e: e.tensor_tensor(out=wsm, in0=wsf, in1=trilf.unsqueeze(1).broadcast_to([128, 4, 128]),
                                               op=ALU.mult), reads=["wsf", "trilf"], writes=["wsm"])
        for g in range(4):
            S.add("pe", lambda e, g=g: e.transpose(pb[0][:, g * 128:(g + 1) * 128], wsm[:, g, :], identb),
                  reads=["wsm", "identb"], writes=[("pb", 0)])
        S.add("dve", lambda e: e.tensor_copy(out=WsT, in_=pb[0][:, 0:512].rearrange("p (g t) -> p g t", g=4)),
              reads=[("pb", 0)], writes=["WsT"])
        for u in range(4):
            cast_dma(wout[:, :, u * 256:(u + 1) * 256],
                     w_out_d[L, :, u * 256:(u + 1) * 256].rearrange("(kc p) c -> p kc c", p=128), [("wout", u)], f"wo{u}")

        def stage_A(p):
            tok0 = p * PASS
            S.add("dve", lambda e: e.memset(ssq, 0.0), writes=["ssq"])
            for tb in range(8):
                r0 = tok0 + tb * 128
                xi = xin[tb % 2]
                xb = xs[tb % 2]
                S.add("sp", lambda e, xi=xi, r0=r0, xsrc=xsrc: e.dma_start(out=xi, in_=xsrc[r0:r0 + 128, :]),
                      reads=[(srckey, p, tb), A1], writes=[("xin", tb % 2)], dma=f"xin{tb % 2}")
                S.add("act", lambda e, xi=xi, tb=tb: e.activation(out=junk, in_=xi, func=AF.Square,
                                                                 accum_out=ssq[:, tb:tb + 1]),
                      reads=[("xin", tb % 2), A1], writes=["junk", "ssq"])
                S.add("act", lambda e, tb=tb: e.activation(out=rstd[:, tb:tb + 1], in_=ssq[:, tb:tb + 1], func=AF.Ln,
                                                           bias=EPS, scale=1.0 / D), reads=["ssq"], writes=["rstd"])
                S.add("act", lambda e, tb=tb: e.activation(out=rstd[:, tb:tb + 1], in_=rstd[:, tb:tb + 1], func=AF.Exp,
                                                           scale=-0.5), reads=["rstd"], writes=["rstd"])
                S.add("dve", lambda e, xi=xi, xb=xb, tb=tb: e.tensor_scalar(out=xb, in0=xi, scalar1=rstd[:, tb:tb + 1],
                                                                          scalar2=None, op0=ALU.mult),
                      reads=[("xin", tb % 2), "rstd", A1], writes=[("xs", tb % 2)])
                for kc in range(8):
                    S.add("pe", lambda e, xb=xb, kc=kc, tb=tb: e.transpose(
                        pb[tb % 2][:, kc * 128:(kc + 1) * 128], xb[:, kc * 128:(kc + 1) * 128], identb),
                        reads=[("xs", tb % 2), "identb", A1], writes=[("pb", tb % 2)])
                S.add("dve", lambda e, tb=tb: e.tensor_tensor(
                    out=hT[:, :, tb * 128:(tb + 1) * 128], in0=pb[tb % 2].rearrange("p (k t) -> p k t", k=8),
                    in1=spT[:, SP_G:SP_G + 8].unsqueeze(2).broadcast_to([128, 8, 128]), op=ALU.mult),
                    reads=[("pb", tb % 2), "spT"], writes=["hT"])


        for p in range(NP):
            tok0 = p * PASS
            nblk_total = (p + 1) * 8
            if p == 0:
                stage_A(0)
            if STOP <= 1:
                continue
            S.add("dve", lambda e: e.memset(ssqv, 0.0), writes=["ssqv"])
            for which, c0 in ((("v", C_V), ("V", C_VV)) if SUB > 1 else (("v", C_V),)):
                for u in range(2):
                    ws = load_w(w_in_l[L][:, c0 + u * 256: c0 + (u + 1) * 256], 8, 256)
                    for tb in range(8):
                        bk = gen.next()
                        for kc in range(8):
                            mm(pf[bk][:, 0:256], hT[:, kc, tb * 128:(tb + 1) * 128], wbuf[ws][:, kc, 0:256],
                               kc == 0, kc == 7, [("w", ws), "hT"], [("pf", bk)])
                        if which == "v":
                            S.add("act", lambda e, bk=bk, tb=tb, u=u: e.activation(
                                out=lnt[:, 0:256], in_=pf[bk][:, 0:256], func=AF.Square, accum_out=ssqv[:, tb, u:u + 1]),
                                reads=[("pf", bk)], writes=["lnt", "ssqv"])
                            S.add("dve", lambda e, bk=bk, tb=tb, u=u: e.tensor_copy(
                                out=vn[:, tb, u * 256:(u + 1) * 256], in_=pf[bk][:, 0:256]),
                                reads=[("pf", bk)], writes=["vn"])
                        else:
                            S.add("act", lambda e, bk=bk, tb=tb, u=u: e.copy(
                                out=vstage[:, tb, u * 256:(u + 1) * 256], in_=pf[bk][:, 0:256]),
                                reads=[("pf", bk), A1], writes=["vstage"])
            if SUB <= 1:
                continue
            for tb in range(8):
                r0 = tok0 + tb * 128
                S.add("sp", lambda e, tb=tb, r0=r0: e.dma_start(out=V_d[r0:r0 + 128, :], in_=vstage[:, tb, :]),
                      reads=["vstage", A1], writes=[("Vd", p, tb)], dma="vst")
            if SUB <= 2:
                continue
            S.add("dve", lambda e: e.tensor_tensor(out=ssqv1, in0=ssqv[:, :, 0], in1=ssqv[:, :, 1], op=ALU.add),
                  reads=["ssqv"], writes=["ssqv1"])
            S.add("act", lambda e: e.activation(out=rstdv, in_=ssqv1, func=AF.Ln, bias=EPS, scale=1.0 / 512),
                  reads=["ssqv1"], writes=["rstdv"])
            S.add("act", lambda e: e.activation(out=rstdv, in_=rstdv, func=AF.Exp, scale=-0.5),
                  reads=["rstdv"], writes=["rstdv"])
            for tb in range(8):
                S.add("dve", lambda e, tb=tb: e.scalar_tensor_tensor(
                    out=vn[:, tb, :], in0=vn[:, tb, :], scalar=rstdv[:, tb:tb + 1], in1=ggbc, op0=ALU.mult, op1=ALU.mult),
                    reads=["vn", "rstdv", "ggbc"], writes=["vn"])
            if SUB <= 3:
                continue
            ws = load_w(w_in_l[L][:, C_F:C_F + 8], 8, 8)
            bk = gen.next()
            for tb in range(8):
                for kc in range(8):
                    mm(pf[bk][:, tb * 8:(tb + 1) * 8], hT[:, kc, tb * 128:(tb + 1) * 128], wbuf[ws][:, kc, 0:8],
                       kc == 0, kc == 7, [("w", ws), "hT"], [("pf", bk)])
            S.add("dve", lambda e, bk=bk: e.tensor_tensor(
                out=fpb.rearrange("p (t h) -> p t h", t=8), in0=pf[bk][:, 0:64].rearrange("p (t h) -> p t h", t=8),
                in1=bfbc.unsqueeze(1).broadcast_to([128, 8, 8]), op=ALU.add),
                reads=[("pf", bk), "bfbc"], writes=["fpb"])
            S.add("act", lambda e: e.activation(out=fpb, in_=fpb, func=AF.Exp, scale=-1.0), reads=["fpb"], writes=["fpb"])
            if SUB <= 4:
                continue
            lfp = LF[:, p * 8:(p + 1) * 8, :].rearrange("p j h -> p (j h)")
            S.add("act", lambda e, lfp=lfp: e.activation(out=lfp, in_=fpb, func=AF.Ln, bias=1.0, scale=1.0),
                  reads=["fpb"], writes=["LF"])
            bk = gen.next()
            mm(pf[bk][:, 0:64], triuf, lfp, True, True, ["triuf", "LF"], [("pf", bk)])
            mm(pf[bk][:, 64:128], onesf, lfp, True, True, ["onesf", "LF"], [("pf", bk)])
            S.add("dve", lambda e, bk=bk, p=p: e.tensor_copy(
                out=TOTs[:, p * 8:(p + 1) * 8, :].rearrange("p j h -> p (j h)"), in_=pf[bk][:, 64:128]),
                reads=[("pf", bk)], writes=["TOTs"])
            for j in range(p * 8, (p + 1) * 8):
                if j == 0:
                    S.add("dve", lambda e: e.memset(EX[:, 0, :], 0.0), writes=["EX"])
                else:
                    S.add("dve", lambda e, j=j: e.tensor_tensor(out=EX[:, j, :], in0=EX[:, j - 1, :],
                                                               in1=TOTs[:, j - 1, :], op=ALU.add),
                          reads=["EX", "TOTs"], writes=["EX"])
            S.add("dve", lambda e, bk=bk, p=p: e.tensor_tensor(
                out=Pt[:, p * 8:(p + 1) * 8, :].rearrange("p j h -> p (j h)"), in0=pf[bk][:, 0:64],
                in1=EX[:, p * 8:(p + 1) * 8, :].rearrange("p j h -> p (j h)"), op=ALU.add),
                reads=[("pf", bk), "EX"], writes=["Pt"])

            if STOP <= 2:
                continue
            def qk_sq(bk, kq):
                S.add("act", lambda e: e.activation(out=ksq[kq], in_=pf[bk], func=AF.Square),
                      reads=[("pf", bk)], writes=[("ksq", kq)])

            def qk_fin(bk, kq, gcol, dst, dstkeys, extra_reads=(), sview=None, skey=None):
                if sview is None:
                    sb_ = gen.next()
                    sview, skey = pf[sb_], ("pf", sb_)
                mm(sview, blockones, ksq[kq], True, True, ["blockones", ("ksq", kq)], [skey])
                S.add("act", lambda e: e.activation(out=lnt, in_=sview, func=AF.Ln, bias=EPS, scale=1.0 / 64),
                      reads=[skey], writes=["lnt"])
                S.add("act", lambda e: e.activation(out=rstdq, in_=lnt, func=AF.Exp, scale=-0.5),
                      reads=["lnt"], writes=["rstdq"])
                S.add("dve", lambda e: e.scalar_tensor_tensor(out=dst, in0=pf[bk], scalar=gcol, in1=rstdq,
                                                              op0=ALU.mult, op1=ALU.mult),
                      reads=[("pf", bk), "rstdq", "spT", "qgs"] + list(extra_reads), writes=dstkeys)

            kr = Rot(range(2))
            kqr = Rot(range(2))
            pend = None

            def k_finish(bk, kq, ks, hc, sl):
                qk_fin(bk, kq, spT[:, SP_KG:SP_KG + 1], kst[ks], [("kst", ks)])
                c0 = tok0 + sl * 512
                S.add("sp", lambda e: e.dma_start(
                    out=KT_d[hc * 128:(hc + 1) * 128, c0:c0 + 512], in_=kst[ks]),
                    reads=[("kst", ks)], writes=[("KTd", hc, p, sl)], dma=f"kst{ks}")

            for u in range(2):
                ws = load_w(w_in_l[L][:, C_K + u * 256:C_K + (u + 1) * 256], 8, 256)
                for ci in range(2):
                    hc = u * 2 + ci
                    for sl in range(2):
                        bk = gen.next()
                        proj_fm(ws, ci * 128, sl, bk)
                        kq = kqr.next()
                        qk_sq(bk, kq)
                        if pend is not None:
                            k_finish(*pend)
                        pend = (bk, kq, kr.next(), hc, sl)
            k_finish(*pend)

            def silu_from_psum(bk, dst, dstkey):
                S.add("act", lambda e: e.activation(out=et, in_=pf[bk], func=AF.Tanh, scale=0.5),
                      reads=[("pf", bk)], writes=["et"])
                S.add("dve", lambda e: e.scalar_tensor_tensor(out=dst, in0=et, scalar=1.0, in1=pf[bk],
                                                              op0=ALU.add, op1=ALU.mult),
                      reads=[("pf", bk), "et"], writes=[dstkey])

            if p == 0:
                S.add("dve", lambda e: e.memset(aT[:, :, 0:32], 0.0), writes=["aT"])
            else:
                S.add("dve", lambda e: e.tensor_copy(out=aT[:, :, 0:32], in_=aT[:, :, PASS:PASS + 32]),
                      reads=["aT"], writes=["aT"])
            for u in range(2):
                ws = load_w(w_in_l[L][:, C_AG + u * 256:C_AG + (u + 1) * 256], 8, 256)
                for ci in range(2):
                    for sl in range(2):
                        bk = gen.next()
                        proj_fm(ws, ci * 128, sl, bk)
                        S.add("act", lambda e, bk=bk, ci=ci, sl=sl: e.activation(
                            out=ew1[:, ci, sl * 512:(sl + 1) * 512], in_=pf[bk], func=AF.Tanh, scale=0.5),
                            reads=[("pf", bk)], writes=["ew1"])
                ws = load_w(w_in_l[L][:, C_AV + u * 256:C_AV + (u + 1) * 256], 8, 256)
                for ci in range(2):
                    cc = u * 2 + ci
                    for sl in range(2):
                        bk = gen.next()
                        proj_fm(ws, ci * 128, sl, bk)
                        S.add("dve", lambda e, bk=bk, cc=cc, ci=ci, sl=sl: e.scalar_tensor_tensor(
                            out=aT[:, cc, 32 + sl * 512:32 + (sl + 1) * 512], in0=ew1[:, ci, sl * 512:(sl + 1) * 512],
                            scalar=1.0, in1=pf[bk], op0=ALU.add, op1=ALU.mult),
                            reads=[("pf", bk), "ew1"], writes=["aT"])

            def conv_tap(sl, cc, k):
                off = 32 + sl * 512 - (30 - k)
                src = aT[:, cc, off:off + 512]
                wcol = cwh[:, cc * 31 + k:cc * 31 + k + 1]
                if k == 30:
                    S.add("dve", lambda e: e.tensor_scalar(
                        out=acc[:, cc, :], in0=src, scalar1=wcol, scalar2=spT[:, SP_CB + cc:SP_CB + cc + 1],
                        op0=ALU.mult, op1=ALU.add), reads=["aT", "spT", "cwh"], writes=[("acc", cc)])
                elif k == 0:
                    S.add("dve", lambda e: e.scalar_tensor_tensor(
                        out=abr[:, cc, sl * 512:(sl + 1) * 512], in0=src, scalar=wcol, in1=acc[:, cc, :],
                        op0=ALU.mult, op1=ALU.add),
                        reads=["aT", "cwh", ("acc", cc)], writes=["abr"])
                else:
                    S.add("dve", lambda e: e.scalar_tensor_tensor(
                        out=acc[:, cc, :], in0=src, scalar=wcol, in1=acc[:, cc, :], op0=ALU.mult, op1=ALU.add),
                        reads=["aT", "cwh", ("acc", cc)], writes=[("acc", cc)])

            def ln_slot(sl):
                b1 = gen.next()
                b2 = gen.next()
                tsl_ = slice(sl * 512, (sl + 1) * 512)
                for cc in range(4):
                    mm(pf[b1], onesb, abr[:, cc, tsl_], cc == 0, cc == 3, ["onesb", "abr"], [("pf", b1)])
                for cc in range(4):
                    yq = cc % 2
                    S.add("act", lambda e, cc=cc, yq=yq: e.activation(out=ysq[yq], in_=abr[:, cc, tsl_], func=AF.Square),
                          reads=["abr"], writes=[("ysq", yq)])
                    mm(pf[b2], onesb, ysq[yq], cc == 0, cc == 3, ["onesb", ("ysq", yq)], [("pf", b2)])
                S.add("dve", lambda e: e.tensor_scalar(out=mean, in0=pf[b1], scalar1=1.0 / 512, scalar2=None,
                                                       op0=ALU.mult), reads=[("pf", b1)], writes=["mean"])
                S.add("dve", lambda e: e.tensor_tensor(out=var, in0=mean, in1=mean, op=ALU.mult),
                      reads=["mean"], writes=["var"])
                S.add("dve", lambda e: e.scalar_tensor_tensor(out=var, in0=pf[b2], scalar=1.0 / 512, in1=var,
                                                              op0=ALU.mult, op1=ALU.subtract),
                      reads=[("pf", b2), "var"], writes=["var"])
                S.add("act", lambda e: e.activation(out=var, in_=var, func=AF.Ln, bias=EPS, scale=1.0),
                      reads=["var"], writes=["var"])
                S.add("act", lambda e: e.activation(out=rstdc, in_=var, func=AF.Exp, scale=-0.5),
                      reads=["var"], writes=["rstdc"])
                for cc in range(4):
                    S.add("dve", lambda e, cc=cc: e.tensor_tensor(out=zt, in0=abr[:, cc, sl * 512:(sl + 1) * 512],
                                                                  in1=mean, op=ALU.subtract),
                          reads=["abr", "mean"], writes=["zt"])
                    S.add("dve", lambda e: e.tensor_tensor(out=zt, in0=zt, in1=rstdc, op=ALU.mult),
                          reads=["zt", "rstdc"], writes=["zt"])
                    S.add("dve", lambda e, cc=cc: e.tensor_scalar(
                        out=zt, in0=zt, scalar1=spT[:, SP_CNG + cc:SP_CNG + cc + 1],
                        scalar2=spT[:, SP_CNB + cc:SP_CNB + cc + 1], op0=ALU.mult, op1=ALU.add),
                        reads=["zt", "spT"], writes=["zt"])
                    S.add("act", lambda e: e.activation(out=et, in_=zt, func=AF.Tanh, scale=0.5),
                          reads=["zt"], writes=["et"])
                    S.add("dve", lambda e, cc=cc: e.scalar_tensor_tensor(
                        out=abr[:, cc, sl * 512:(sl + 1) * 512], in0=et, scalar=1.0, in1=zt, op0=ALU.add, op1=ALU.mult),
                        reads=["zt", "et"], writes=["abr"])

            convq = []
            for sl_ in range(2):
                for cpair in range(2):
                    for k in range(30, -1, -1):
                        for cc_ in (2 * cpair, 2 * cpair + 1):
                            convq.append((conv_tap, (sl_, cc_, k)))
            convq.reverse()

            def pull_conv(nitems):
                while nitems > 0 and convq:
                    f, a = convq.pop()
                    f(*a)
                    nitems -= 1

            S.barrier(A1)
            for sl in range(2):
                n = 4 * (2 * p + sl) + 4
                jref = p * 8 + 4 * sl + 2
                S.add("dve", lambda e, sl=sl, n=n, jref=jref: e.tensor_tensor(
                    out=biasT[:, sl, 0:n, :], in0=Pt[:, 0:n, :],
                    in1=EX[:, jref:jref + 1, :].broadcast_to([128, n, 8]), op=ALU.subtract),
                    reads=["Pt", "EX", A1], writes=["biasT"])
            for i in range(2):
                S.add("dve", lambda e, i=i: e.memset(Vb[i][:, :, 64:128], 1.0), reads=[A1], writes=[("V", i)])
            ptr = Rot(range(3))
            orot = Rot([2, 3])

            def attn_prep(hc):
                bi = hc % 2
                nk = nblk_total * 128
                S.add("sp", lambda e: e.dma_start(
                    out=KTb[bi][:, 0:nk], in_=KT_d[hc * 128:(hc + 1) * 128, 0:nk]),
                    reads=[("KTd", hc, pp, s_) for pp in range(p + 1) for s_ in range(2)] + [A1],
                    writes=[("KT", bi)], dma=f"kt{bi}")
                for e_ in range(2):
                    cs = hc * 128 + e_ * 64
                    for j0 in range(0, nblk_total, 8):
                        S.add("sp", lambda e, cs=cs, e_=e_, j0=j0: e.dma_start(
                            out=Vb[bi][:, j0:j0 + 8, e_ * 128:e_ * 128 + 64],
                            in_=V_d[j0 * 128:(j0 + 8) * 128, cs:cs + 64].rearrange("(j q) c -> q j c", q=128)),
                            reads=[("Vd", pp, t_) for pp in range(p + 1) for t_ in range(8)] + [A1],
                            writes=[("V", bi)], dma=f"v{bi}")
                ws = load_w(w_in_l[L][:, C_Q + hc * 128:C_Q + (hc + 1) * 128], 8, 128)
                for sl in range(2):
                    proj_fm(ws, 0, sl, 4 + sl)
                    qk_sq(4 + sl, sl)
                for sl in range(2):
                    qk_fin(4 + sl, sl, qgs[:, 0:1], qT[bi][:, sl * 512:(sl + 1) * 512], [("qT", bi)],
                           extra_reads=[A1], sview=pbf[sl], skey=("pb", sl))
                ws = load_w(w_in_l[L][:, C_CZ + hc * 128:C_CZ + (hc + 1) * 128], 8, 128)
                for sl in range(2):
                    bk = 4 + (sl % 2)
                    proj_fm(ws, 0, sl, bk)
                    S.add("act", lambda e, bk=bk: e.activation(out=lnt, in_=pf[bk], func=AF.Tanh, scale=0.5),
                          reads=[("pf", bk)], writes=["lnt"])
                    S.add("dve", lambda e, bk=bk, sl=sl: e.scalar_tensor_tensor(
                        out=scz[bi][:, sl * 512:(sl + 1) * 512], in0=lnt, scalar=1.0, in1=pf[bk],
                        op0=ALU.add, op1=ALU.mult),
                        reads=[("pf", bk), "lnt", A1], writes=[("scz", bi)])

            def attn_iter(hc, sl, e_):
                bi = hc % 2
                n = 4 * (2 * p + sl) + 4
                h = 2 * hc + e_
                rows = slice(64 * e_, 64 * e_ + 64)
                drows = slice(64 * (1 - e_), 64 * (1 - e_) + 64)
                ob = orot.next()
                qv = qT[bi][rows, sl * 512:(sl + 1) * 512]
                pts = {}

                def emit_s(j):
                    sbk = j % 2
                    diag = j >= n - 4
                    q0 = 128 * (j - (n - 4)) if diag else 0
                    mm(pf[sbk][:, q0:512], KTb[bi][rows, j * 128:(j + 1) * 128], qv[:, q0:512], True, not diag,
                       [("KT", bi), ("qT", bi), A1], [("pf", sbk)])
                    if diag:
                        mm(pf[sbk][:, q0:q0 + 128], identb, maskU[:, j - (n - 4), q0:q0 + 128], False, True,
                           ["identb", "maskU"], [("pf", sbk)])
                    pt = ptr.next()
                    pts[j] = (pt, q0)
                    S.add("act", lambda e: e.activation(
                        out=PT[pt][:, q0:512], in_=pf[sbk][:, q0:512], func=AF.Exp, bias=biasT[:, sl, j, h:h + 1],
                        scale=1.0),
                        reads=[("pf", sbk), "biasT", A1], writes=[("PT", pt)])

                def emit_pv(j):
                    pt, q0 = pts[j]
                    mm(pf[ob][:, q0:512], Vb[bi][:, j, e_ * 64:e_ * 64 + 128], PT[pt][:, q0:512], j == 0, j == n - 1,
                       [("V", bi), ("PT", pt), A1], [("pf", ob)])

                emit_s(0)
                for j in range(n):
                    if j + 1 < n:
                        emit_s(j + 1)
                    emit_pv(j)
                tsl = slice(sl * 512, (sl + 1) * 512)
                S.add("dve", lambda e: e.reciprocal(out=rden[rows, :], in_=pf[ob][drows, :]),
                      reads=[("pf", ob), A1], writes=["rden"])
                S.add("dve", lambda e: e.tensor_tensor(
                    out=tmpO[rows, :], in0=pf[ob][rows, :], in1=scz[bi][rows, tsl], op=ALU.mult),
                    reads=[("pf", ob), ("scz", bi), A1], writes=["tmpO"])
                S.add("dve", lambda e: e.scalar_tensor_tensor(
                    out=cbr[rows, hc, tsl], in0=tmpO[rows, :], scalar=0.5, in1=rden[rows, :], op0=ALU.mult, op1=ALU.mult),
                    reads=["tmpO", "rden", A1], writes=["cbr"])

            attn_prep(0)
            for hc in range(4):
                if hc + 1 < 4:
                    attn_prep(hc + 1)
                for sl in range(2):
                    for e_ in range(2):
                        attn_iter(hc, sl, e_)
                        pull_conv(17)
            S.barrier(A1)

            for u in range(2):
                ws = load_w(w_in_l[L][:, C_BZ + u * 256:C_BZ + (u + 1) * 256], 8, 256)
                for ci in range(2):
                    for sl in range(2):
                        bk = gen.next()
                        proj_fm(ws, ci * 128, sl, bk)
                        silu_from_psum(bk, ew1[:, ci, sl * 512:(sl + 1) * 512], "ew1")
                ws = load_w(w_in_l[L][:, C_U + u * 256:C_U + (u + 1) * 256], 8, 256)
                for ci in range(2):
                    for sl in range(2):
                        bk = gen.next()
                        proj_fm(ws, ci * 128, sl, bk)
                        tsl = slice(sl * 512, (sl + 1) * 512)
                        S.add("dve", lambda e, bk=bk, ci=ci, tsl=tsl: e.tensor_tensor(
                            out=ew2[:, ci, tsl], in0=pf[bk], in1=ew1[:, ci, tsl], op=ALU.mult),
                            reads=[("pf", bk), "ew1"], writes=["ew2"])
                for ci in range(2):
                    g = u * 2 + ci
                    for sl in range(2):
                        bk = gen.next()
                        for t4 in range(4):
                            tb = sl * 4 + t4
                            mm(pf[bk][:, t4 * 128:(t4 + 1) * 128], vn[:, tb, g * 128:(g + 1) * 128], WsT[:, g, :],
                               True, False, ["vn", "WsT"], [("pf", bk)])
                            mm(pf[bk][:, t4 * 128:(t4 + 1) * 128], onesf[0:1, :], bsf[0:1, g, :],
                               False, True, ["onesf", "bsf"], [("pf", bk)])
                        tsl = slice(sl * 512, (sl + 1) * 512)
                        S.add("dve", lambda e, bk=bk, g=g, ci=ci, tsl=tsl: e.scalar_tensor_tensor(
                            out=bbr[:, g, tsl], in0=pf[bk], scalar=0.5, in1=ew2[:, ci, tsl], op0=ALU.mult, op1=ALU.mult),
                            reads=[("pf", bk), "ew2"], writes=["bbr"])

            pull_conv(10 ** 6)
            ln_slot(0)
            ln_slot(1)
            for u in range(2):
                ws = load_w(w_in_l[L][:, C_AZ + u * 256:C_AZ + (u + 1) * 256], 8, 256)
                for ci in range(2):
                    cc = u * 2 + ci
                    for sl in range(2):
                        bk = gen.next()
                        proj_fm(ws, ci * 128, sl, bk)
                        tsl = slice(sl * 512, (sl + 1) * 512)
                        silu_from_psum(bk, zt, "zt")
                        S.add("dve", lambda e, cc=cc, tsl=tsl: e.scalar_tensor_tensor(
                            out=abr[:, cc, tsl], in0=abr[:, cc, tsl], scalar=0.25, in1=zt, op0=ALU.mult, op1=ALU.mult),
                            reads=["abr", "zt"], writes=["abr"])

            brs = ((abr, "abr", w_a_d), (bbr, "bbr", w_b_d), (cbr, "cbr", w_c_d))
            for oc in range(8):
                for i, (br, brk, wd) in enumerate(brs):
                    gc = C_GATE + i * 1024 + oc * 128
                    wg = wrot.next()
                    S.add("pool", lambda e, wg=wg, gc=gc, L=L: e.dma_start(
                        out=wbuf[wg][:, 0:8, 0:128], in_=w_in_l[L][:, gc:gc + 128].rearrange("(kc p) c -> p kc c", p=128)),
                        writes=[("w", wg)], dma=f"w{wg}")
                    S.add("pool", lambda e, wg=wg, wd=wd, oc=oc, L=L: e.dma_start(
                        out=wbuf[wg][:, 0:4, 128:256],
                        in_=wd[L, :, oc * 128:(oc + 1) * 128].rearrange("(kc p) c -> p kc c", p=128)),
                        writes=[("w", wg)], dma=f"w{wg}")
                    for sl in range(2):
                        tsl = slice(sl * 512, (sl + 1) * 512)
                        bg = gen.next()
                        proj_fm(wg, 0, sl, bg)
                        by = gen.next()
                        proj_fm(wg, 128, sl, by, nkc=4, src=br, srckey=brk)
                        S.add("act", lambda e, bg=bg, i=i, oc=oc: e.activation(
                            out=gden[i], in_=pf[bg], func=AF.Tanh, bias=nbg[:, i * 8 + oc:i * 8 + oc + 1], scale=0.5),
                            reads=[("pf", bg), "nbg", A1], writes=[("gden", i)])
                        if i == 0:
                            S.add("dve", lambda e, by=by, i=i, sl=sl: e.scalar_tensor_tensor(
                                out=macc[sl], in0=gden[i], scalar=1.0, in1=pf[by], op0=ALU.add, op1=ALU.mult),
                                reads=[("pf", by), ("gden", i), A1], writes=[("macc", sl)])
                        else:
                            S.add("dve", lambda e, by=by, i=i: e.scalar_tensor_tensor(
                                out=mt, in0=gden[i], scalar=1.0, in1=pf[by], op0=ALU.add, op1=ALU.mult),
                                reads=[("pf", by), ("gden", i), A1], writes=["mt"])
                            if i == 1:
                                S.add("dve", lambda e, sl=sl: e.tensor_tensor(out=macc[sl], in0=macc[sl], in1=mt,
                                                                               op=ALU.add),
                                      reads=[("macc", sl), "mt", A1], writes=[("macc", sl)])
                            else:
                                S.add("dve", lambda e, oc=oc, tsl=tsl, sl=sl: e.tensor_tensor(
                                    out=mergedT[:, oc, tsl], in0=macc[sl], in1=mt, op=ALU.add),
                                    reads=[("macc", sl), "mt", A1], writes=["mergedT"])

            if p + 1 < NP:
                stage_A(p + 1)
            for tb in range(8):
                r0 = tok0 + tb * 128
                S.add("sp", lambda e, tb=tb, r0=r0, xsrc=xsrc: e.dma_start(out=xh[tb % 2], in_=xsrc[r0:r0 + 128, :]),
                      reads=[(srckey, p, tb), A1], writes=[("xh", tb % 2)], dma=f"xh{tb % 2}")
                for half in range(2):
                    bk = gen.next()
                    for kc in range(8):
                        mm(pf[bk], mergedT[:, kc, tb * 128:(tb + 1) * 128], wout[:, kc, half * 512:(half + 1) * 512],
                           kc == 0, kc == 7, ["mergedT", ("wout", 2 * half), ("wout", 2 * half + 1), A1], [("pf", bk)])
                    S.add("dve", lambda e, bk=bk, tb=tb, half=half: e.scalar_tensor_tensor(
                        out=xo[tb % 2][:, half * 512:(half + 1) * 512], in0=pf[bk], scalar=0.5,
                        in1=xh[tb % 2][:, half * 512:(half + 1) * 512], op0=ALU.mult, op1=ALU.add),
                        reads=[("pf", bk), ("xh", tb % 2), A1], writes=[("xo", tb % 2)])
                S.add("sp", lambda e, tb=tb, r0=r0, xdst=xdst: e.dma_start(out=xdst[r0:r0 + 128, :], in_=xo[tb % 2]),
                      reads=[("xo", tb % 2), A1], writes=[(dstkey, p, tb)], dma=f"xo{tb % 2}")
            S.barrier(A1)

    print("MK ops:", len(S.ops), flush=True)
    fin = S.add("sp", None)
    S.ops[fin].deps = S.ops[fin].deps if S.ops[fin].fn is not None else {}
    for op in S.ops:
        if op.fn is not None and op.dma is not None and op.dma.startswith("xo"):
            S._dep(S.ops[fin], op.idx)
    S.emit()
    return nc


def _consts():
    ident = np.eye(128, dtype=np.float32)
    triu = np.triu(np.ones((128, 128), np.float32))
    tril = np.tril(np.ones((128, 128), np.float32))
    k = np.arange(128)[:, None, None]
    m = np.arange(4)[None, :, None]
    q = np.arange(512)[None, None, :]
    maskU = np.where(128 * m + k > q, -30000.0, 0.0).astype(np.float32)
    return ident, triu, tril, maskU


def _pack_small(inp):
    L = 2
    sp = np.zeros((L, 128, NSP), np.float32)
    for l in range(L):
        sp[l, :, SP_G:SP_G + 8] = inp["norm_g"][l].reshape(8, 128).T
        sp[l, :, SP_BG:SP_BG + 24] = inp["b_gate"][l].reshape(24, 128).T
        cw = inp["conv_w"][l]
        sp[l, :, SP_CW:SP_CW + 124] = cw.reshape(31, 4, 128).transpose(2, 1, 0).reshape(128, 124)
        sp[l, :, SP_CB:SP_CB + 4] = inp["conv_b"][l].reshape(4, 128).T
        sp[l, :, SP_CNG:SP_CNG + 4] = inp["conv_norm_g"][l].reshape(4, 128).T
        sp[l, :, SP_CNB:SP_CNB + 4] = inp["conv_norm_b"][l].reshape(4, 128).T
        sp[l, :, SP_QG] = np.tile(inp["q_norm_g"][l], 2)
        sp[l, :, SP_KG] = np.tile(inp["k_norm_g"][l], 2)
    return sp


_NC_CACHE = {}


def kernel(**inputs):
    inp = {k: np.asarray(v, dtype=np.float32) for k, v in inputs.items()}
    NL = int(os.environ.get("MK_NL", "2"))
    NP = int(os.environ.get("MK_NP", "4"))
    key = (NL, NP)
    if key not in _NC_CACHE:
        _NC_CACHE[key] = build(NL, NP)
    nc = _NC_CACHE[key]
    ident, triu, tril, maskU = _consts()
    spT = _pack_small(inp)
    shared = {
        "w_in0": np.ascontiguousarray(inp["w_in"][0]), "w_in1": np.ascontiguousarray(inp["w_in"][1]), "w_a": inp["w_a"], "w_b": inp["w_b"], "w_c": inp["w_c"], "w_out": inp["w_out"],
        "w_s": inp["w_s"], "b_s": inp["b_s"], "gmlp_norm_g": inp["gmlp_norm_g"], "b_f": inp["b_f"],
        "spT": spT, "ident": ident, "triu": triu, "tril": tril, "maskU": maskU,
    }
    in_maps = []
    for c in range(8):
        m = dict(shared)
        m["x"] = np.ascontiguousarray(inp["x"][c // 2])
        in_maps.append(m)
    res = run_bass_kernel_spmd(nc, in_maps, core_ids=list(range(8)))
    out = np.empty((4, SEQ, D), np.float32)
    for b in range(4):
        out[b, :SEQ // 2] = res.results[2 * b]["out"][:SEQ // 2]
        out[b, SEQ // 2:] = res.results[2 * b + 1]["out"][SEQ // 2:]
    return out
```
